# Optimizing a Trainium2 kernel written in Bass

```python
import math
import jax, jax.numpy as jnp
from jax import lax
import numpy as np

D_MODEL = 2048
BATCH = 1
SEQ = 8192
DEPTH = 4

N_A = DEPTH // 2
N_B = DEPTH - N_A
EPS = 1e-6
D_FF = 4 * D_MODEL

GLA_HEADS = 4
GLA_DK = (D_MODEL // 2) // GLA_HEADS
GLA_DV = D_MODEL // GLA_HEADS
GLA_GATE_RANK = 16
GLA_GATE_TAU = 16.0
GLA_CHUNK = 64
GLA_QK = GLA_HEADS * GLA_DK
GLA_VV = GLA_HEADS * GLA_DV
GLA_IN = 2 * GLA_QK + 2 * GLA_VV + GLA_GATE_RANK

SWA_HEAD_DIM = 64
SWA_Q_HEADS = D_MODEL // SWA_HEAD_DIM
SWA_KV_HEADS = SWA_Q_HEADS // 8
SWA_GROUP = SWA_Q_HEADS // SWA_KV_HEADS
SWA_WINDOW = 128
SWA_BLOCK = 128

kernel_name = "yoco_gla_swa_sink_hybrid"


def rmsnorm(x, g):
    xf = x.astype(jnp.float32)
    y = xf * lax.rsqrt(jnp.mean(xf * xf, axis=-1, keepdims=True) + EPS)
    return (y * g.astype(jnp.float32)).astype(x.dtype)


def sqrelu_mlp(h, w1, w2):
    u = jax.nn.relu(h @ w1)
    return (u * u) @ w2


def gla_mixer(h, w_in, w_g2, b_g, g_o, w_o):
    B, S, _ = h.shape
    H, dk, dv, C = GLA_HEADS, GLA_DK, GLA_DV, GLA_CHUNK
    nC = S // C
    f32 = jnp.float32
    proj = h @ w_in
    q, k, v, r, glr = jnp.split(
        proj, [GLA_QK, 2 * GLA_QK, 2 * GLA_QK + GLA_VV, 2 * GLA_QK + 2 * GLA_VV], axis=-1)
    log_a = jax.nn.log_sigmoid((glr @ w_g2 + b_g).astype(f32)) / GLA_GATE_TAU

    def to_chunks(t, d):
        return t.reshape(B, nC, C, H, d).transpose(1, 0, 3, 2, 4)

    qc = to_chunks(q.astype(f32) * (dk ** -0.5), dk)
    kc = to_chunks(k.astype(f32), dk)
    vc = to_chunks(v.astype(f32), dv)
    bc = jnp.cumsum(to_chunks(log_a, dk), axis=-2)
    causal = jnp.tril(jnp.ones((C, C), dtype=bool))[:, :, None]

    def step(state, inp):
        qi, ki, vi, bi = inp
        o_inter = jnp.einsum('bhcd,bhde->bhce', qi * jnp.exp(bi), state)
        diff = bi[:, :, :, None, :] - bi[:, :, None, :, :]
        decay = jnp.exp(jnp.where(causal, diff, -jnp.inf))
        attn = jnp.einsum('bhid,bhjd,bhijd->bhij', qi, ki, decay)
        o = o_inter + jnp.einsum('bhij,bhje->bhie', attn, vi)
        b_last = bi[:, :, -1:, :]
        k_dec = ki * jnp.exp(b_last - bi)
        state = jnp.exp(b_last[:, :, 0, :])[..., None] * state + \
            jnp.einsum('bhcd,bhce->bhde', k_dec, vi)
        return state, o

    state0 = jnp.zeros((B, H, dk, dv), f32)
    _, oc = lax.scan(step, state0, (qc, kc, vc, bc))
    o = oc.transpose(1, 0, 3, 2, 4).reshape(B, S, H, dv)
    o = rmsnorm(o, g_o)
    o = o * jax.nn.silu(r.astype(f32)).reshape(B, S, H, dv)
    return (o.reshape(B, S, H * dv) @ w_o.astype(f32)).astype(h.dtype)


def shared_kv(h, g_kv, w_k, w_v, g_k):
    B, S, _ = h.shape
    nB = S // SWA_BLOCK
    u = rmsnorm(h, g_kv)
    k = rmsnorm((u @ w_k).reshape(B, S, SWA_KV_HEADS, SWA_HEAD_DIM), g_k)
    v = (u @ w_v).reshape(B, S, SWA_KV_HEADS, SWA_HEAD_DIM)

    def band(t):
        tb = t.reshape(B, nB, SWA_BLOCK, SWA_KV_HEADS, SWA_HEAD_DIM)
        prev = jnp.pad(tb, ((0, 0), (1, 0), (0, 0), (0, 0), (0, 0)))[:, :-1]
        return jnp.concatenate([prev, tb], axis=2)

    return band(k), band(v)


def swa_sink_mixer(h, k_band, v_band, w_q, g_q, sinks, w_o):
    B, S, _ = h.shape
    nB = S // SWA_BLOCK
    q = (h @ w_q).reshape(B, S, SWA_KV_HEADS, SWA_GROUP, SWA_HEAD_DIM)
    q = rmsnorm(q, g_q) * (SWA_HEAD_DIM ** -0.5)
    qb = q.reshape(B, nB, SWA_BLOCK, SWA_KV_HEADS, SWA_GROUP, SWA_HEAD_DIM)
    s = jnp.einsum('bnqkgd,bnjkd->bkgnqj', qb, k_band).astype(jnp.float32)
    qi = jnp.arange(SWA_BLOCK)[:, None]
    kj = jnp.arange(2 * SWA_BLOCK)[None, :]
    rel = qi + SWA_BLOCK - kj
    valid = (rel >= 0) & (rel < SWA_WINDOW)
    first = (jnp.arange(nB)[:, None, None] > 0) | (kj >= SWA_BLOCK)[None]
    mask = valid[None] & first
    s = jnp.where(mask, s, -jnp.inf)
    sink = jnp.broadcast_to(
        sinks.astype(jnp.float32).reshape(1, SWA_KV_HEADS, SWA_GROUP, 1, 1, 1), s.shape[:-1] + (1,))
    p = jax.nn.softmax(jnp.concatenate([s, sink], axis=-1), axis=-1)[..., :-1]
    o = jnp.einsum('bkgnqj,bnjkd->bnqkgd', p.astype(v_band.dtype), v_band)
    return o.reshape(B, S, D_MODEL) @ w_o


def setup_inputs(seed: int = 0) -> dict:
    key = jax.random.key(seed)
    ks = jax.random.split(key, 20)
    n = jax.random.normal
    f = jnp.float32
    D = D_MODEL
    kvw = SWA_KV_HEADS * SWA_HEAD_DIM
    return {
        "x": n(ks[0], (BATCH, SEQ, D), f),
        "norm_mix": 1.0 + 0.02 * n(ks[1], (DEPTH, D), f),
        "norm_mlp": 1.0 + 0.02 * n(ks[2], (DEPTH, D), f),
        "mlp_w1": n(ks[3], (DEPTH, D, D_FF), f) * D ** -0.5,
        "mlp_w2": n(ks[4], (DEPTH, D_FF, D), f) * D_FF ** -0.5,
        "a_w_in": n(ks[5], (N_A, D, GLA_IN), f) * D ** -0.5,
        "a_w_g2": n(ks[6], (N_A, GLA_GATE_RANK, GLA_QK), f) * GLA_GATE_RANK ** -0.5,
        "a_b_g": 0.1 * n(ks[7], (N_A, GLA_QK), f),
        "a_g_o": 1.0 + 0.02 * n(ks[8], (N_A, GLA_DV), f),
        "a_w_o": n(ks[9], (N_A, GLA_VV, D), f) * GLA_VV ** -0.5,
        "kv_norm": 1.0 + 0.02 * n(ks[10], (D,), f),
        "kv_w_k": n(ks[11], (D, kvw), f) * D ** -0.5,
        "kv_w_v": n(ks[12], (D, kvw), f) * D ** -0.5,
        "kv_g_k": 1.0 + 0.02 * n(ks[13], (SWA_HEAD_DIM,), f),
        "b_w_q": n(ks[14], (N_B, D, SWA_Q_HEADS * SWA_HEAD_DIM), f) * D ** -0.5,
        "b_g_q": 1.0 + 0.02 * n(ks[15], (N_B, SWA_HEAD_DIM), f),
        "b_sinks": 0.5 * n(ks[16], (N_B, SWA_Q_HEADS), f),
        "b_w_o": n(ks[17], (N_B, SWA_Q_HEADS * SWA_HEAD_DIM, D), f) * D ** -0.5,
    }


def reference(x, norm_mix, norm_mlp, mlp_w1, mlp_w2, a_w_in, a_w_g2, a_b_g, a_g_o, a_w_o,
              kv_norm, kv_w_k, kv_w_v, kv_g_k, b_w_q, b_g_q, b_sinks, b_w_o):
    h = x
    k_band = None
    v_band = None
    for layer in range(DEPTH):
        u = rmsnorm(h, norm_mix[layer])
        if layer < N_A:
            i = layer
            h = h + gla_mixer(u, a_w_in[i], a_w_g2[i], a_b_g[i], a_g_o[i], a_w_o[i])
        else:
            if layer == N_A:
                k_band, v_band = shared_kv(h, kv_norm, kv_w_k, kv_w_v, kv_g_k)
            j = layer - N_A
            h = h + swa_sink_mixer(u, k_band, v_band, b_w_q[j], b_g_q[j], b_sinks[j], b_w_o[j])
        h = h + sqrelu_mlp(rmsnorm(h, norm_mlp[layer]), mlp_w1[layer], mlp_w2[layer])
    return h
```

```python
import contextlib
import numpy as np
import concourse.bass as bass
import concourse.mybir as mybir
from concourse.bass_utils import run_bass_kernel_spmd

F32 = mybir.dt.float32
BF16 = mybir.dt.bfloat16
AF = mybir.ActivationFunctionType
ALU = mybir.AluOpType
AX = mybir.AxisListType

ENGS = ("pe", "act", "dve", "pool", "sp")
EPOCH = 30000
NCORE = 8
D = 2048
TOK = 1024
NT = 8
EPS = 1e-6


class Prog:
    def __init__(self, nc):
        self.nc = nc
        self.ops = {e: [] for e in ENGS}
        self.cnt = {e: 0 for e in ENGS}
        self.seen = {e: {} for e in ENGS}
        self.lastw = {}
        self.readers = {}
        self.dma_cnt = {}
        self.semkeys = set()

    def _deps(self, eng, reads, writes):
        deps = {}

        def need(tok, raw):
            base, ep, v = tok
            if base == eng and (eng == "pe" or not raw):
                return
            cur = deps.get(base)
            if cur is None or (ep, v) > cur:
                deps[base] = (ep, v)

        for k in reads:
            t = self.lastw.get(k)
            if t is not None:
                need(t, True)
        for k in writes:
            t = self.lastw.get(k)
            if t is not None:
                need(t, k in reads)
            for base, (ep, v) in self.readers.get(k, {}).items():
                need((base, ep, v), False)
        waits = []
        seen = self.seen[eng]
        for base, (ep, v) in deps.items():
            s = seen.get(base)
            if s is None or s < (ep, v):
                seen[base] = (ep, v)
                waits.append((base, ep, v))
        return waits

    def _record(self, tok, reads, writes):
        base, ep, v = tok
        for k in reads:
            if k in writes:
                continue
            self.readers.setdefault(k, {})[base] = (ep, v)
        for k in writes:
            self.lastw[k] = tok
            self.readers[k] = {}

    def op(self, eng, fn, reads=(), writes=()):
        reads = tuple(reads)
        writes = tuple(writes)
        waits = self._deps(eng, reads, writes)
        self.cnt[eng] += 1
        seq = self.cnt[eng]
        sk = (eng, (seq - 1) // EPOCH)
        self.semkeys.add(sk)
        self.ops[eng].append((fn, waits, (sk, 1)))
        self._record((eng, (seq - 1) // EPOCH, (seq - 1) % EPOCH + 1), reads, writes)

    def dma(self, eng, fn, sem, reads=(), writes=(), inc=16):
        reads = tuple(reads)
        writes = tuple(writes)
        waits = self._deps(eng, reads, writes)
        base = "d:" + sem
        self.dma_cnt[base] = self.dma_cnt.get(base, 0) + inc
        v = self.dma_cnt[base]
        self.semkeys.add((base, 0))
        self.ops[eng].append((fn, waits, ((base, 0), inc)))
        self._record((base, 0, v), reads, writes)

    def barrier(self):
        toks = []
        for e in ENGS:
            if self.cnt[e] > 0:
                toks.append((e, (self.cnt[e] - 1) // EPOCH, (self.cnt[e] - 1) % EPOCH + 1))
        for base, v in self.dma_cnt.items():
            toks.append((base, 0, v))
        for e in ENGS:
            waits = []
            for base, ep, v in toks:
                if base == e:
                    continue
                s = self.seen[e].get(base)
                if s is None or s < (ep, v):
                    self.seen[e][base] = (ep, v)
                    waits.append((base, ep, v))
            if waits:
                self.ops[e].append((None, waits, None))
        self.lastw = {}
        self.readers = {}

    def emit(self):
        nc = self.nc
        with contextlib.ExitStack() as st:
            sems = {}
            for i, sk in enumerate(sorted(self.semkeys, key=str)):
                sems[sk] = st.enter_context(nc.semaphore("s%d" % i))
            block = st.enter_context(nc.Block())

            def run(engname):
                def body(e):
                    for fn, waits, inc in self.ops[engname]:
                        for base, ep, v in waits:
                            e.wait_ge(sems[(base, ep)], v)
                        if fn is not None:
                            ins = fn(e)
                            if inc is not None:
                                ins.then_inc(sems[inc[0]], inc[1])
                return body

            block.tensor(run("pe"))
            block.scalar(run("act"))
            block.vector(run("dve"))
            block.gpsimd(run("pool"))
            block.sync(run("sp"))


O_H = 0
O_UT = 16384
O_W = 24576
O_WX = 24576 + 8192
O_X = 40960
O_M = 49152
ARENA = 51200


class Core:
    def __init__(self, nc, st):
        self.nc = nc
        self.P = Prog(nc)
        self.arena = st.enter_context(nc.sbuf_tensor("arena", [128, ARENA], F32))
        self.ps = st.enter_context(nc.psum_tensor("ps", [128, 8, 512], F32))
        a = self.arena
        self.H = a[:, O_H:O_H + 16384].rearrange("p (t d) -> p t d", t=NT)
        self.UT = a[:, O_UT:O_UT + 8192].bitcast(BF16).rearrange("p (j t) -> p j t", j=16)
        m = O_M
        self.U = a[:, m:m + 1024].bitcast(BF16)
        m += 1024
        self.IDb = a[:, m:m + 64].bitcast(BF16)
        m += 64
        self.MCb = a[:, m:m + 64].bitcast(BF16)
        m += 64
        self.MPb = a[:, m:m + 64].bitcast(BF16)
        m += 64
        self.CST = a[:, m:m + 384]
        m += 384
        self.GT = a[:, m:m + 16]
        m += 16
        self.STAT = a[:, m:m + 64]
        m += 64
        self.FL = a[:, m:m + 16]
        m += 16
        assert m <= ARENA, m
        self.wslot = 0
        self.ndma = 0
        self.inputs = {}
        self.outputs = {}

    def din(self, name, shape, dt=F32):
        t = self.nc.dram_tensor(name, list(shape), dt, kind="ExternalInput").ap()
        self.inputs[name] = t
        return t

    def dout(self, name, shape, dt=F32):
        t = self.nc.dram_tensor(name, list(shape), dt, kind="ExternalOutput").ap()
        self.outputs[name] = t
        return t

    def words(self, off, n):
        return self.arena[:, off:off + n]

    def bank(self, b, n=512):
        return self.ps[:, b, 0:n]

    def bankb(self, b0, nb=2):
        return self.ps[:, b0:b0 + nb, :].rearrange("p a b -> p (a b)").bitcast(BF16)

    def mm(self, out, lhsT, rhs, start, stop, reads, bank):
        self.P.op("pe", lambda e: e.matmul(out, lhsT=lhsT, rhs=rhs, start=start, stop=stop),
                  reads=reads, writes=["ps%d" % bank])

    def tp(self, out, in_, ident, reads, bank):
        self.P.op("pe", lambda e: e.transpose(out, in_, ident), reads=reads, writes=["ps%d" % bank])

    def act(self, out, in_, func, reads, writes, scale=1.0, bias=None, accum_out=None):
        kw = {}
        if bias is not None:
            kw["bias"] = bias
        if accum_out is not None:
            kw["accum_out"] = accum_out
        self.P.op("act", lambda e: e.activation(out=out, in_=in_, func=func, scale=scale, **kw),
                  reads=reads, writes=writes)

    def tt(self, out, in0, in1, op, reads, writes, eng="dve"):
        self.P.op(eng, lambda e: e.tensor_tensor(out=out, in0=in0, in1=in1, op=op), reads=reads, writes=writes)

    def ts(self, out, in0, s1, op0, reads, writes, s2=None, op1=None, eng="dve"):
        if op1 is None:
            self.P.op(eng, lambda e: e.tensor_scalar(out=out, in0=in0, scalar1=s1, scalar2=None, op0=op0),
                      reads=reads, writes=writes)
        else:
            self.P.op(eng, lambda e: e.tensor_scalar(out=out, in0=in0, scalar1=s1, scalar2=s2, op0=op0, op1=op1),
                      reads=reads, writes=writes)

    def stt(self, out, in0, scalar, in1, op0, op1, reads, writes, eng="dve"):
        self.P.op(eng, lambda e: e.scalar_tensor_tensor(out=out, in0=in0, scalar=scalar, in1=in1, op0=op0, op1=op1),
                  reads=reads, writes=writes)

    def copy(self, eng, out, in_, reads, writes):
        if eng == "act":
            self.act(out, in_, AF.Copy, reads, writes)
        else:
            self.P.op(eng, lambda e: e.tensor_copy(out=out, in_=in_), reads=reads, writes=writes)

    def dma_sp(self, out, in_, reads, writes):
        self.ndma += 1
        self.P.dma("sp", lambda e: e.dma_start(out=out, in_=in_), "sp%d" % (self.ndma % 8), reads=reads, writes=writes)

    def wload(self, src, j, f, big):
        s = self.wslot % 4
        self.wslot += 1
        if big:
            off = O_W + s * 4096
            n = 4096
            keys = ["W%d" % (2 * s), "W%d" % (2 * s + 1)]
        else:
            off = O_W + s * 2048
            n = 2048
            keys = ["W%d" % s]
        assert j * f <= 2 * n
        view = self.arena[:, off:off + (j * f) // 2].bitcast(BF16).rearrange("p (j f) -> p j f", j=j)
        self.P.dma("pool", lambda e: e.dma_start(out=view, in_=src), "w%d%s" % (s, "b" if big else "s"), writes=keys)
        return view, keys

    def wload_f32(self, src, j, f):
        s = self.wslot % 4
        self.wslot += 1
        off = O_W + s * 2048
        keys = ["W%d" % s]
        view = self.arena[:, off:off + j * f].rearrange("p (j f) -> p j f", j=j)
        self.P.dma("sp", lambda e: e.dma_start(out=view, in_=src), "wf%d" % s, writes=keys)
        return view, keys

    def setup(self):
        cst = self.din("cst", [128, 512])
        fl = self.din("fl", [128, 16])
        self.dma_sp(self.CST, cst[:, 0:384], [], ["CST"])
        self.dma_sp(self.FL, fl, [], ["FL"])
        tmp = self.words(O_X, 128)
        self.dma_sp(tmp, cst[:, 384:512], [], ["TMPC"])
        self.copy("dve", self.IDb, self.CST[:, 0:128], ["CST"], ["IDb"])
        self.copy("dve", self.MCb, self.CST[:, 128:256], ["CST"], ["MCb"])
        self.copy("dve", self.MPb, tmp, ["TMPC"], ["MPb"])
        self.ID32 = self.CST[:, 0:128]
        self.TRI = self.CST[:, 256:384]
        self.P.barrier()

    def load_h(self, src):
        v = src.rearrange("(t p) d -> p t d", p=128)
        for i in range(4):
            self.dma_sp(self.H[:, 2 * i:2 * i + 2, :], v[:, 2 * i:2 * i + 2, :], [], ["H%d" % (2 * i), "H%d" % (2 * i + 1)])

    def store_h(self, dst):
        v = dst.rearrange("(t p) d -> p t d", p=128)
        for i in range(4):
            self.dma_sp(v[:, 2 * i:2 * i + 2, :], self.H[:, 2 * i:2 * i + 2, :], ["H%d" % (2 * i), "H%d" % (2 * i + 1)], ["OUT"])

    def finish(self):
        self.P.barrier()
        self.P.emit()

    def norm(self, g):
        P = self.P
        P.barrier()
        g16 = self.words(O_X, 128)[0:16, :]
        self.dma_sp(g16, g.rearrange("(j p) -> j p", p=128), [], ["G16"])
        self.P.op("pe", lambda e: e.transpose(self.ps[:, 0, 0:16], g16, self.ID32[0:16, 0:16]), reads=["G16", "CST"], writes=["ps0"])
        self.copy("dve", self.GT, self.ps[:, 0, 0:16], [], ["ps0", "GT"])
        SS = self.STAT[:, 0:8]
        TM = self.STAT[:, 8:16]
        RS = self.STAT[:, 16:24]
        psT = self.bankb(6, 2)
        for t in range(NT):
            hk = "H%d" % t
            self.act(self.U, self.H[:, t, :], AF.Square, [hk], ["U", "SS%d" % t], accum_out=SS[:, t:t + 1])
            self.act(TM[:, t:t + 1], SS[:, t:t + 1], AF.Ln, ["SS%d" % t], ["TM%d" % t], scale=1.0 / D, bias=EPS)
            self.act(RS[:, t:t + 1], TM[:, t:t + 1], AF.Exp, ["TM%d" % t], ["RS%d" % t], scale=-0.5)
            self.ts(self.U, self.H[:, t, :], RS[:, t:t + 1], ALU.mult, [hk, "RS%d" % t], ["U"])
            for j in range(16):
                self.tp(psT[:, j * 128:(j + 1) * 128], self.U[:, j * 128:(j + 1) * 128], self.IDb, ["U", "IDb"], 6 + j // 8)
            self.tt(self.UT[:, :, t * 128:(t + 1) * 128], psT.rearrange("p (j c) -> p j c", j=16),
                    self.GT.unsqueeze(2).broadcast_to([128, 16, 128]), ALU.mult, ["GT"], ["ps6", "ps7", "UT"])
        P.barrier()

    def mlp(self, w1, w2):
        P = self.P
        HT = [self.words(O_X + i * 2048, 2048).bitcast(BF16).rearrange("p (c t) -> p c t", c=4) for i in range(2)]
        R = [self.words(O_X + 4096 + i * 512, 512) for i in range(2)]
        n1 = 0
        n2 = 0
        for fb in range(16):
            w1b, k1 = self.wload(w1[:, fb * 512:(fb + 1) * 512].rearrange("(j p) f -> p j f", p=128), 16, 512, True)
            w2b, k2 = self.wload(w2[fb * 512:(fb + 1) * 512, :].rearrange("(j p) f -> p j f", p=128), 4, 2048, True)
            ht = HT[fb % 2]
            hk = "HT%d" % (fb % 2)
            for fc in range(4):
                for half in range(2):
                    b = n1 % 2
                    r = n1 % 2
                    n1 += 1
                    for k in range(16):
                        self.mm(self.bank(b), w1b[:, k, fc * 128:(fc + 1) * 128], self.UT[:, k, half * 512:(half + 1) * 512],
                                k == 0, k == 15, k1 + ["UT"], b)
                    self.act(R[r], self.bank(b), AF.Relu, [], ["ps%d" % b, "R%d" % r])
                    self.tt(ht[:, fc, half * 512:(half + 1) * 512], R[r], R[r], ALU.mult, ["R%d" % r], [hk])
            for t in range(NT):
                for cb in range(4):
                    b = 2 + n2 % 4
                    n2 += 1
                    for fc in range(4):
                        self.mm(self.bank(b), ht[:, fc, t * 128:(t + 1) * 128], w2b[:, fc, cb * 512:(cb + 1) * 512],
                                fc == 0, fc == 3, k2 + [hk], b)
                    hv = self.H[:, t, cb * 512:(cb + 1) * 512]
                    self.tt(hv, self.bank(b), hv, ALU.add, ["H%d" % t], ["ps%d" % b, "H%d" % t])
        P.barrier()

    def gla(self, w_in, w_g2, b_g, g_o, w_o, mode, sa_out=None, at_out=None, sa_in=None, at_in=None):
        P = self.P
        X = O_X
        QT = self.words(X, 1024).bitcast(BF16).rearrange("p (c t) -> p c t", c=2)
        KT = self.words(X + 1024, 1024).bitcast(BF16).rearrange("p (c t) -> p c t", c=2)
        KD = self.words(X + 2048, 1024).bitcast(BF16).rearrange("p (t d) -> p t d", t=NT)
        V = self.words(X + 3072, 2048).bitcast(BF16).rearrange("p (t e) -> p t e", t=NT)
        S = self.words(X + 5120, 1024).rearrange("p (c e) -> p c e", c=2)
        SB = self.words(X + 6144, 512).bitcast(BF16).rearrange("p (c e) -> p c e", c=2)
        RSC = [self.words(X + 6656 + i * 256, 256) for i in range(2)]
        ATT = self.words(X + 7168, 64).bitcast(BF16)
        AT = self.words(X + 7232, 2)
        OSS = self.words(X + 7240, 8)
        OTM = self.words(X + 7248, 8)
        ORS = self.words(X + 7256, 8)
        CF = self.words(X + 7264, 2)
        GOv = self.words(X + 7296, 512)
        AG = self.words(X + 7808, 64)
        WX = O_WX
        EB = self.words(WX, 2048).rearrange("p (c t) -> p c t", c=2)
        ENB = self.words(WX + 2048, 2048).rearrange("p (c t) -> p c t", c=2)
        KDT = [self.words(WX + 2048 + c * 1024, 512).bitcast(BF16) for c in range(2)]
        L = self.words(WX + 4096, 2048).rearrange("p (t d) -> p t d", t=NT)
        ON = self.words(WX + 4096, 2048).bitcast(BF16).rearrange("p (t e) -> p t e", t=NT)
        GLR = self.words(WX + 6144, 1024)
        WG2 = self.words(WX + 7168, 1024)
        OGT = self.words(WX, 2048).bitcast(BF16).rearrange("p (c t) -> p c t", c=4)
        ST2 = self.words(WX + 2048, 1024).rearrange("p (c e) -> p c e", c=2)

        self.P.op("dve", lambda e: e.memset(GLR[0:32, :], 1.0), writes=["GLR"])
        self.dma_sp(WG2[0:16, :], w_g2, [], ["WG2"])
        self.dma_sp(WG2[16:17, :], b_g.rearrange("(o f) -> o f", o=1), [], ["WG2"])
        self.dma_sp(GOv, g_o.partition_broadcast(128), [], ["GO"])
        if mode == "B":
            self.dma_sp(AG, at_in, [], ["AG"])
        wg, kg = self.wload(w_in[:, 6144:6160].rearrange("(j p) f -> p j f", p=128), 16, 16, False)
        for half in range(2):
            for k in range(16):
                self.mm(self.ps[0:16, half, :], wg[:, k, :], self.UT[:, k, half * 512:(half + 1) * 512], k == 0, k == 15, kg + ["UT"], half)
            self.copy("act", GLR[0:16, half * 512:(half + 1) * 512], self.ps[0:16, half, :], [], ["ps%d" % half, "GLR"])

        nproj = 0
        for h in range(4):
            for t in range(NT):
                b = t % 2
                self.mm(self.bank(b, 256), GLR[0:17, t * 128:(t + 1) * 128], WG2[0:17, h * 256:(h + 1) * 256], True, True, ["GLR", "WG2"], b)
                self.act(L[:, t, :], self.bank(b, 256), AF.Exp, [], ["ps%d" % b, "L"], scale=-1.0)
                self.act(L[:, t, :], L[:, t, :], AF.Ln, ["L"], ["L"], bias=1.0)
            for c in range(2):
                pb = self.ps[:, 2 + 2 * c:4 + 2 * c, :].rearrange("p a b -> p (a b)")
                for t in range(NT):
                    self.mm(pb[:, t * 128:(t + 1) * 128], L[:, t, c * 128:(c + 1) * 128], self.TRI, True, True, ["L", "CST"], 2 + 2 * c + t // 4)
                bk = ["ps%d" % (2 + 2 * c), "ps%d" % (3 + 2 * c)]
                self.act(EB[:, c, :], pb, AF.Exp, [], bk + ["EB%d" % c])
                self.act(ENB[:, c, :], pb, AF.Exp, [], bk + ["ENB%d" % c], scale=-1.0)
                self.P.op("dve", lambda e, c=c: e.tensor_reduce(out=AT[:, c:c + 1], in_=EB[:, c, 127::128], axis=AX.X, op=ALU.mult),
                          reads=["EB%d" % c], writes=["AT"])
            wk, kk = self.wload(w_in[:, 1024 + h * 256:1024 + (h + 1) * 256].rearrange("(j p) f -> p j f", p=128), 16, 256, False)
            for c in range(2):
                for half in range(2):
                    b = nproj % 2
                    nproj += 1
                    for k in range(16):
                        self.mm(self.bank(b), wk[:, k, c * 128:(c + 1) * 128], self.UT[:, k, half * 512:(half + 1) * 512], k == 0, k == 15, kk + ["UT"], b)
                    self.tt(KT[:, c, half * 512:(half + 1) * 512], self.bank(b), ENB[:, c, half * 512:(half + 1) * 512], ALU.mult,
                            ["ENB%d" % c], ["ps%d" % b, "KT%d" % c])
                ebl = EB[:, c, 127::128].unsqueeze(2).broadcast_to([128, NT, 128])
                self.tt(KDT[c].rearrange("p (t i) -> p t i", t=NT), KT[:, c, :].rearrange("p (t i) -> p t i", t=NT), ebl, ALU.mult,
                        ["KT%d" % c, "EB%d" % c], ["ENB%d" % c])
            psT = self.bankb(6, 2)
            for t in range(NT):
                for c in range(2):
                    self.tp(psT[:, (t * 2 + c) * 128:(t * 2 + c + 1) * 128], KDT[c][:, t * 128:(t + 1) * 128], self.IDb, ["ENB%d" % c, "IDb"], 6 + t // 4)
            self.copy("act", KD.rearrange("p t d -> p (t d)"), psT, [], ["ps6", "ps7", "KD"])
            if mode == "B":
                wq, kq = self.wload(w_in[:, h * 256:(h + 1) * 256].rearrange("(j p) f -> p j f", p=128), 16, 256, False)
                for c in range(2):
                    for half in range(2):
                        b = nproj % 2
                        nproj += 1
                        for k in range(16):
                            self.mm(self.bank(b), wq[:, k, c * 128:(c + 1) * 128], self.UT[:, k, half * 512:(half + 1) * 512], k == 0, k == 15, kq + ["UT"], b)
                        self.stt(QT[:, c, half * 512:(half + 1) * 512], self.bank(b), 1.0 / 16.0, EB[:, c, half * 512:(half + 1) * 512],
                                 ALU.mult, ALU.mult, ["EB%d" % c], ["ps%d" % b, "QT"])
            for cb in range(2):
                wv, kv = self.wload(w_in[:, 2048 + h * 512 + cb * 256:2048 + h * 512 + (cb + 1) * 256].rearrange("(j p) f -> p j f", p=128), 16, 256, False)
                for t in range(NT):
                    b = nproj % 2
                    nproj += 1
                    for k in range(16):
                        self.mm(self.bank(b, 256), self.UT[:, k, t * 128:(t + 1) * 128], wv[:, k, :], k == 0, k == 15, kv + ["UT"], b)
                    self.copy("act", V[:, t, cb * 256:(cb + 1) * 256], self.bank(b, 256), [], ["ps%d" % b, "V%d" % t])
            if mode == "A":
                self.P.op("dve", lambda e: e.memset(S.rearrange("p c e -> p (c e)"), 0.0), writes=["S0", "S1"])
            else:
                self.P.op("dve", lambda e: e.memset(S.rearrange("p c e -> p (c e)"), 0.0), writes=["S0", "S1"])
                for cp in range(NCORE - 1):
                    sl, ks = self.wload_f32(sa_in[cp, h], 2, 512)
                    for c in range(2):
                        ai = (cp * 4 + h) * 2 + c
                        self.ts(CF[:, c:c + 1], AG[:, ai:ai + 1], self.FL[:, cp:cp + 1], ALU.mult, ["AG", "FL"], ["CF%d" % c],
                                s2=self.FL[:, 8 + cp:9 + cp], op1=ALU.add)
                        self.ts(ST2[:, c, :], sl[:, c, :], self.FL[:, cp:cp + 1], ALU.mult, ks + ["FL"], ["ENB0"])
                        self.stt(S[:, c, :], S[:, c, :], CF[:, c:c + 1], ST2[:, c, :], ALU.mult, ALU.add,
                                 ["S%d" % c, "CF%d" % c, "ENB0"], ["S%d" % c])
                for c in range(2):
                    self.copy("act", SB[:, c, :], S[:, c, :], ["S%d" % c], ["SB%d" % c])
            for t in range(NT):
                tk = slice(t * 128, (t + 1) * 128)
                if mode == "B":
                    for c in range(2):
                        self.mm(self.bank(2, 128), KT[:, c, tk], QT[:, c, tk], c == 0, c == 1, ["KT%d" % c, "QT"], 2)
                    self.tt(ATT, self.bank(2, 128), self.MCb, ALU.mult, ["MCb"], ["ps2", "ATT"])
                    self.mm(self.bank(3), ATT, V[:, t, :], True, False, ["ATT", "V%d" % t], 3)
                    for c in range(2):
                        self.mm(self.bank(3), QT[:, c, tk], SB[:, c, :], False, c == 1, ["QT", "SB%d" % c], 3)
                    self.act(self.U[:, 0:512], self.bank(3), AF.Square, [], ["ps3", "U", "OSS%d" % t], accum_out=OSS[:, t:t + 1])
                    self.act(OTM[:, t:t + 1], OSS[:, t:t + 1], AF.Ln, ["OSS%d" % t], ["OTM%d" % t], scale=1.0 / 512, bias=EPS)
                    self.act(ORS[:, t:t + 1], OTM[:, t:t + 1], AF.Exp, ["OTM%d" % t], ["ORS%d" % t], scale=-0.5)
                    self.stt(ON[:, t, :], self.bank(3), ORS[:, t:t + 1], GOv, ALU.mult, ALU.mult, ["ORS%d" % t, "GO"], ["ps3", "L"])
                if mode == "A" or t < NT - 1:
                    for c in range(2):
                        self.mm(self.bank(4 + c), KD[:, t, c * 128:(c + 1) * 128], V[:, t, :], True, True, ["KD", "V%d" % t], 4 + c)
                        self.stt(S[:, c, :], S[:, c, :], EB[:, c, t * 128 + 127:t * 128 + 128], self.bank(4 + c), ALU.mult, ALU.add,
                                 ["S%d" % c, "EB%d" % c], ["ps%d" % (4 + c), "S%d" % c])
                        if mode == "B":
                            self.copy("act", SB[:, c, :], S[:, c, :], ["S%d" % c], ["SB%d" % c])
            if mode == "A":
                self.dma_sp(sa_out[h], S, ["S0", "S1"], ["SAO"])
                self.dma_sp(at_out[h], AT, ["AT"], ["ATO"])
                P.barrier()
                continue
            for cb in range(2):
                wr, kr = self.wload(w_in[:, 4096 + h * 512 + cb * 256:4096 + h * 512 + (cb + 1) * 256].rearrange("(j p) f -> p j f", p=128), 16, 256, False)
                for t in range(NT):
                    b = nproj % 2
                    nproj += 1
                    for k in range(16):
                        self.mm(self.bank(b, 256), self.UT[:, k, t * 128:(t + 1) * 128], wr[:, k, :], k == 0, k == 15, kr + ["UT"], b)
                    self.act(RSC[b], self.bank(b, 256), AF.Silu, [], ["ps%d" % b, "RSC%d" % b])
                    ov = ON[:, t, cb * 256:(cb + 1) * 256]
                    self.tt(ov, ov, RSC[b], ALU.mult, ["L", "RSC%d" % b], ["L"])
            psT = self.bankb(6, 2)
            for t in range(NT):
                for ec in range(4):
                    self.tp(psT[:, (t % 4 * 4 + ec) * 128:(t % 4 * 4 + ec + 1) * 128], ON[:, t, ec * 128:(ec + 1) * 128], self.IDb, ["L", "IDb"], 6 + (t % 4) // 2)
                if t % 4 == 3:
                    t0 = t - 3
                    self.copy("act", OGT[:, :, t0 * 128:(t0 + 4) * 128].rearrange("p c (t i) -> p t c i", t=4),
                              psT.rearrange("p (t c i) -> p t c i", t=4, c=4), [], ["ps6", "ps7", "EB0", "EB1"])
            nw = 0
            for cbo in range(2):
                wo, ko = self.wload(w_o[h * 512:(h + 1) * 512, cbo * 1024:(cbo + 1) * 1024].rearrange("(j p) f -> p j f", p=128), 4, 1024, False)
                for t in range(NT):
                    for c2 in range(2):
                        b = 2 + nw % 4
                        nw += 1
                        for ec in range(4):
                            self.mm(self.bank(b), OGT[:, ec, t * 128:(t + 1) * 128], wo[:, ec, c2 * 512:(c2 + 1) * 512], ec == 0, ec == 3,
                                    ko + ["EB0", "EB1"], b)
                        hv = self.H[:, t, cbo * 1024 + c2 * 512:cbo * 1024 + (c2 + 1) * 512]
                        self.tt(hv, self.bank(b), hv, ALU.add, ["H%d" % t], ["ps%d" % b, "H%d" % t])
            P.barrier()


    def out_proj(self, ON, OGT, w_o, row0, kon, kogt):
        psT = self.bankb(6, 2)
        for t in range(NT):
            for ec in range(4):
                self.tp(psT[:, (t % 4 * 4 + ec) * 128:(t % 4 * 4 + ec + 1) * 128], ON[:, t, ec * 128:(ec + 1) * 128], self.IDb,
                        kon + ["IDb"], 6 + (t % 4) // 2)
            if t % 4 == 3:
                t0 = t - 3
                self.copy("act", OGT[:, :, t0 * 128:(t0 + 4) * 128].rearrange("p c (t i) -> p t c i", t=4),
                          psT.rearrange("p (t c i) -> p t c i", t=4, c=4), [], ["ps6", "ps7"] + kogt)
        nw = 0
        for cbo in range(2):
            wo, ko = self.wload(w_o[row0:row0 + 512, cbo * 1024:(cbo + 1) * 1024].rearrange("(j p) f -> p j f", p=128), 4, 1024, False)
            for t in range(NT):
                for c2 in range(2):
                    b = 2 + nw % 4
                    nw += 1
                    for ec in range(4):
                        self.mm(self.bank(b), OGT[:, ec, t * 128:(t + 1) * 128], wo[:, ec, c2 * 512:(c2 + 1) * 512], ec == 0, ec == 3,
                                ko + kogt, b)
                    hv = self.H[:, t, cbo * 1024 + c2 * 512:cbo * 1024 + (c2 + 1) * 512]
                    self.tt(hv, self.bank(b), hv, ALU.add, ["H%d" % t], ["ps%d" % b, "H%d" % t])

    def headnorm(self, b, QF, SQ, ST, QN, GB, tag):
        self.copy("act", QF, self.bank(b, 256), [], ["ps%d" % b, tag + "QF"])
        self.tt(SQ, QF, QF, ALU.mult, [tag + "QF"], [tag + "SQ"])
        self.P.op("dve", lambda e: e.tensor_reduce(out=ST[:, 0:4], in_=SQ.rearrange("p (h d) -> p h d", h=4), axis=AX.X, op=ALU.add),
                  reads=[tag + "SQ"], writes=[tag + "SS"])
        self.act(ST[:, 4:8], ST[:, 0:4], AF.Ln, [tag + "SS"], [tag + "TM"], scale=1.0 / 64, bias=EPS)
        self.act(ST[:, 8:12], ST[:, 4:8], AF.Exp, [tag + "TM"], [tag + "RS"], scale=-0.5)
        q3 = QF.rearrange("p (h d) -> p h d", h=4)
        self.tt(q3, q3, ST[:, 8:12].unsqueeze(2).broadcast_to([128, 4, 64]), ALU.mult, [tag + "QF", tag + "RS"], [tag + "QF"])
        self.tt(QN.rearrange("p (h d) -> p h d", h=4), q3, GB.unsqueeze(1).broadcast_to([128, 4, 64]), ALU.mult,
                [tag + "QF", tag + "GB"], [tag + "QN"])

    def kv_compute(self, w_k, w_v, g_k, kt_out, va_out):
        P = self.P
        X = O_X
        KTs = self.words(X, 2048).bitcast(BF16).rearrange("p (h t) -> p h t", h=4)
        VA = self.words(X + 2048, 1040).bitcast(BF16).rearrange("p (t h e) -> p t h e", t=NT, h=4)
        QF = self.words(X + 3200, 256)
        SQ = self.words(X + 3456, 256)
        QN = self.words(X + 3712, 128).bitcast(BF16)
        ST = self.words(X + 3840, 16)
        GK = self.words(X + 3872, 64)
        self.dma_sp(GK, g_k.partition_broadcast(128), [], ["kGB"])
        self.P.op("dve", lambda e: e.memset(VA[:, :, :, 64:65], 1.0), writes=["VA"])
        wk, kk = self.wload(w_k.rearrange("(j p) f -> p j f", p=128), 16, 256, False)
        wv, kv = self.wload(w_v.rearrange("(j p) f -> p j f", p=128), 16, 256, False)
        psT = self.bankb(6, 1)
        for t in range(NT):
            b = t % 2
            for k in range(16):
                self.mm(self.bank(b, 256), self.UT[:, k, t * 128:(t + 1) * 128], wk[:, k, :], k == 0, k == 15, kk + ["UT"], b)
            self.headnorm(b, QF, SQ, ST, QN, GK, "k")
            for kh in range(4):
                self.tp(psT[0:64, kh * 128:(kh + 1) * 128], QN[:, kh * 64:(kh + 1) * 64], self.IDb, ["kQN", "IDb"], 6)
            self.copy("act", KTs[0:64, :, t * 128:(t + 1) * 128], psT[0:64, 0:512].rearrange("p (h i) -> p h i", h=4), [], ["ps6", "KTs"])
            b2 = 2 + t % 2
            for k in range(16):
                self.mm(self.bank(b2, 256), self.UT[:, k, t * 128:(t + 1) * 128], wv[:, k, :], k == 0, k == 15, kv + ["UT"], b2)
            self.copy("act", VA[:, t, :, 0:64], self.bank(b2, 256).rearrange("p (h d) -> p h d", h=4), [], ["ps%d" % b2, "VA"])
        self.dma_sp(kt_out, KTs[0:64], ["KTs"], ["KTO"])
        self.dma_sp(va_out, VA, ["VA"], ["VAO"])
        P.barrier()

    def swa(self, w_q, g_q, sinks, w_o, kt_in, kth_in, va_in, vah_in):
        P = self.P
        X = O_X
        KTs = self.words(X, 2304).bitcast(BF16).rearrange("p (h t) -> p h t", h=4)
        VA = self.words(X + 2304, 1170).bitcast(BF16).rearrange("p (t h e) -> p t h e", t=9, h=4)
        QTg = self.words(X + 3488, 4096).bitcast(BF16).rearrange("p (h t) -> p h t", h=8)
        GQ = self.words(X + 7584, 64)
        ES = self.words(X + 7648, 32)
        ST = self.words(X + 7680, 16)
        DEN = self.words(X + 7696, 8)
        REC = self.words(X + 7704, 8)
        WX = O_WX
        OA = self.words(WX, 2048).bitcast(BF16).rearrange("p (t e) -> p t e", t=NT)
        OAT = self.words(WX + 2048, 2048).bitcast(BF16).rearrange("p (c t) -> p c t", c=4)
        PT = [self.words(WX + 4096 + i * 1024, 1024).bitcast(BF16).rearrange("p (k n) -> p k n", k=2) for i in range(2)]
        QF = [self.words(WX + 6144 + i * 256, 256) for i in range(2)]
        SQ = self.words(WX + 6656, 256)
        QN = self.words(WX + 6912, 128).bitcast(BF16)
        self.dma_sp(KTs[0:64, :, 0:128], kth_in, [], ["KTs"])
        self.dma_sp(KTs[0:64, :, 128:1152], kt_in, [], ["KTs"])
        self.dma_sp(VA[:, 0], vah_in, [], ["VA"])
        self.dma_sp(VA[:, 1:9], va_in, [], ["VA"])
        self.dma_sp(GQ, g_q.partition_broadcast(128), [], ["GQraw"])
        self.act(GQ, GQ, AF.Copy, ["GQraw"], ["qGB"], scale=0.125)
        self.dma_sp(ES, sinks.partition_broadcast(128), [], ["ESraw"])
        self.act(ES, ES, AF.Exp, ["ESraw"], ["ES"])
        psT = self.bankb(6, 1)
        nq = 0
        ns = 0
        for g in range(4):
            for blk in range(2):
                wq, kq = self.wload(w_q[:, g * 512 + blk * 256:g * 512 + (blk + 1) * 256].rearrange("(j p) f -> p j f", p=128), 16, 256, False)
                for t in range(NT):
                    b = nq % 2
                    nq += 1
                    for k in range(16):
                        self.mm(self.bank(b, 256), self.UT[:, k, t * 128:(t + 1) * 128], wq[:, k, :], k == 0, k == 15, kq + ["UT"], b)
                    self.headnorm(b, QF[b], SQ, ST, QN, GQ, "q")
                    for hd in range(4):
                        self.tp(psT[0:64, hd * 128:(hd + 1) * 128], QN[:, hd * 64:(hd + 1) * 64], self.IDb, ["qQN", "IDb"], 6)
                    self.copy("act", QTg[0:64, blk * 4:(blk + 1) * 4, t * 128:(t + 1) * 128],
                              psT[0:64, 0:512].rearrange("p (h i) -> p h i", h=4), [], ["ps6", "QTg"])
            for t in range(NT):
                pt = PT[t % 2]
                pk = "PT%d" % (t % 2)
                for kb in range(2):
                    kt = t + kb
                    for hh in range(2):
                        b = 2 + ns % 2
                        ns += 1
                        self.mm(self.bank(b), KTs[0:64, g, kt * 128:(kt + 1) * 128], QTg[0:64, hh * 4:(hh + 1) * 4, t * 128:(t + 1) * 128],
                                True, True, ["KTs", "QTg"], b)
                        self.act(pt[:, kb, hh * 512:(hh + 1) * 512], self.bank(b), AF.Exp, [], ["ps%d" % b, pk])
                    msk = self.MPb if kb == 0 else self.MCb
                    pv = pt[:, kb, :].rearrange("p (h i) -> p h i", h=8)
                    self.tt(pv, pv, msk.unsqueeze(1).broadcast_to([128, 8, 128]), ALU.mult, [pk, "MPb", "MCb"], [pk])
                for hd in range(8):
                    bo = 4 + hd // 4
                    o = self.ps[:, bo, (hd % 4) * 65:(hd % 4) * 65 + 65]
                    for kb in range(2):
                        self.mm(o, pt[:, kb, hd * 128:(hd + 1) * 128], VA[:, t + kb, g, :], kb == 0, kb == 1, [pk, "VA"], bo)
                for bb in range(2):
                    pso = self.ps[:, 4 + bb, 0:260].rearrange("p (h e) -> p h e", h=4)
                    dn = DEN[:, bb * 4:(bb + 1) * 4]
                    self.tt(dn.unsqueeze(2), pso[:, :, 64:65], ES[:, g * 8 + bb * 4:g * 8 + bb * 4 + 4].unsqueeze(2), ALU.add, ["ES"],
                            ["ps%d" % (4 + bb), "DEN%d" % bb])
                    self.P.op("dve", lambda e, bb=bb, dn=dn: e.reciprocal(out=REC[:, bb * 4:(bb + 1) * 4], in_=dn),
                              reads=["DEN%d" % bb], writes=["REC%d" % bb])
                    self.tt(OA[:, t, bb * 256:(bb + 1) * 256].rearrange("p (h d) -> p h d", h=4), pso[:, :, 0:64],
                            REC[:, bb * 4:(bb + 1) * 4].unsqueeze(2).broadcast_to([128, 4, 64]), ALU.mult, ["REC%d" % bb],
                            ["ps%d" % (4 + bb), "OA"])
            self.out_proj(OA, OAT, w_o, g * 512, ["OA"], ["OAT"])
        P.barrier()


def _consts():
    c = np.zeros((128, 512), np.float32)
    j = np.arange(128)[:, None]
    i = np.arange(128)[None, :]
    c[:, 0:128] = np.eye(128, dtype=np.float32)
    c[:, 128:256] = (j <= i)
    c[:, 256:384] = (j <= i) * (-1.0 / 16.0)
    c[:, 384:512] = (j > i)
    return c


def _flags(core):
    f = np.zeros((128, 16), np.float32)
    for cp in range(8):
        m = 1.0 if cp < core else 0.0
        f[:, cp] = m
        f[:, 8 + cp] = 1.0 - m
    return f


def _run(build, in_maps):
    nc = bass.Bass("TRN2", target_bir_lowering=False)
    with contextlib.ExitStack() as st:
        core = Core(nc, st)
        core.setup()
        build(core)
        core.finish()
    for c in range(NCORE):
        in_maps[c]["cst"] = _consts()
        in_maps[c]["fl"] = _flags(c)
    res = run_bass_kernel_spmd(nc, in_maps, core_ids=list(range(NCORE)))
    return res.results


def launch1(x, inp):
    def build(k):
        xin = k.din("x", [TOK, D])
        w_in = k.din("w_in", [D, 6160])
        w_g2 = k.din("w_g2", [16, 1024])
        b_g = k.din("b_g", [1024])
        g_o = k.din("g_o", [512])
        g = k.din("g", [D])
        sa = k.dout("sa", [4, 128, 2, 512])
        at = k.dout("at", [4, 128, 2])
        k.load_h(xin)
        k.norm(g)
        k.gla(w_in, w_g2, b_g, g_o, None, "A", sa_out=sa, at_out=at)
    maps = [{"x": x[c], "w_in": inp["a_w_in"][0], "w_g2": inp["a_w_g2"][0], "b_g": inp["a_b_g"][0],
             "g_o": inp["a_g_o"][0], "g": inp["norm_mix"][0]} for c in range(NCORE)]
    return _run(build, maps)


def _gather_states(res):
    sa = np.stack([r["sa"] for r in res], 0)
    at = np.stack([r["at"] for r in res], 0)
    at = np.ascontiguousarray(at.transpose(2, 0, 1, 3).reshape(128, 64))
    return np.ascontiguousarray(sa), at


def launch2(x, inp, sa, at):
    def build(k):
        xin = k.din("x", [TOK, D])
        w_in = k.din("w_in", [D, 6160])
        w_g2 = k.din("w_g2", [16, 1024])
        b_g = k.din("b_g", [1024])
        g_o = k.din("g_o", [512])
        w_o = k.din("w_o", [D, D])
        g = k.din("g", [D])
        g2 = k.din("g2", [D])
        w1 = k.din("w1", [D, 4 * D])
        w2 = k.din("w2", [4 * D, D])
        sai = k.din("sa_in", [8, 4, 128, 2, 512])
        ati = k.din("at_in", [128, 64])
        w_in1 = k.din("w_in1", [D, 6160])
        w_g21 = k.din("w_g21", [16, 1024])
        b_g1 = k.din("b_g1", [1024])
        g_o1 = k.din("g_o1", [512])
        g3 = k.din("g3", [D])
        hout = k.dout("h", [TOK, D])
        sa = k.dout("sa", [4, 128, 2, 512])
        at = k.dout("at", [4, 128, 2])
        k.load_h(xin)
        k.norm(g)
        k.gla(w_in, w_g2, b_g, g_o, w_o, "B", sa_in=sai, at_in=ati)
        k.norm(g2)
        k.mlp(w1, w2)
        k.store_h(hout)
        k.norm(g3)
        k.gla(w_in1, w_g21, b_g1, g_o1, None, "A", sa_out=sa, at_out=at)
    maps = [{"x": x[c], "w_in": inp["a_w_in"][0], "w_g2": inp["a_w_g2"][0], "b_g": inp["a_b_g"][0],
             "g_o": inp["a_g_o"][0], "w_o": inp["a_w_o"][0], "g": inp["norm_mix"][0], "g2": inp["norm_mlp"][0],
             "w1": inp["mlp_w1"][0], "w2": inp["mlp_w2"][0], "sa_in": sa, "at_in": at,
             "w_in1": inp["a_w_in"][1], "w_g21": inp["a_w_g2"][1], "b_g1": inp["a_b_g"][1], "g_o1": inp["a_g_o"][1],
             "g3": inp["norm_mix"][1]} for c in range(NCORE)]
    return _run(build, maps)


def launch3(hin, inp, sa, at):
    import ml_dtypes
    def build(k):
        xin = k.din("x", [TOK, D])
        w_in = k.din("w_in", [D, 6160])
        w_g2 = k.din("w_g2", [16, 1024])
        b_g = k.din("b_g", [1024])
        g_o = k.din("g_o", [512])
        w_o = k.din("w_o", [D, D])
        g = k.din("g", [D])
        g2 = k.din("g2", [D])
        w1 = k.din("w1", [D, 4 * D])
        w2 = k.din("w2", [4 * D, D])
        sai = k.din("sa_in", [8, 4, 128, 2, 512])
        ati = k.din("at_in", [128, 64])
        g3 = k.din("g3", [D])
        w_k = k.din("w_k", [D, 256])
        w_v = k.din("w_v", [D, 256])
        g_k = k.din("g_k", [64])
        hout = k.dout("h", [TOK, D])
        kt = k.dout("kt", [64, 4, 1024], BF16)
        va = k.dout("va", [128, 8, 4, 65], BF16)
        k.load_h(xin)
        k.norm(g)
        k.gla(w_in, w_g2, b_g, g_o, w_o, "B", sa_in=sai, at_in=ati)
        k.norm(g2)
        k.mlp(w1, w2)
        k.store_h(hout)
        k.norm(g3)
        k.kv_compute(w_k, w_v, g_k, kt, va)
    maps = [{"x": hin[c], "w_in": inp["a_w_in"][1], "w_g2": inp["a_w_g2"][1], "b_g": inp["a_b_g"][1],
             "g_o": inp["a_g_o"][1], "w_o": inp["a_w_o"][1], "g": inp["norm_mix"][1], "g2": inp["norm_mlp"][1],
             "w1": inp["mlp_w1"][1], "w2": inp["mlp_w2"][1], "sa_in": sa, "at_in": at,
             "g3": inp["kv_norm"], "w_k": inp["kv_w_k"], "w_v": inp["kv_w_v"], "g_k": inp["kv_g_k"]} for c in range(NCORE)]
    return _run(build, maps)


def launch4(hin, inp, kts, vas):
    import ml_dtypes
    def build(k):
        xin = k.din("x", [TOK, D])
        kt = k.din("kt", [64, 4, 1024], BF16)
        kth = k.din("kth", [64, 4, 128], BF16)
        va = k.din("va", [128, 8, 4, 65], BF16)
        vah = k.din("vah", [128, 4, 65], BF16)
        hout = k.dout("h", [TOK, D])
        k.load_h(xin)
        for j in range(2):
            g = k.din("g%d" % j, [D])
            g2 = k.din("gm%d" % j, [D])
            w_q = k.din("w_q%d" % j, [D, D])
            g_q = k.din("g_q%d" % j, [64])
            sk = k.din("sk%d" % j, [32])
            w_o = k.din("w_o%d" % j, [D, D])
            w1 = k.din("w1%d" % j, [D, 4 * D])
            w2 = k.din("w2%d" % j, [4 * D, D])
            k.norm(g)
            k.swa(w_q, g_q, sk, w_o, kt, kth, va, vah)
            k.norm(g2)
            k.mlp(w1, w2)
        k.store_h(hout)
    maps = []
    for c in range(NCORE):
        m = {"x": hin[c], "kt": kts[c], "va": vas[c]}
        if c == 0:
            m["kth"] = np.zeros((64, 4, 128), ml_dtypes.bfloat16)
            m["vah"] = np.zeros((128, 4, 65), ml_dtypes.bfloat16)
        else:
            m["kth"] = np.ascontiguousarray(kts[c - 1][:, :, 896:1024])
            m["vah"] = np.ascontiguousarray(vas[c - 1][:, 7])
        for j in range(2):
            m["g%d" % j] = inp["norm_mix"][2 + j]
            m["gm%d" % j] = inp["norm_mlp"][2 + j]
            m["w_q%d" % j] = inp["b_w_q"][j]
            m["g_q%d" % j] = inp["b_g_q"][j]
            m["sk%d" % j] = inp["b_sinks"][j]
            m["w_o%d" % j] = inp["b_w_o"][j]
            m["w1%d" % j] = inp["mlp_w1"][2 + j]
            m["w2%d" % j] = inp["mlp_w2"][2 + j]
        maps.append(m)
    return _run(build, maps)


def kernel(**inp):
    inp = {k: np.ascontiguousarray(np.asarray(v)) for k, v in inp.items()}
    x = inp["x"][0].reshape(NCORE, TOK, D)
    r1 = launch1(x, inp)
    sa, at = _gather_states(r1)
    r2 = launch2(x, inp, sa, at)
    h = [r["h"] for r in r2]
    sa, at = _gather_states(r2)
    r3 = launch3(h, inp, sa, at)
    h = [r["h"] for r in r3]
    r4 = launch4(h, inp, [r["kt"] for r in r3], [r["va"] for r in r3])
    out = np.concatenate([r["h"] for r in r4], 0)
    return out.reshape(1, NCORE * TOK, D).astype(np.float32)
```

```python
import contextlib
import numpy as np
import concourse.bass as bass
import concourse.mybir as mybir
from concourse.bass_utils import run_bass_kernel_spmd

F32 = mybir.dt.float32
BF16 = mybir.dt.bfloat16
AF = mybir.ActivationFunctionType
ALU = mybir.AluOpType
AX = mybir.AxisListType

ENGS = ("pe", "act", "dve", "pool", "sp")
EPOCH = 30000
NCORE = 8
D = 2048
TOK = 1024
NT = 8
EPS = 1e-6


class Prog:
    def __init__(self, nc):
        self.nc = nc
        self.ops = {e: [] for e in ENGS}
        self.cnt = {e: 0 for e in ENGS}
        self.seen = {e: {} for e in ENGS}
        self.lastw = {}
        self.readers = {}
        self.dma_cnt = {}
        self.semkeys = set()
        self.async_sems = set()

    def _deps(self, eng, reads, writes):
        deps = {}

        def need(tok, raw):
            base, ep, v = tok
            if base == eng and (eng == "pe" or not raw):
                return
            cur = deps.get(base)
            if cur is None or (ep, v) > cur:
                deps[base] = (ep, v)

        for k in reads:
            t = self.lastw.get(k)
            if t is not None:
                need(t, True)
        for k in writes:
            t = self.lastw.get(k)
            if t is not None:
                need(t, k in reads)
            for base, (ep, v) in self.readers.get(k, {}).items():
                need((base, ep, v), False)
        waits = []
        seen = self.seen[eng]
        for base, (ep, v) in deps.items():
            s = seen.get(base)
            if s is None or s < (ep, v):
                seen[base] = (ep, v)
                waits.append((base, ep, v))
        return waits

    def _record(self, tok, reads, writes):
        base, ep, v = tok
        for k in reads:
            if k in writes:
                continue
            self.readers.setdefault(k, {})[base] = (ep, v)
        for k in writes:
            self.lastw[k] = tok
            self.readers[k] = {}

    def op(self, eng, fn, reads=(), writes=()):
        reads = tuple(reads)
        writes = tuple(writes)
        waits = self._deps(eng, reads, writes)
        self.cnt[eng] += 1
        seq = self.cnt[eng]
        sk = (eng, (seq - 1) // EPOCH)
        self.semkeys.add(sk)
        self.ops[eng].append((fn, waits, (sk, 1)))
        self._record((eng, (seq - 1) // EPOCH, (seq - 1) % EPOCH + 1), reads, writes)

    def dma(self, eng, fn, sem, reads=(), writes=(), inc=16):
        reads = tuple(reads)
        writes = tuple(writes)
        waits = self._deps(eng, reads, writes)
        base = "d:" + sem
        self.dma_cnt[base] = self.dma_cnt.get(base, 0) + inc
        v = self.dma_cnt[base]
        self.semkeys.add((base, 0))
        self.ops[eng].append((fn, waits, ((base, 0), inc)))
        self._record((base, 0, v), reads, writes)

    STICKY = ("W", "SAO", "ATO", "CCS", "CCD")

    def barrier(self, full=True):
        toks = []
        for e in ENGS:
            if self.cnt[e] > 0:
                toks.append((e, (self.cnt[e] - 1) // EPOCH, (self.cnt[e] - 1) % EPOCH + 1))
        for base, v in self.dma_cnt.items():
            if not full and base in self.async_sems:
                continue
            toks.append((base, 0, v))
        for e in ENGS:
            if not full and e == "pool":
                continue
            waits = []
            for base, ep, v in toks:
                if base == e:
                    continue
                s = self.seen[e].get(base)
                if s is None or s < (ep, v):
                    self.seen[e][base] = (ep, v)
                    waits.append((base, ep, v))
            if waits:
                self.ops[e].append((None, waits, None))
        if full:
            self.lastw = {}
            self.readers = {}
        else:
            self.lastw = {k: v for k, v in self.lastw.items() if k.startswith(self.STICKY)}
            self.readers = {k: v for k, v in self.readers.items() if k.startswith(self.STICKY)}

    def emit(self):
        nc = self.nc
        with contextlib.ExitStack() as st:
            sems = {}
            for i, sk in enumerate(sorted(self.semkeys, key=str)):
                sems[sk] = st.enter_context(nc.semaphore("s%d" % i))
            block = st.enter_context(nc.Block())

            def run(engname):
                def body(e):
                    for fn, waits, inc in self.ops[engname]:
                        for base, ep, v in waits:
                            e.wait_ge(sems[(base, ep)], v)
                        if fn is not None:
                            ins = fn(e)
                            if inc is not None:
                                ins.then_inc(sems[inc[0]], inc[1])
                return body

            block.tensor(run("pe"))
            block.scalar(run("act"))
            block.vector(run("dve"))
            block.gpsimd(run("pool"))
            block.sync(run("sp"))


O_H = 0
O_UT = 16384
O_W = 24576
O_WX = 24576 + 8192
O_X = 40960
O_M = 49152
ARENA = 51200


class Core:
    def __init__(self, nc, st):
        self.nc = nc
        self.P = Prog(nc)
        self.arena = st.enter_context(nc.sbuf_tensor("arena", [128, ARENA], F32))
        self.ps = st.enter_context(nc.psum_tensor("ps", [128, 8, 512], F32))
        a = self.arena
        self.H = a[:, O_H:O_H + 16384].rearrange("p (t d) -> p t d", t=NT)
        self.UT = a[:, O_UT:O_UT + 8192].bitcast(BF16).rearrange("p (j t) -> p j t", j=16)
        m = O_M
        self.U = a[:, m:m + 1024].bitcast(BF16)
        m += 1024
        self.IDb = a[:, m:m + 64].bitcast(BF16)
        m += 64
        self.MCb = a[:, m:m + 64].bitcast(BF16)
        m += 64
        self.MPb = a[:, m:m + 64].bitcast(BF16)
        m += 64
        self.CST = a[:, m:m + 384]
        m += 384
        self.GT = a[:, m:m + 16]
        m += 16
        self.STAT = a[:, m:m + 64]
        m += 64
        self.FL = a[:, m:m + 24]
        m += 24
        assert m <= ARENA, m
        self.wslot = 0
        self.ndma = 0
        self.pending = []
        self.inputs = {}
        self.outputs = {}

    def din(self, name, shape, dt=F32):
        t = self.nc.dram_tensor(name, list(shape), dt, kind="ExternalInput").ap()
        self.inputs[name] = t
        return t

    def dout(self, name, shape, dt=F32):
        t = self.nc.dram_tensor(name, list(shape), dt, kind="ExternalOutput").ap()
        self.outputs[name] = t
        return t

    def dscratch(self, name, shape, dt=F32):
        return self.nc.dram_tensor(name, list(shape), dt).ap()

    def defer(self, fn, nloads):
        self.pending.append([nloads, fn])

    def _tick(self):
        for p in self.pending:
            p[0] -= 1
        while self.pending and self.pending[0][0] <= 0:
            self.pending.pop(0)[1]()

    def flush(self):
        while self.pending:
            self.pending.pop(0)[1]()

    def allgather(self, src, dst, rkeys, wkeys, sem):
        self.P.async_sems.add("d:" + sem)
        self.P.dma("pool", lambda e: e.collective_compute("AllGather", ALU.bypass, replica_groups=[list(range(NCORE))],
                                                          ins=[src.opt()], outs=[dst.opt()]),
                   sem, reads=rkeys, writes=wkeys, inc=1)

    def words(self, off, n):
        return self.arena[:, off:off + n]

    def bank(self, b, n=512):
        return self.ps[:, b, 0:n]

    def bankb(self, b0, nb=2):
        return self.ps[:, b0:b0 + nb, :].rearrange("p a b -> p (a b)").bitcast(BF16)

    def mm(self, out, lhsT, rhs, start, stop, reads, bank):
        self.P.op("pe", lambda e: e.matmul(out, lhsT=lhsT, rhs=rhs, start=start, stop=stop),
                  reads=reads, writes=["ps%d" % bank])

    def tp(self, out, in_, ident, reads, bank):
        self.P.op("pe", lambda e: e.transpose(out, in_, ident), reads=reads, writes=["ps%d" % bank])

    def act(self, out, in_, func, reads, writes, scale=1.0, bias=None, accum_out=None):
        kw = {}
        if bias is not None:
            kw["bias"] = bias
        if accum_out is not None:
            kw["accum_out"] = accum_out
        self.P.op("act", lambda e: e.activation(out=out, in_=in_, func=func, scale=scale, **kw),
                  reads=reads, writes=writes)

    def tt(self, out, in0, in1, op, reads, writes, eng="dve"):
        self.P.op(eng, lambda e: e.tensor_tensor(out=out, in0=in0, in1=in1, op=op), reads=reads, writes=writes)

    def ts(self, out, in0, s1, op0, reads, writes, s2=None, op1=None, eng="dve"):
        if op1 is None:
            self.P.op(eng, lambda e: e.tensor_scalar(out=out, in0=in0, scalar1=s1, scalar2=None, op0=op0),
                      reads=reads, writes=writes)
        else:
            self.P.op(eng, lambda e: e.tensor_scalar(out=out, in0=in0, scalar1=s1, scalar2=s2, op0=op0, op1=op1),
                      reads=reads, writes=writes)

    def stt(self, out, in0, scalar, in1, op0, op1, reads, writes, eng="dve"):
        self.P.op(eng, lambda e: e.scalar_tensor_tensor(out=out, in0=in0, scalar=scalar, in1=in1, op0=op0, op1=op1),
                  reads=reads, writes=writes)

    def copy(self, eng, out, in_, reads, writes):
        if eng == "act":
            self.act(out, in_, AF.Copy, reads, writes)
        else:
            self.P.op(eng, lambda e: e.tensor_copy(out=out, in_=in_), reads=reads, writes=writes)

    def dma_sp(self, out, in_, reads, writes):
        self.ndma += 1
        self.P.dma("sp", lambda e: e.dma_start(out=out, in_=in_), "sp%d" % (self.ndma % 8), reads=reads, writes=writes)

    def wload(self, src, j, f, big):
        s = self.wslot % 4
        self.wslot += 1
        if big:
            off = O_W + s * 4096
            n = 4096
            keys = ["W%d" % (2 * s), "W%d" % (2 * s + 1)]
        else:
            off = O_W + s * 2048
            n = 2048
            keys = ["W%d" % s]
        assert j * f <= 2 * n
        view = self.arena[:, off:off + (j * f) // 2].bitcast(BF16).rearrange("p (j f) -> p j f", j=j)
        self.P.dma("pool", lambda e: e.dma_start(out=view, in_=src), "w%d%s" % (s, "b" if big else "s"), writes=keys)
        self._tick()
        return view, keys

    def wload_f32(self, src, j, f, rkeys=()):
        s = self.wslot % 4
        self.wslot += 1
        off = O_W + s * 2048
        keys = ["W%d" % s]
        view = self.arena[:, off:off + j * f].rearrange("p (j f) -> p j f", j=j)
        self.P.dma("sp", lambda e: e.dma_start(out=view, in_=src), "wf%d" % s, reads=list(rkeys), writes=keys)
        return view, keys

    def setup(self):
        cst = self.din("cst", [128, 512])
        fl = self.din("fl", [128, 24])
        self.dma_sp(self.CST, cst[:, 0:384], [], ["CST"])
        self.dma_sp(self.FL, fl, [], ["FL"])
        tmp = self.words(O_X, 128)
        self.dma_sp(tmp, cst[:, 384:512], [], ["TMPC"])
        self.copy("dve", self.IDb, self.CST[:, 0:128], ["CST"], ["IDb"])
        self.copy("dve", self.MCb, self.CST[:, 128:256], ["CST"], ["MCb"])
        self.copy("dve", self.MPb, tmp, ["TMPC"], ["MPb"])
        self.ID32 = self.CST[:, 0:128]
        self.TRI = self.CST[:, 256:384]
        self.P.barrier()

    def load_h(self, src):
        v = src.rearrange("(t p) d -> p t d", p=128)
        for i in range(4):
            self.dma_sp(self.H[:, 2 * i:2 * i + 2, :], v[:, 2 * i:2 * i + 2, :], [], ["H%d" % (2 * i), "H%d" % (2 * i + 1)])

    def store_h(self, dst):
        v = dst.rearrange("(t p) d -> p t d", p=128)
        for i in range(4):
            self.dma_sp(v[:, 2 * i:2 * i + 2, :], self.H[:, 2 * i:2 * i + 2, :], ["H%d" % (2 * i), "H%d" % (2 * i + 1)], ["OUT"])

    def finish(self):
        self.flush()
        self.P.barrier()
        self.P.emit()

    def norm(self, g, full_before=False, full_after=False):
        P = self.P
        P.barrier(full_before)
        g16 = self.words(O_X, 128)[0:16, :]
        self.dma_sp(g16, g.rearrange("(j p) -> j p", p=128), [], ["G16"])
        self.P.op("pe", lambda e: e.transpose(self.ps[:, 0, 0:16], g16, self.ID32[0:16, 0:16]), reads=["G16", "CST"], writes=["ps0"])
        self.copy("dve", self.GT, self.ps[:, 0, 0:16], [], ["ps0", "GT"])
        SS = self.STAT[:, 0:8]
        TM = self.STAT[:, 8:16]
        RS = self.STAT[:, 16:24]
        psT = self.bankb(6, 2)
        for t in range(NT):
            hk = "H%d" % t
            self.act(self.U, self.H[:, t, :], AF.Square, [hk], ["U", "SS%d" % t], accum_out=SS[:, t:t + 1])
            self.act(TM[:, t:t + 1], SS[:, t:t + 1], AF.Ln, ["SS%d" % t], ["TM%d" % t], scale=1.0 / D, bias=EPS)
            self.act(RS[:, t:t + 1], TM[:, t:t + 1], AF.Exp, ["TM%d" % t], ["RS%d" % t], scale=-0.5)
            self.ts(self.U, self.H[:, t, :], RS[:, t:t + 1], ALU.mult, [hk, "RS%d" % t], ["U"])
            for j in range(16):
                self.tp(psT[:, j * 128:(j + 1) * 128], self.U[:, j * 128:(j + 1) * 128], self.IDb, ["U", "IDb"], 6 + j // 8)
            self.tt(self.UT[:, :, t * 128:(t + 1) * 128], psT.rearrange("p (j c) -> p j c", j=16),
                    self.GT.unsqueeze(2).broadcast_to([128, 16, 128]), ALU.mult, ["GT"], ["ps6", "ps7", "UT"])
        if full_after:
            self.flush()
        P.barrier(full_after)

    def mlp(self, w1, w2):
        P = self.P
        HT = [self.words(O_X + i * 2048, 2048).bitcast(BF16).rearrange("p (c t) -> p c t", c=4) for i in range(2)]
        R = [self.words(O_X + 4096 + i * 512, 512) for i in range(2)]
        n1 = 0
        n2 = 0
        for fb in range(16):
            w1b, k1 = self.wload(w1[:, fb * 512:(fb + 1) * 512].rearrange("(j p) f -> p j f", p=128), 16, 512, True)
            w2b, k2 = self.wload(w2[fb * 512:(fb + 1) * 512, :].rearrange("(j p) f -> p j f", p=128), 4, 2048, True)
            ht = HT[fb % 2]
            hk = "HT%d" % (fb % 2)
            for fc in range(4):
                for half in range(2):
                    b = n1 % 2
                    r = n1 % 2
                    n1 += 1
                    for k in range(16):
                        self.mm(self.bank(b), w1b[:, k, fc * 128:(fc + 1) * 128], self.UT[:, k, half * 512:(half + 1) * 512],
                                k == 0, k == 15, k1 + ["UT"], b)
                    self.act(R[r], self.bank(b), AF.Relu, [], ["ps%d" % b, "R%d" % r])
                    self.tt(ht[:, fc, half * 512:(half + 1) * 512], R[r], R[r], ALU.mult, ["R%d" % r], [hk])
            for t in range(NT):
                for cb in range(4):
                    b = 2 + n2 % 4
                    n2 += 1
                    for fc in range(4):
                        self.mm(self.bank(b), ht[:, fc, t * 128:(t + 1) * 128], w2b[:, fc, cb * 512:(cb + 1) * 512],
                                fc == 0, fc == 3, k2 + [hk], b)
                    hv = self.H[:, t, cb * 512:(cb + 1) * 512]
                    self.tt(hv, self.bank(b), hv, ALU.add, ["H%d" % t], ["ps%d" % b, "H%d" % t])
        self.flush()
        P.barrier()

    def gla(self, w_in, w_g2, b_g, g_o, w_o, mode, sa_out=None, at_out=None, sa_in=None, at_in=None, gather=None):
        P = self.P
        X = O_X
        QT = self.words(X, 1024).bitcast(BF16).rearrange("p (c t) -> p c t", c=2)
        KT = self.words(X + 1024, 1024).bitcast(BF16).rearrange("p (c t) -> p c t", c=2)
        KD = self.words(X + 2048, 1024).bitcast(BF16).rearrange("p (t d) -> p t d", t=NT)
        V = self.words(X + 3072, 2048).bitcast(BF16).rearrange("p (t e) -> p t e", t=NT)
        S = self.words(X + 5120, 1024).rearrange("p (c e) -> p c e", c=2)
        SB = self.words(X + 6144, 512).bitcast(BF16).rearrange("p (c e) -> p c e", c=2)
        RSC = [self.words(X + 6656 + i * 256, 256) for i in range(2)]
        ATT = self.words(X + 7168, 64).bitcast(BF16)
        AT = self.words(X + 7232, 2)
        OSS = self.words(X + 7240, 8)
        OTM = self.words(X + 7248, 8)
        ORS = self.words(X + 7256, 8)
        CF = self.words(X + 7264, 2)
        GOv = self.words(X + 7296, 512)
        AG = self.words(X + 7808, 64)
        WX = O_WX
        EB = self.words(WX, 2048).rearrange("p (c t) -> p c t", c=2)
        ENB = self.words(WX + 2048, 2048).rearrange("p (c t) -> p c t", c=2)
        KDT = [self.words(WX + 2048 + c * 1024, 512).bitcast(BF16) for c in range(2)]
        L = self.words(WX + 4096, 2048).rearrange("p (t d) -> p t d", t=NT)
        ON = self.words(WX + 4096, 2048).bitcast(BF16).rearrange("p (t e) -> p t e", t=NT)
        GLR = self.words(WX + 6144, 1024)
        WG2 = self.words(WX + 7168, 1024)
        OGT = self.words(WX, 2048).bitcast(BF16).rearrange("p (c t) -> p c t", c=4)
        ST2 = self.words(WX + 2048, 1024).rearrange("p (c e) -> p c e", c=2)

        self.P.op("dve", lambda e: e.memset(GLR[0:32, :], 1.0), writes=["GLR"])
        self.dma_sp(WG2[0:16, :], w_g2, [], ["WG2"])
        self.dma_sp(WG2[16:17, :], b_g.rearrange("(o f) -> o f", o=1), [], ["WG2"])
        self.dma_sp(GOv, g_o.partition_broadcast(128), [], ["GO"])
        wg, kg = self.wload(w_in[:, 6144:6160].rearrange("(j p) f -> p j f", p=128), 16, 16, False)
        for half in range(2):
            for k in range(16):
                self.mm(self.ps[0:16, half, :], wg[:, k, :], self.UT[:, k, half * 512:(half + 1) * 512], k == 0, k == 15, kg + ["UT"], half)
            self.copy("act", GLR[0:16, half * 512:(half + 1) * 512], self.ps[0:16, half, :], [], ["ps%d" % half, "GLR"])

        nproj = 0
        for h in range(4):
            for t in range(NT):
                b = t % 2
                self.mm(self.bank(b, 256), GLR[0:17, t * 128:(t + 1) * 128], WG2[0:17, h * 256:(h + 1) * 256], True, True, ["GLR", "WG2"], b)
                self.act(L[:, t, :], self.bank(b, 256), AF.Exp, [], ["ps%d" % b, "L"], scale=-1.0)
                self.act(L[:, t, :], L[:, t, :], AF.Ln, ["L"], ["L"], bias=1.0)
            for c in range(2):
                pb = self.ps[:, 2 + 2 * c:4 + 2 * c, :].rearrange("p a b -> p (a b)")
                for t in range(NT):
                    self.mm(pb[:, t * 128:(t + 1) * 128], L[:, t, c * 128:(c + 1) * 128], self.TRI, True, True, ["L", "CST"], 2 + 2 * c + t // 4)
                bk = ["ps%d" % (2 + 2 * c), "ps%d" % (3 + 2 * c)]
                self.act(EB[:, c, :], pb, AF.Exp, [], bk + ["EB%d" % c])
                self.act(ENB[:, c, :], pb, AF.Exp, [], bk + ["ENB%d" % c], scale=-1.0)
                self.P.op("dve", lambda e, c=c: e.tensor_reduce(out=AT[:, c:c + 1], in_=EB[:, c, 127::128], axis=AX.X, op=ALU.mult),
                          reads=["EB%d" % c], writes=["AT"])
            wk, kk = self.wload(w_in[:, 1024 + h * 256:1024 + (h + 1) * 256].rearrange("(j p) f -> p j f", p=128), 16, 256, False)
            for c in range(2):
                for half in range(2):
                    b = nproj % 2
                    nproj += 1
                    for k in range(16):
                        self.mm(self.bank(b), wk[:, k, c * 128:(c + 1) * 128], self.UT[:, k, half * 512:(half + 1) * 512], k == 0, k == 15, kk + ["UT"], b)
                    self.tt(KT[:, c, half * 512:(half + 1) * 512], self.bank(b), ENB[:, c, half * 512:(half + 1) * 512], ALU.mult,
                            ["ENB%d" % c], ["ps%d" % b, "KT%d" % c])
                ebl = EB[:, c, 127::128].unsqueeze(2).broadcast_to([128, NT, 128])
                self.tt(KDT[c].rearrange("p (t i) -> p t i", t=NT), KT[:, c, :].rearrange("p (t i) -> p t i", t=NT), ebl, ALU.mult,
                        ["KT%d" % c, "EB%d" % c], ["ENB%d" % c])
            psT = self.bankb(6, 2)
            for t in range(NT):
                for c in range(2):
                    self.tp(psT[:, (t * 2 + c) * 128:(t * 2 + c + 1) * 128], KDT[c][:, t * 128:(t + 1) * 128], self.IDb, ["ENB%d" % c, "IDb"], 6 + t // 4)
            self.copy("act", KD.rearrange("p t d -> p (t d)"), psT, [], ["ps6", "ps7", "KD"])
            if mode == "B":
                wq, kq = self.wload(w_in[:, h * 256:(h + 1) * 256].rearrange("(j p) f -> p j f", p=128), 16, 256, False)
                for c in range(2):
                    for half in range(2):
                        b = nproj % 2
                        nproj += 1
                        for k in range(16):
                            self.mm(self.bank(b), wq[:, k, c * 128:(c + 1) * 128], self.UT[:, k, half * 512:(half + 1) * 512], k == 0, k == 15, kq + ["UT"], b)
                        self.stt(QT[:, c, half * 512:(half + 1) * 512], self.bank(b), 1.0 / 16.0, EB[:, c, half * 512:(half + 1) * 512],
                                 ALU.mult, ALU.mult, ["EB%d" % c], ["ps%d" % b, "QT"])
            for cb in range(2):
                wv, kv = self.wload(w_in[:, 2048 + h * 512 + cb * 256:2048 + h * 512 + (cb + 1) * 256].rearrange("(j p) f -> p j f", p=128), 16, 256, False)
                for t in range(NT):
                    b = nproj % 2
                    nproj += 1
                    for k in range(16):
                        self.mm(self.bank(b, 256), self.UT[:, k, t * 128:(t + 1) * 128], wv[:, k, :], k == 0, k == 15, kv + ["UT"], b)
                    self.copy("act", V[:, t, cb * 256:(cb + 1) * 256], self.bank(b, 256), [], ["ps%d" % b, "V%d" % t])
            if mode == "A":
                self.P.op("dve", lambda e: e.memset(S.rearrange("p c e -> p (c e)"), 0.0), writes=["S0", "S1"])
            else:
                self.P.op("dve", lambda e: e.memset(S.rearrange("p c e -> p (c e)"), 0.0), writes=["S0", "S1"])
                self.dma_sp(AG[:, 0:16].rearrange("p (q k) -> p q k", k=2), at_in(h), ["CCD"], ["AG"])
                for cp in range(NCORE - 1):
                    sl, ks = self.wload_f32(sa_in(cp, h), 2, 512, rkeys=["CCD"])
                    for c in range(2):
                        ai = cp * 2 + c
                        self.ts(CF[:, c:c + 1], AG[:, ai:ai + 1], self.FL[:, cp:cp + 1], ALU.mult, ["AG", "FL"], ["CF%d" % c],
                                s2=self.FL[:, 8 + cp:9 + cp], op1=ALU.add)
                        self.ts(ST2[:, c, :], sl[:, c, :], self.FL[:, cp:cp + 1], ALU.mult, ks + ["FL"], ["ENB0"])
                        self.stt(S[:, c, :], S[:, c, :], CF[:, c:c + 1], ST2[:, c, :], ALU.mult, ALU.add,
                                 ["S%d" % c, "CF%d" % c, "ENB0"], ["S%d" % c])
                for c in range(2):
                    self.copy("act", SB[:, c, :], S[:, c, :], ["S%d" % c], ["SB%d" % c])
            for t in range(NT):
                tk = slice(t * 128, (t + 1) * 128)
                if mode == "B":
                    for c in range(2):
                        self.mm(self.bank(2, 128), KT[:, c, tk], QT[:, c, tk], c == 0, c == 1, ["KT%d" % c, "QT"], 2)
                    self.tt(ATT, self.bank(2, 128), self.MCb, ALU.mult, ["MCb"], ["ps2", "ATT"])
                    self.mm(self.bank(3), ATT, V[:, t, :], True, False, ["ATT", "V%d" % t], 3)
                    for c in range(2):
                        self.mm(self.bank(3), QT[:, c, tk], SB[:, c, :], False, c == 1, ["QT", "SB%d" % c], 3)
                    self.act(self.U[:, 0:512], self.bank(3), AF.Square, [], ["ps3", "U", "OSS%d" % t], accum_out=OSS[:, t:t + 1])
                    self.act(OTM[:, t:t + 1], OSS[:, t:t + 1], AF.Ln, ["OSS%d" % t], ["OTM%d" % t], scale=1.0 / 512, bias=EPS)
                    self.act(ORS[:, t:t + 1], OTM[:, t:t + 1], AF.Exp, ["OTM%d" % t], ["ORS%d" % t], scale=-0.5)
                    self.stt(ON[:, t, :], self.bank(3), ORS[:, t:t + 1], GOv, ALU.mult, ALU.mult, ["ORS%d" % t, "GO"], ["ps3", "L"])
                if mode == "A" or t < NT - 1:
                    for c in range(2):
                        self.mm(self.bank(4 + c), KD[:, t, c * 128:(c + 1) * 128], V[:, t, :], True, True, ["KD", "V%d" % t], 4 + c)
                        self.stt(S[:, c, :], S[:, c, :], EB[:, c, t * 128 + 127:t * 128 + 128], self.bank(4 + c), ALU.mult, ALU.add,
                                 ["S%d" % c, "EB%d" % c], ["ps%d" % (4 + c), "S%d" % c])
                        if mode == "B":
                            self.copy("act", SB[:, c, :], S[:, c, :], ["S%d" % c], ["SB%d" % c])
            if mode == "A":
                self.dma_sp(sa_out(h), S, ["S0", "S1"], ["SAO%d" % h])
                self.dma_sp(at_out(h), AT, ["AT"], ["ATO%d" % h])
                if h == 3:
                    self.defer(gather, 3)
                P.barrier(False)
                continue
            for cb in range(2):
                wr, kr = self.wload(w_in[:, 4096 + h * 512 + cb * 256:4096 + h * 512 + (cb + 1) * 256].rearrange("(j p) f -> p j f", p=128), 16, 256, False)
                for t in range(NT):
                    b = nproj % 2
                    nproj += 1
                    for k in range(16):
                        self.mm(self.bank(b, 256), self.UT[:, k, t * 128:(t + 1) * 128], wr[:, k, :], k == 0, k == 15, kr + ["UT"], b)
                    self.act(RSC[b], self.bank(b, 256), AF.Silu, [], ["ps%d" % b, "RSC%d" % b])
                    ov = ON[:, t, cb * 256:(cb + 1) * 256]
                    self.tt(ov, ov, RSC[b], ALU.mult, ["L", "RSC%d" % b], ["L"])
            psT = self.bankb(6, 2)
            for t in range(NT):
                for ec in range(4):
                    self.tp(psT[:, (t % 4 * 4 + ec) * 128:(t % 4 * 4 + ec + 1) * 128], ON[:, t, ec * 128:(ec + 1) * 128], self.IDb, ["L", "IDb"], 6 + (t % 4) // 2)
                if t % 4 == 3:
                    t0 = t - 3
                    self.copy("act", OGT[:, :, t0 * 128:(t0 + 4) * 128].rearrange("p c (t i) -> p t c i", t=4),
                              psT.rearrange("p (t c i) -> p t c i", t=4, c=4), [], ["ps6", "ps7", "EB0", "EB1"])
            nw = 0
            for cbo in range(2):
                wo, ko = self.wload(w_o[h * 512:(h + 1) * 512, cbo * 1024:(cbo + 1) * 1024].rearrange("(j p) f -> p j f", p=128), 4, 1024, False)
                for t in range(NT):
                    for c2 in range(2):
                        b = 2 + nw % 4
                        nw += 1
                        for ec in range(4):
                            self.mm(self.bank(b), OGT[:, ec, t * 128:(t + 1) * 128], wo[:, ec, c2 * 512:(c2 + 1) * 512], ec == 0, ec == 3,
                                    ko + ["EB0", "EB1"], b)
                        hv = self.H[:, t, cbo * 1024 + c2 * 512:cbo * 1024 + (c2 + 1) * 512]
                        self.tt(hv, self.bank(b), hv, ALU.add, ["H%d" % t], ["ps%d" % b, "H%d" % t])
            P.barrier(False)

    def out_proj(self, ON, OGT, w_o, row0, kon, kogt):
        psT = self.bankb(6, 2)
        for t in range(NT):
            for ec in range(4):
                self.tp(psT[:, (t % 4 * 4 + ec) * 128:(t % 4 * 4 + ec + 1) * 128], ON[:, t, ec * 128:(ec + 1) * 128], self.IDb,
                        kon + ["IDb"], 6 + (t % 4) // 2)
            if t % 4 == 3:
                t0 = t - 3
                self.copy("act", OGT[:, :, t0 * 128:(t0 + 4) * 128].rearrange("p c (t i) -> p t c i", t=4),
                          psT.rearrange("p (t c i) -> p t c i", t=4, c=4), [], ["ps6", "ps7"] + kogt)
        nw = 0
        for cbo in range(2):
            wo, ko = self.wload(w_o[row0:row0 + 512, cbo * 1024:(cbo + 1) * 1024].rearrange("(j p) f -> p j f", p=128), 4, 1024, False)
            for t in range(NT):
                for c2 in range(2):
                    b = 2 + nw % 4
                    nw += 1
                    for ec in range(4):
                        self.mm(self.bank(b), OGT[:, ec, t * 128:(t + 1) * 128], wo[:, ec, c2 * 512:(c2 + 1) * 512], ec == 0, ec == 3,
                                ko + kogt, b)
                    hv = self.H[:, t, cbo * 1024 + c2 * 512:cbo * 1024 + (c2 + 1) * 512]
                    self.tt(hv, self.bank(b), hv, ALU.add, ["H%d" % t], ["ps%d" % b, "H%d" % t])

    def headnorm(self, b, QF, SQ, ST, QN, GB, tag):
        self.copy("act", QF, self.bank(b, 256), [], ["ps%d" % b, tag + "QF"])
        self.tt(SQ, QF, QF, ALU.mult, [tag + "QF"], [tag + "SQ"])
        self.P.op("dve", lambda e: e.tensor_reduce(out=ST[:, 0:4], in_=SQ.rearrange("p (h d) -> p h d", h=4), axis=AX.X, op=ALU.add),
                  reads=[tag + "SQ"], writes=[tag + "SS"])
        self.act(ST[:, 4:8], ST[:, 0:4], AF.Ln, [tag + "SS"], [tag + "TM"], scale=1.0 / 64, bias=EPS)
        self.act(ST[:, 8:12], ST[:, 4:8], AF.Exp, [tag + "TM"], [tag + "RS"], scale=-0.5)
        q3 = QF.rearrange("p (h d) -> p h d", h=4)
        self.tt(q3, q3, ST[:, 8:12].unsqueeze(2).broadcast_to([128, 4, 64]), ALU.mult, [tag + "QF", tag + "RS"], [tag + "QF"])
        self.tt(QN.rearrange("p (h d) -> p h d", h=4), q3, GB.unsqueeze(1).broadcast_to([128, 4, 64]), ALU.mult,
                [tag + "QF", tag + "GB"], [tag + "QN"])

    def kv_compute(self, w_k, w_v, g_k, kt_out, va_out, cc_src=None):
        P = self.P
        X = O_X
        KTs = self.words(X, 2048).bitcast(BF16).rearrange("p (h t) -> p h t", h=4)
        VA = self.words(X + 2048, 1040).bitcast(BF16).rearrange("p (t h e) -> p t h e", t=NT, h=4)
        QF = self.words(X + 3200, 256)
        SQ = self.words(X + 3456, 256)
        QN = self.words(X + 3712, 128).bitcast(BF16)
        ST = self.words(X + 3840, 16)
        GK = self.words(X + 3872, 64)
        self.dma_sp(GK, g_k.partition_broadcast(128), [], ["kGB"])
        self.P.op("dve", lambda e: e.memset(VA[:, :, :, 64:65], 1.0), writes=["VA"])
        wk, kk = self.wload(w_k.rearrange("(j p) f -> p j f", p=128), 16, 256, False)
        wv, kv = self.wload(w_v.rearrange("(j p) f -> p j f", p=128), 16, 256, False)
        psT = self.bankb(6, 1)
        for t in range(NT):
            b = t % 2
            for k in range(16):
                self.mm(self.bank(b, 256), self.UT[:, k, t * 128:(t + 1) * 128], wk[:, k, :], k == 0, k == 15, kk + ["UT"], b)
            self.headnorm(b, QF, SQ, ST, QN, GK, "k")
            for kh in range(4):
                self.tp(psT[0:64, kh * 128:(kh + 1) * 128], QN[:, kh * 64:(kh + 1) * 64], self.IDb, ["kQN", "IDb"], 6)
            self.copy("act", KTs[0:64, :, t * 128:(t + 1) * 128], psT[0:64, 0:512].rearrange("p (h i) -> p h i", h=4), [], ["ps6", "KTs"])
            b2 = 2 + t % 2
            for k in range(16):
                self.mm(self.bank(b2, 256), self.UT[:, k, t * 128:(t + 1) * 128], wv[:, k, :], k == 0, k == 15, kv + ["UT"], b2)
            self.copy("act", VA[:, t, :, 0:64], self.bank(b2, 256).rearrange("p (h d) -> p h d", h=4), [], ["ps%d" % b2, "VA"])
        self.dma_sp(kt_out, KTs[0:64], ["KTs"], ["KTO"])
        self.dma_sp(va_out, VA, ["VA"], ["VAO"])
        if cc_src is not None:
            wk_ = self.words(X, 2048)[0:64, :].rearrange("p (h w) -> p h w", h=4)[:, :, 448:512]
            self.dma_sp(cc_src[0:64, :].rearrange("p (h w) -> p h w", h=4), wk_, ["KTs"], ["CCS2"])
            self.dma_sp(cc_src[64:192, 0:130], self.words(X + 2048, 1040)[:, 910:1040], ["VA"], ["CCS2"])
        P.barrier(False)

    def swa(self, w_q, g_q, sinks, w_o, kt_in, kth_in, va_in, vah_in, cc_dst=None):
        P = self.P
        X = O_X
        WX = O_WX
        KTs = self.words(X, 2304).bitcast(BF16).rearrange("p (h t) -> p h t", h=4)
        VA = self.words(X + 2304, 1170).bitcast(BF16).rearrange("p (t h e) -> p t h e", t=9, h=4)
        QTg = self.words(X + 3488, 4096).bitcast(BF16).rearrange("p (h t) -> p h t", h=8)
        GQ = self.words(X + 7584, 64)
        ES = self.words(X + 7648, 32)
        ST = self.words(X + 7680, 16)
        DEN = self.words(X + 7696, 8)
        REC = self.words(X + 7704, 8)
        OA = self.words(WX, 2048).bitcast(BF16).rearrange("p (t e) -> p t e", t=NT)
        OAT = self.words(WX + 2048, 2048).bitcast(BF16).rearrange("p (c t) -> p c t", c=4)
        PT = [self.words(WX + 4096 + i * 1024, 1024).bitcast(BF16).rearrange("p (k n) -> p k n", k=2) for i in range(2)]
        QF = [self.words(WX + 6144 + i * 256, 256) for i in range(2)]
        SQ = self.words(WX + 6656, 256)
        QN = self.words(WX + 6912, 128).bitcast(BF16)
        self.dma_sp(KTs[0:64, :, 128:1152], kt_in, [], ["KTs"])
        self.dma_sp(VA[:, 1:9], va_in, [], ["VA"])
        if cc_dst is None:
            self.dma_sp(KTs[0:64, :, 0:128], kth_in, [], ["KTs"])
            self.dma_sp(VA[:, 0], vah_in, [], ["VA"])
        else:
            g3 = cc_dst.rearrange("(c r) w -> r c w", r=192)
            KTg = self.words(WX, 2048)[0:64, :].rearrange("p (c w) -> p c w", c=8)
            VAg = self.words(WX + 2048, 1040).rearrange("p (c w) -> p c w", c=8)
            self.dma_sp(KTg, g3[0:64], ["CCD2"], ["KTg"])
            self.dma_sp(VAg, g3[64:192, :, 0:130], ["CCD2"], ["VAg"])
            for cp in range(NCORE):
                kin = KTg[:, cp, :].bitcast(BF16).rearrange("p (h i) -> p h i", h=4)
                vin = VAg[:, cp, :].bitcast(BF16).rearrange("p (h e) -> p h e", h=4)
                f = self.FL[:, 16 + cp:17 + cp]
                if cp == 0:
                    self.ts(KTs[0:64, :, 0:128], kin, f[0:64], ALU.mult, ["KTg", "FL"], ["KTs"])
                    self.ts(VA[:, 0], vin, f, ALU.mult, ["VAg", "FL"], ["VA"])
                else:
                    self.stt(KTs[0:64, :, 0:128], kin, f[0:64], KTs[0:64, :, 0:128], ALU.mult, ALU.add, ["KTg", "FL", "KTs"], ["KTs"])
                    self.stt(VA[:, 0], vin, f, VA[:, 0], ALU.mult, ALU.add, ["VAg", "FL", "VA"], ["VA"])
            P.barrier(False)
        self.dma_sp(GQ, g_q.partition_broadcast(128), [], ["GQraw"])
        self.act(GQ, GQ, AF.Copy, ["GQraw"], ["qGB"], scale=0.125)
        self.dma_sp(ES, sinks.partition_broadcast(128), [], ["ESraw"])
        self.act(ES, ES, AF.Exp, ["ESraw"], ["ES"])
        psT = self.bankb(6, 1)
        nq = 0
        ns = 0
        for g in range(4):
            for blk in range(2):
                wq, kq = self.wload(w_q[:, g * 512 + blk * 256:g * 512 + (blk + 1) * 256].rearrange("(j p) f -> p j f", p=128), 16, 256, False)
                for t in range(NT):
                    b = nq % 2
                    nq += 1
                    for k in range(16):
                        self.mm(self.bank(b, 256), self.UT[:, k, t * 128:(t + 1) * 128], wq[:, k, :], k == 0, k == 15, kq + ["UT"], b)
                    self.headnorm(b, QF[b], SQ, ST, QN, GQ, "q")
                    for hd in range(4):
                        self.tp(psT[0:64, hd * 128:(hd + 1) * 128], QN[:, hd * 64:(hd + 1) * 64], self.IDb, ["qQN", "IDb"], 6)
                    self.copy("act", QTg[0:64, blk * 4:(blk + 1) * 4, t * 128:(t + 1) * 128],
                              psT[0:64, 0:512].rearrange("p (h i) -> p h i", h=4), [], ["ps6", "QTg"])
            for t in range(NT):
                pt = PT[t % 2]
                pk = "PT%d" % (t % 2)
                for kb in range(2):
                    kt = t + kb
                    for hh in range(2):
                        b = 2 + ns % 2
                        ns += 1
                        self.mm(self.bank(b), KTs[0:64, g, kt * 128:(kt + 1) * 128], QTg[0:64, hh * 4:(hh + 1) * 4, t * 128:(t + 1) * 128],
                                True, True, ["KTs", "QTg"], b)
                        self.act(pt[:, kb, hh * 512:(hh + 1) * 512], self.bank(b), AF.Exp, [], ["ps%d" % b, pk])
                    msk = self.MPb if kb == 0 else self.MCb
                    pv = pt[:, kb, :].rearrange("p (h i) -> p h i", h=8)
                    self.tt(pv, pv, msk.unsqueeze(1).broadcast_to([128, 8, 128]), ALU.mult, [pk, "MPb", "MCb"], [pk])
                for hd in range(8):
                    bo = 4 + hd // 4
                    o = self.ps[:, bo, (hd % 4) * 65:(hd % 4) * 65 + 65]
                    for kb in range(2):
                        self.mm(o, pt[:, kb, hd * 128:(hd + 1) * 128], VA[:, t + kb, g, :], kb == 0, kb == 1, [pk, "VA"], bo)
                for bb in range(2):
                    pso = self.ps[:, 4 + bb, 0:260].rearrange("p (h e) -> p h e", h=4)
                    dn = DEN[:, bb * 4:(bb + 1) * 4]
                    self.tt(dn.unsqueeze(2), pso[:, :, 64:65], ES[:, g * 8 + bb * 4:g * 8 + bb * 4 + 4].unsqueeze(2), ALU.add, ["ES"],
                            ["ps%d" % (4 + bb), "DEN%d" % bb])
                    self.P.op("dve", lambda e, bb=bb, dn=dn: e.reciprocal(out=REC[:, bb * 4:(bb + 1) * 4], in_=dn),
                              reads=["DEN%d" % bb], writes=["REC%d" % bb])
                    self.tt(OA[:, t, bb * 256:(bb + 1) * 256].rearrange("p (h d) -> p h d", h=4), pso[:, :, 0:64],
                            REC[:, bb * 4:(bb + 1) * 4].unsqueeze(2).broadcast_to([128, 4, 64]), ALU.mult, ["REC%d" % bb],
                            ["ps%d" % (4 + bb), "OA"])
            self.out_proj(OA, OAT, w_o, g * 512, ["OA"], ["OAT"])
        P.barrier(False)


def _consts():
    c = np.zeros((128, 512), np.float32)
    j = np.arange(128)[:, None]
    i = np.arange(128)[None, :]
    c[:, 0:128] = np.eye(128, dtype=np.float32)
    c[:, 128:256] = (j <= i)
    c[:, 256:384] = (j <= i) * (-1.0 / 16.0)
    c[:, 384:512] = (j > i)
    return c


def _flags(core):
    f = np.zeros((128, 24), np.float32)
    for cp in range(8):
        m = 1.0 if cp < core else 0.0
        f[:, cp] = m
        f[:, 8 + cp] = 1.0 - m
        f[:, 16 + cp] = 1.0 if cp == core - 1 else 0.0
    return f


def _run(build, in_maps):
    nc = bass.Bass("TRN2", target_bir_lowering=False)
    with contextlib.ExitStack() as st:
        core = Core(nc, st)
        core.setup()
        build(core)
        core.finish()
    for c in range(NCORE):
        in_maps[c]["cst"] = _consts()
        in_maps[c]["fl"] = _flags(c)
    res = run_bass_kernel_spmd(nc, in_maps, core_ids=list(range(NCORE)))
    return res.results


def build_fused(k):
    x = k.din("x", [TOK, D])
    norm_mix = k.din("norm_mix", [4, D])
    norm_mlp = k.din("norm_mlp", [4, D])
    mlp_w1 = k.din("mlp_w1", [4, D, 4 * D])
    mlp_w2 = k.din("mlp_w2", [4, 4 * D, D])
    a_w_in = k.din("a_w_in", [2, D, 6160])
    a_w_g2 = k.din("a_w_g2", [2, 16, 1024])
    a_b_g = k.din("a_b_g", [2, 1024])
    a_g_o = k.din("a_g_o", [2, 512])
    a_w_o = k.din("a_w_o", [2, D, D])
    kv_norm = k.din("kv_norm", [D])
    kv_w_k = k.din("kv_w_k", [D, 256])
    kv_w_v = k.din("kv_w_v", [D, 256])
    kv_g_k = k.din("kv_g_k", [64])
    b_w_q = k.din("b_w_q", [2, D, D])
    b_g_q = k.din("b_g_q", [2, 64])
    b_sinks = k.din("b_sinks", [2, 32])
    b_w_o = k.din("b_w_o", [2, D, D])
    out = k.dout("out", [TOK, D])
    k.load_h(x)
    for i in range(2):
        cs = k.dscratch("cc_src%d" % i, [512, 1026])
        cd = k.dscratch("cc_dst%d" % i, [NCORE * 512, 1026])

        def gather(i=i, cs=cs, cd=cd):
            k.allgather(cs, cd, ["SAO0", "SAO1", "SAO2", "SAO3", "ATO0", "ATO1", "ATO2", "ATO3"], ["CCD"], "cc%d" % i)

        k.norm(norm_mix[i])
        k.gla(a_w_in[i], a_w_g2[i], a_b_g[i], a_g_o[i], None, "A", gather=gather,
              sa_out=lambda h, cs=cs: cs[h * 128:(h + 1) * 128, 0:1024].rearrange("p (c e) -> p c e", c=2),
              at_out=lambda h, cs=cs: cs[h * 128:(h + 1) * 128, 1024:1026])
        k.gla(a_w_in[i], a_w_g2[i], a_b_g[i], a_g_o[i], a_w_o[i], "B",
              sa_in=lambda cp, h, cd=cd: cd[cp * 512 + h * 128:cp * 512 + (h + 1) * 128, 0:1024].rearrange("p (c e) -> p c e", c=2),
              at_in=lambda h, cd=cd: cd[:, 1024:1026].rearrange("(c h d) k -> h d c k", c=NCORE, h=4)[h])
        k.norm(norm_mlp[i], full_after=True)
        k.mlp(mlp_w1[i], mlp_w2[i])
    kvk = k.dscratch("kvs_k", [64, 4, 1024], BF16)
    kvv = k.dscratch("kvs_v", [128, 8, 4, 65], BF16)
    cs2 = k.dscratch("cc_src2", [192, 256])
    cd2 = k.dscratch("cc_dst2", [NCORE * 192, 256])
    k.norm(kv_norm)
    k.kv_compute(kv_w_k, kv_w_v, kv_g_k, kvk, kvv, cc_src=cs2)
    k.allgather(cs2, cd2, ["CCS2"], ["CCD2"], "cc2")
    for j in range(2):
        k.norm(norm_mix[2 + j])
        k.swa(b_w_q[j], b_g_q[j], b_sinks[j], b_w_o[j], kvk, None, kvv, None, cc_dst=cd2)
        k.norm(norm_mlp[2 + j], full_after=True)
        k.mlp(mlp_w1[2 + j], mlp_w2[2 + j])
    k.store_h(out)


def kernel(**inp):
    inp = {k: np.ascontiguousarray(np.asarray(v, dtype=np.float32)) for k, v in inp.items()}
    x = inp["x"][0].reshape(NCORE, TOK, D)
    maps = []
    for c in range(NCORE):
        m = {k: v for k, v in inp.items() if k != "x"}
        m["x"] = x[c]
        maps.append(m)
    res = _run(build_fused, maps)
    out = np.concatenate([r["out"] for r in res], 0)
    return out.reshape(1, NCORE * TOK, D).astype(np.float32)
```

```python
import contextlib
import numpy as np
import concourse.bass as bass
import concourse.mybir as mybir
from concourse.bass_utils import run_bass_kernel_spmd

F32 = mybir.dt.float32
BF16 = mybir.dt.bfloat16
AF = mybir.ActivationFunctionType
ALU = mybir.AluOpType
AX = mybir.AxisListType

ENGS = ("pe", "act", "dve", "pool", "sp")
EPOCH = 30000
NCORE = 8
D = 2048
TOK = 1024
NT = 8
EPS = 1e-6


class Prog:
    def __init__(self, nc):
        self.nc = nc
        self.ops = {e: [] for e in ENGS}
        self.cnt = {e: 0 for e in ENGS}
        self.seen = {e: {} for e in ENGS}
        self.lastw = {}
        self.readers = {}
        self.dma_cnt = {}
        self.semkeys = set()
        self.async_sems = set()

    def _deps(self, eng, reads, writes):
        deps = {}

        def need(tok, raw):
            base, ep, v = tok
            if base == eng and (eng == "pe" or not raw):
                return
            cur = deps.get(base)
            if cur is None or (ep, v) > cur:
                deps[base] = (ep, v)

        for k in reads:
            t = self.lastw.get(k)
            if t is not None:
                need(t, True)
        for k in writes:
            t = self.lastw.get(k)
            if t is not None:
                need(t, k in reads)
            for base, (ep, v) in self.readers.get(k, {}).items():
                need((base, ep, v), False)
        waits = []
        seen = self.seen[eng]
        for base, (ep, v) in deps.items():
            s = seen.get(base)
            if s is None or s < (ep, v):
                seen[base] = (ep, v)
                waits.append((base, ep, v))
        return waits

    def _record(self, tok, reads, writes):
        base, ep, v = tok
        for k in reads:
            if k in writes:
                continue
            self.readers.setdefault(k, {})[base] = (ep, v)
        for k in writes:
            self.lastw[k] = tok
            self.readers[k] = {}

    def op(self, eng, fn, reads=(), writes=()):
        reads = tuple(reads)
        writes = tuple(writes)
        waits = self._deps(eng, reads, writes)
        self.cnt[eng] += 1
        seq = self.cnt[eng]
        sk = (eng, (seq - 1) // EPOCH)
        self.semkeys.add(sk)
        self.ops[eng].append((fn, waits, (sk, 1)))
        self._record((eng, (seq - 1) // EPOCH, (seq - 1) % EPOCH + 1), reads, writes)

    def dma(self, eng, fn, sem, reads=(), writes=(), inc=16):
        reads = tuple(reads)
        writes = tuple(writes)
        waits = self._deps(eng, reads, writes)
        base = "d:" + sem
        self.dma_cnt[base] = self.dma_cnt.get(base, 0) + inc
        v = self.dma_cnt[base]
        self.semkeys.add((base, 0))
        self.ops[eng].append((fn, waits, ((base, 0), inc)))
        self._record((base, 0, v), reads, writes)

    STICKY = ("W", "SAO", "ATO", "CCS", "CCD")

    def barrier(self, full=True):
        toks = []
        for e in ENGS:
            if self.cnt[e] > 0:
                toks.append((e, (self.cnt[e] - 1) // EPOCH, (self.cnt[e] - 1) % EPOCH + 1))
        for base, v in self.dma_cnt.items():
            if not full and base in self.async_sems:
                continue
            toks.append((base, 0, v))
        for e in ENGS:
            if not full and e == "pool":
                continue
            waits = []
            for base, ep, v in toks:
                if base == e:
                    continue
                s = self.seen[e].get(base)
                if s is None or s < (ep, v):
                    self.seen[e][base] = (ep, v)
                    waits.append((base, ep, v))
            if waits:
                self.ops[e].append((None, waits, None))
        if full:
            self.lastw = {}
            self.readers = {}
        else:
            self.lastw = {k: v for k, v in self.lastw.items() if k.startswith(self.STICKY)}
            self.readers = {k: v for k, v in self.readers.items() if k.startswith(self.STICKY)}

    def emit(self):
        nc = self.nc
        with contextlib.ExitStack() as st:
            sems = {}
            for i, sk in enumerate(sorted(self.semkeys, key=str)):
                sems[sk] = st.enter_context(nc.semaphore("s%d" % i))
            block = st.enter_context(nc.Block())

            def run(engname):
                def body(e):
                    for fn, waits, inc in self.ops[engname]:
                        for base, ep, v in waits:
                            e.wait_ge(sems[(base, ep)], v)
                        if fn is not None:
                            ins = fn(e)
                            if inc is not None:
                                ins.then_inc(sems[inc[0]], inc[1])
                return body

            block.tensor(run("pe"))
            block.scalar(run("act"))
            block.vector(run("dve"))
            block.gpsimd(run("pool"))
            block.sync(run("sp"))


O_H = 0
O_UT = 16384
O_W = 24576
O_WX = 24576 + 8192
O_X = 40960
O_M = 49152
ARENA = 51200


class Core:
    def __init__(self, nc, st):
        self.nc = nc
        self.P = Prog(nc)
        self.arena = st.enter_context(nc.sbuf_tensor("arena", [128, ARENA], F32))
        self.ps = st.enter_context(nc.psum_tensor("ps", [128, 8, 512], F32))
        a = self.arena
        self.H = a[:, O_H:O_H + 16384].rearrange("p (t d) -> p t d", t=NT)
        self.UT = a[:, O_UT:O_UT + 8192].bitcast(BF16).rearrange("p (j t) -> p j t", j=16)
        m = O_M
        self.U = a[:, m:m + 1024].bitcast(BF16)
        m += 1024
        self.IDb = a[:, m:m + 64].bitcast(BF16)
        m += 64
        self.MCb = a[:, m:m + 64].bitcast(BF16)
        m += 64
        self.MPb = a[:, m:m + 64].bitcast(BF16)
        m += 64
        self.CST = a[:, m:m + 384]
        m += 384
        self.GT = a[:, m:m + 16]
        m += 16
        self.STAT = a[:, m:m + 64]
        m += 64
        self.FL = a[:, m:m + 24]
        m += 24
        assert m <= ARENA, m
        self.wslot = 0
        self.ndma = 0
        self.pending = []
        self.inputs = {}
        self.outputs = {}

    def din(self, name, shape, dt=F32):
        t = self.nc.dram_tensor(name, list(shape), dt, kind="ExternalInput").ap()
        self.inputs[name] = t
        return t

    def dout(self, name, shape, dt=F32):
        t = self.nc.dram_tensor(name, list(shape), dt, kind="ExternalOutput").ap()
        self.outputs[name] = t
        return t

    def dscratch(self, name, shape, dt=F32):
        return self.nc.dram_tensor(name, list(shape), dt).ap()

    def defer(self, fn, nloads):
        self.pending.append([nloads, fn])

    def _tick(self):
        for p in self.pending:
            p[0] -= 1
        while self.pending and self.pending[0][0] <= 0:
            self.pending.pop(0)[1]()

    def flush(self):
        while self.pending:
            self.pending.pop(0)[1]()

    def allgather(self, src, dst, rkeys, wkeys, sem):
        self.P.async_sems.add("d:" + sem)
        self.P.dma("pool", lambda e: e.collective_compute("AllGather", ALU.bypass, replica_groups=[list(range(NCORE))],
                                                          ins=[src.opt()], outs=[dst.opt()]),
                   sem, reads=rkeys, writes=wkeys, inc=1)

    def words(self, off, n):
        return self.arena[:, off:off + n]

    def bank(self, b, n=512):
        return self.ps[:, b, 0:n]

    def bankb(self, b0, nb=2):
        return self.ps[:, b0:b0 + nb, :].rearrange("p a b -> p (a b)").bitcast(BF16)

    def mm(self, out, lhsT, rhs, start, stop, reads, bank):
        self.P.op("pe", lambda e: e.matmul(out, lhsT=lhsT, rhs=rhs, start=start, stop=stop),
                  reads=reads, writes=["ps%d" % bank])

    def tp(self, out, in_, ident, reads, bank):
        self.P.op("pe", lambda e: e.transpose(out, in_, ident), reads=reads, writes=["ps%d" % bank])

    def act(self, out, in_, func, reads, writes, scale=1.0, bias=None, accum_out=None):
        kw = {}
        if bias is not None:
            kw["bias"] = bias
        if accum_out is not None:
            kw["accum_out"] = accum_out
        self.P.op("act", lambda e: e.activation(out=out, in_=in_, func=func, scale=scale, **kw),
                  reads=reads, writes=writes)

    def tt(self, out, in0, in1, op, reads, writes, eng="dve"):
        self.P.op(eng, lambda e: e.tensor_tensor(out=out, in0=in0, in1=in1, op=op), reads=reads, writes=writes)

    def ts(self, out, in0, s1, op0, reads, writes, s2=None, op1=None, eng="dve"):
        if op1 is None:
            self.P.op(eng, lambda e: e.tensor_scalar(out=out, in0=in0, scalar1=s1, scalar2=None, op0=op0),
                      reads=reads, writes=writes)
        else:
            self.P.op(eng, lambda e: e.tensor_scalar(out=out, in0=in0, scalar1=s1, scalar2=s2, op0=op0, op1=op1),
                      reads=reads, writes=writes)

    def stt(self, out, in0, scalar, in1, op0, op1, reads, writes, eng="dve"):
        self.P.op(eng, lambda e: e.scalar_tensor_tensor(out=out, in0=in0, scalar=scalar, in1=in1, op0=op0, op1=op1),
                  reads=reads, writes=writes)

    def copy(self, eng, out, in_, reads, writes):
        if eng == "act":
            self.act(out, in_, AF.Copy, reads, writes)
        else:
            self.P.op(eng, lambda e: e.tensor_copy(out=out, in_=in_), reads=reads, writes=writes)

    def dma_sp(self, out, in_, reads, writes):
        self.ndma += 1
        self.P.dma("sp", lambda e: e.dma_start(out=out, in_=in_), "sp%d" % (self.ndma % 8), reads=reads, writes=writes)

    def wload(self, src, j, f, big):
        s = self.wslot % 4
        self.wslot += 1
        if big:
            off = O_W + s * 4096
            n = 4096
            keys = ["W%d" % (2 * s), "W%d" % (2 * s + 1)]
        else:
            off = O_W + s * 2048
            n = 2048
            keys = ["W%d" % s]
        assert j * f <= 2 * n
        view = self.arena[:, off:off + (j * f) // 2].bitcast(BF16).rearrange("p (j f) -> p j f", j=j)
        self.P.dma("pool", lambda e: e.dma_start(out=view, in_=src), "w%d%s" % (s, "b" if big else "s"), writes=keys)
        self._tick()
        return view, keys

    def wload_f32(self, src, j, f, rkeys=()):
        s = self.wslot % 4
        self.wslot += 1
        off = O_W + s * 2048
        keys = ["W%d" % s]
        view = self.arena[:, off:off + j * f].rearrange("p (j f) -> p j f", j=j)
        self.P.dma("sp", lambda e: e.dma_start(out=view, in_=src), "wf%d" % s, reads=list(rkeys), writes=keys)
        return view, keys

    def setup(self):
        cst = self.din("cst", [128, 512])
        fl = self.din("fl", [128, 24])
        self.dma_sp(self.CST, cst[:, 0:384], [], ["CST"])
        self.dma_sp(self.FL, fl, [], ["FL"])
        tmp = self.words(O_X, 128)
        self.dma_sp(tmp, cst[:, 384:512], [], ["TMPC"])
        self.copy("dve", self.IDb, self.CST[:, 0:128], ["CST"], ["IDb"])
        self.copy("dve", self.MCb, self.CST[:, 128:256], ["CST"], ["MCb"])
        self.copy("dve", self.MPb, tmp, ["TMPC"], ["MPb"])
        self.ID32 = self.CST[:, 0:128]
        self.TRI = self.CST[:, 256:384]
        self.P.barrier()

    def load_h(self, src):
        v = src.rearrange("(t p) d -> p t d", p=128)
        for i in range(4):
            self.dma_sp(self.H[:, 2 * i:2 * i + 2, :], v[:, 2 * i:2 * i + 2, :], [], ["H%d" % (2 * i), "H%d" % (2 * i + 1)])

    def store_h(self, dst):
        v = dst.rearrange("(t p) d -> p t d", p=128)
        for i in range(4):
            self.dma_sp(v[:, 2 * i:2 * i + 2, :], self.H[:, 2 * i:2 * i + 2, :], ["H%d" % (2 * i), "H%d" % (2 * i + 1)], ["OUT"])

    def finish(self):
        self.flush()
        self.P.barrier()
        self.P.emit()

    def norm(self, g, full_before=False, full_after=False):
        P = self.P
        P.barrier(full_before)
        g16 = self.words(O_X, 128)[0:16, :]
        self.dma_sp(g16, g.rearrange("(j p) -> j p", p=128), [], ["G16"])
        self.P.op("pe", lambda e: e.transpose(self.ps[:, 0, 0:16], g16, self.ID32[0:16, 0:16]), reads=["G16", "CST"], writes=["ps0"])
        self.copy("dve", self.GT, self.ps[:, 0, 0:16], [], ["ps0", "GT"])
        SS = self.STAT[:, 0:8]
        TM = self.STAT[:, 8:16]
        RS = self.STAT[:, 16:24]
        UU = [self.words(O_X + 256 + i * 1024, 1024).bitcast(BF16) for i in range(2)]
        psTs = [self.bankb(4, 2), self.bankb(6, 2)]

        def nA(t):
            hk = "H%d" % t
            u = UU[t % 2]
            uk = "UU%d" % (t % 2)
            self.act(self.U, self.H[:, t, :], AF.Square, [hk], ["U", "SS%d" % t], accum_out=SS[:, t:t + 1])
            self.act(TM[:, t:t + 1], SS[:, t:t + 1], AF.Ln, ["SS%d" % t], ["TM%d" % t], scale=1.0 / D, bias=EPS)
            self.act(RS[:, t:t + 1], TM[:, t:t + 1], AF.Exp, ["TM%d" % t], ["RS%d" % t], scale=-0.5)
            self.ts(u, self.H[:, t, :], RS[:, t:t + 1], ALU.mult, [hk, "RS%d" % t], [uk])

        def nB(t):
            u = UU[t % 2]
            uk = "UU%d" % (t % 2)
            psT = psTs[t % 2]
            b0 = 4 + 2 * (t % 2)
            for j in range(16):
                self.tp(psT[:, j * 128:(j + 1) * 128], u[:, j * 128:(j + 1) * 128], self.IDb, [uk, "IDb"], b0 + j // 8)
            self.tt(self.UT[:, :, t * 128:(t + 1) * 128], psT.rearrange("p (j c) -> p j c", j=16),
                    self.GT.unsqueeze(2).broadcast_to([128, 16, 128]), ALU.mult, ["GT"], ["ps%d" % b0, "ps%d" % (b0 + 1), "UT"])

        for t in range(NT + 1):
            if t < NT:
                nA(t)
            if t >= 1:
                nB(t - 1)
        if full_after:
            self.flush()
        P.barrier(full_after)

    def mlp(self, w1, w2):
        P = self.P
        HT = [self.words(O_X + i * 2048, 2048).bitcast(BF16).rearrange("p (c t) -> p c t", c=4) for i in range(2)]
        R = [self.words(O_X + 4096 + i * 512, 512) for i in range(2)]
        n1 = 0
        n2 = 0
        for fb in range(16):
            w1b, k1 = self.wload(w1[:, fb * 512:(fb + 1) * 512].rearrange("(j p) f -> p j f", p=128), 16, 512, True)
            w2b, k2 = self.wload(w2[fb * 512:(fb + 1) * 512, :].rearrange("(j p) f -> p j f", p=128), 4, 2048, True)
            ht = HT[fb % 2]
            hk = "HT%d" % (fb % 2)
            for fc in range(4):
                for half in range(2):
                    b = n1 % 2
                    r = n1 % 2
                    n1 += 1
                    for k in range(16):
                        self.mm(self.bank(b), w1b[:, k, fc * 128:(fc + 1) * 128], self.UT[:, k, half * 512:(half + 1) * 512],
                                k == 0, k == 15, k1 + ["UT"], b)
                    self.act(R[r], self.bank(b), AF.Relu, [], ["ps%d" % b, "R%d" % r])
                    self.tt(ht[:, fc, half * 512:(half + 1) * 512], R[r], R[r], ALU.mult, ["R%d" % r], [hk])
            for t in range(NT):
                for cb in range(4):
                    b = 2 + n2 % 4
                    n2 += 1
                    for fc in range(4):
                        self.mm(self.bank(b), ht[:, fc, t * 128:(t + 1) * 128], w2b[:, fc, cb * 512:(cb + 1) * 512],
                                fc == 0, fc == 3, k2 + [hk], b)
                    hv = self.H[:, t, cb * 512:(cb + 1) * 512]
                    self.tt(hv, self.bank(b), hv, ALU.add, ["H%d" % t], ["ps%d" % b, "H%d" % t])
        self.flush()
        P.barrier()

    def gla(self, w_in, w_g2, b_g, g_o, w_o, mode, sa_out=None, at_out=None, sa_in=None, at_in=None, gather=None):
        P = self.P
        X = O_X
        QT = self.words(X, 1024).bitcast(BF16).rearrange("p (c t) -> p c t", c=2)
        KT = self.words(X + 1024, 1024).bitcast(BF16).rearrange("p (c t) -> p c t", c=2)
        KD = self.words(X + 2048, 1024).bitcast(BF16).rearrange("p (t d) -> p t d", t=NT)
        V = self.words(X + 3072, 2048).bitcast(BF16).rearrange("p (t e) -> p t e", t=NT)
        S = self.words(X + 5120, 1024).rearrange("p (c e) -> p c e", c=2)
        SB = self.words(X + 6144, 512).bitcast(BF16).rearrange("p (c e) -> p c e", c=2)
        RSC = [self.words(X + 6656 + i * 256, 256) for i in range(2)]
        ATT = self.words(X + 7168, 64).bitcast(BF16)
        AT = self.words(X + 7232, 2)
        OSS = self.words(X + 7240, 8)
        OTM = self.words(X + 7248, 8)
        ORS = self.words(X + 7256, 8)
        CF = self.words(X + 7264, 2)
        GOv = self.words(X + 7296, 512)
        AG = self.words(X + 7808, 64)
        WX = O_WX
        EB = self.words(WX, 2048).rearrange("p (c t) -> p c t", c=2)
        ENB = self.words(WX + 2048, 2048).rearrange("p (c t) -> p c t", c=2)
        KDT = [self.words(WX + 2048 + c * 1024, 512).bitcast(BF16) for c in range(2)]
        L = self.words(WX + 4096, 2048).rearrange("p (t d) -> p t d", t=NT)
        ON = self.words(WX + 4096, 2048).bitcast(BF16).rearrange("p (t e) -> p t e", t=NT)
        GLR = self.words(WX + 6144, 1024)
        WG2 = self.words(WX + 7168, 1024)
        OGT = self.words(WX, 2048).bitcast(BF16).rearrange("p (c t) -> p c t", c=4)
        ST2 = self.words(WX + 2048, 1024).rearrange("p (c e) -> p c e", c=2)

        self.P.op("dve", lambda e: e.memset(GLR[0:32, :], 1.0), writes=["GLR"])
        self.dma_sp(WG2[0:16, :], w_g2, [], ["WG2"])
        self.dma_sp(WG2[16:17, :], b_g.rearrange("(o f) -> o f", o=1), [], ["WG2"])
        self.dma_sp(GOv, g_o.partition_broadcast(128), [], ["GO"])
        wg, kg = self.wload(w_in[:, 6144:6160].rearrange("(j p) f -> p j f", p=128), 16, 16, False)
        for half in range(2):
            for k in range(16):
                self.mm(self.ps[0:16, half, :], wg[:, k, :], self.UT[:, k, half * 512:(half + 1) * 512], k == 0, k == 15, kg + ["UT"], half)
            self.copy("act", GLR[0:16, half * 512:(half + 1) * 512], self.ps[0:16, half, :], [], ["ps%d" % half, "GLR"])

        nproj = 0
        for h in range(4):
            for t in range(NT):
                b = t % 2
                self.mm(self.bank(b, 256), GLR[0:17, t * 128:(t + 1) * 128], WG2[0:17, h * 256:(h + 1) * 256], True, True, ["GLR", "WG2"], b)
                self.act(L[:, t, :], self.bank(b, 256), AF.Exp, [], ["ps%d" % b, "L"], scale=-1.0)
                self.act(L[:, t, :], L[:, t, :], AF.Ln, ["L"], ["L"], bias=1.0)
            for c in range(2):
                pb = self.ps[:, 2 + 2 * c:4 + 2 * c, :].rearrange("p a b -> p (a b)")
                for t in range(NT):
                    self.mm(pb[:, t * 128:(t + 1) * 128], L[:, t, c * 128:(c + 1) * 128], self.TRI, True, True, ["L", "CST"], 2 + 2 * c + t // 4)
                bk = ["ps%d" % (2 + 2 * c), "ps%d" % (3 + 2 * c)]
                self.act(EB[:, c, :], pb, AF.Exp, [], bk + ["EB%d" % c])
                self.act(ENB[:, c, :], pb, AF.Exp, [], bk + ["ENB%d" % c], scale=-1.0)
                self.P.op("dve", lambda e, c=c: e.tensor_reduce(out=AT[:, c:c + 1], in_=EB[:, c, 127::128], axis=AX.X, op=ALU.mult),
                          reads=["EB%d" % c], writes=["AT"])
            wk, kk = self.wload(w_in[:, 1024 + h * 256:1024 + (h + 1) * 256].rearrange("(j p) f -> p j f", p=128), 16, 256, False)
            for c in range(2):
                for half in range(2):
                    b = nproj % 2
                    nproj += 1
                    for k in range(16):
                        self.mm(self.bank(b), wk[:, k, c * 128:(c + 1) * 128], self.UT[:, k, half * 512:(half + 1) * 512], k == 0, k == 15, kk + ["UT"], b)
                    self.tt(KT[:, c, half * 512:(half + 1) * 512], self.bank(b), ENB[:, c, half * 512:(half + 1) * 512], ALU.mult,
                            ["ENB%d" % c], ["ps%d" % b, "KT%d" % c])
                ebl = EB[:, c, 127::128].unsqueeze(2).broadcast_to([128, NT, 128])
                self.tt(KDT[c].rearrange("p (t i) -> p t i", t=NT), KT[:, c, :].rearrange("p (t i) -> p t i", t=NT), ebl, ALU.mult,
                        ["KT%d" % c, "EB%d" % c], ["ENB%d" % c])
            psT = self.bankb(6, 2)
            for t in range(NT):
                for c in range(2):
                    self.tp(psT[:, (t * 2 + c) * 128:(t * 2 + c + 1) * 128], KDT[c][:, t * 128:(t + 1) * 128], self.IDb, ["ENB%d" % c, "IDb"], 6 + t // 4)
            self.copy("act", KD.rearrange("p t d -> p (t d)"), psT, [], ["ps6", "ps7", "KD"])
            if mode == "B":
                wq, kq = self.wload(w_in[:, h * 256:(h + 1) * 256].rearrange("(j p) f -> p j f", p=128), 16, 256, False)
                for c in range(2):
                    for half in range(2):
                        b = nproj % 2
                        nproj += 1
                        for k in range(16):
                            self.mm(self.bank(b), wq[:, k, c * 128:(c + 1) * 128], self.UT[:, k, half * 512:(half + 1) * 512], k == 0, k == 15, kq + ["UT"], b)
                        self.stt(QT[:, c, half * 512:(half + 1) * 512], self.bank(b), 1.0 / 16.0, EB[:, c, half * 512:(half + 1) * 512],
                                 ALU.mult, ALU.mult, ["EB%d" % c], ["ps%d" % b, "QT"])
            for cb in range(2):
                wv, kv = self.wload(w_in[:, 2048 + h * 512 + cb * 256:2048 + h * 512 + (cb + 1) * 256].rearrange("(j p) f -> p j f", p=128), 16, 256, False)
                for t in range(NT):
                    b = nproj % 2
                    nproj += 1
                    for k in range(16):
                        self.mm(self.bank(b, 256), self.UT[:, k, t * 128:(t + 1) * 128], wv[:, k, :], k == 0, k == 15, kv + ["UT"], b)
                    self.copy("act", V[:, t, cb * 256:(cb + 1) * 256], self.bank(b, 256), [], ["ps%d" % b, "V%d" % t])
            if mode == "A":
                self.P.op("dve", lambda e: e.memset(S.rearrange("p c e -> p (c e)"), 0.0), writes=["S0", "S1"])
            else:
                self.P.op("dve", lambda e: e.memset(S.rearrange("p c e -> p (c e)"), 0.0), writes=["S0", "S1"])
                self.dma_sp(AG[:, 0:16].rearrange("p (q k) -> p q k", k=2), at_in(h), ["CCD"], ["AG"])
                for cp in range(NCORE - 1):
                    sl, ks = self.wload_f32(sa_in(cp, h), 2, 512, rkeys=["CCD"])
                    for c in range(2):
                        ai = cp * 2 + c
                        self.ts(CF[:, c:c + 1], AG[:, ai:ai + 1], self.FL[:, cp:cp + 1], ALU.mult, ["AG", "FL"], ["CF%d" % c],
                                s2=self.FL[:, 8 + cp:9 + cp], op1=ALU.add)
                        self.ts(ST2[:, c, :], sl[:, c, :], self.FL[:, cp:cp + 1], ALU.mult, ks + ["FL"], ["ENB0"])
                        self.stt(S[:, c, :], S[:, c, :], CF[:, c:c + 1], ST2[:, c, :], ALU.mult, ALU.add,
                                 ["S%d" % c, "CF%d" % c, "ENB0"], ["S%d" % c])
                for c in range(2):
                    self.copy("act", SB[:, c, :], S[:, c, :], ["S%d" % c], ["SB%d" % c])
            for t in range(NT):
                tk = slice(t * 128, (t + 1) * 128)
                if mode == "B":
                    for c in range(2):
                        self.mm(self.bank(2, 128), KT[:, c, tk], QT[:, c, tk], c == 0, c == 1, ["KT%d" % c, "QT"], 2)
                    self.tt(ATT, self.bank(2, 128), self.MCb, ALU.mult, ["MCb"], ["ps2", "ATT"])
                    self.mm(self.bank(3), ATT, V[:, t, :], True, False, ["ATT", "V%d" % t], 3)
                    for c in range(2):
                        self.mm(self.bank(3), QT[:, c, tk], SB[:, c, :], False, c == 1, ["QT", "SB%d" % c], 3)
                    self.act(self.U[:, 0:512], self.bank(3), AF.Square, [], ["ps3", "U", "OSS%d" % t], accum_out=OSS[:, t:t + 1])
                    self.act(OTM[:, t:t + 1], OSS[:, t:t + 1], AF.Ln, ["OSS%d" % t], ["OTM%d" % t], scale=1.0 / 512, bias=EPS)
                    self.act(ORS[:, t:t + 1], OTM[:, t:t + 1], AF.Exp, ["OTM%d" % t], ["ORS%d" % t], scale=-0.5)
                    self.stt(ON[:, t, :], self.bank(3), ORS[:, t:t + 1], GOv, ALU.mult, ALU.mult, ["ORS%d" % t, "GO"], ["ps3", "L"])
                if mode == "A" or t < NT - 1:
                    for c in range(2):
                        self.mm(self.bank(4 + c), KD[:, t, c * 128:(c + 1) * 128], V[:, t, :], True, True, ["KD", "V%d" % t], 4 + c)
                        self.stt(S[:, c, :], S[:, c, :], EB[:, c, t * 128 + 127:t * 128 + 128], self.bank(4 + c), ALU.mult, ALU.add,
                                 ["S%d" % c, "EB%d" % c], ["ps%d" % (4 + c), "S%d" % c])
                        if mode == "B":
                            self.copy("act", SB[:, c, :], S[:, c, :], ["S%d" % c], ["SB%d" % c])
            if mode == "A":
                self.dma_sp(sa_out(h), S, ["S0", "S1"], ["SAO%d" % h])
                self.dma_sp(at_out(h), AT, ["AT"], ["ATO%d" % h])
                if h == 3:
                    self.defer(gather, 3)
                P.barrier(False)
                continue
            for cb in range(2):
                wr, kr = self.wload(w_in[:, 4096 + h * 512 + cb * 256:4096 + h * 512 + (cb + 1) * 256].rearrange("(j p) f -> p j f", p=128), 16, 256, False)
                for t in range(NT):
                    b = nproj % 2
                    nproj += 1
                    for k in range(16):
                        self.mm(self.bank(b, 256), self.UT[:, k, t * 128:(t + 1) * 128], wr[:, k, :], k == 0, k == 15, kr + ["UT"], b)
                    self.act(RSC[b], self.bank(b, 256), AF.Silu, [], ["ps%d" % b, "RSC%d" % b])
                    ov = ON[:, t, cb * 256:(cb + 1) * 256]
                    self.tt(ov, ov, RSC[b], ALU.mult, ["L", "RSC%d" % b], ["L"])
            psT = self.bankb(6, 2)
            for t in range(NT):
                for ec in range(4):
                    self.tp(psT[:, (t % 4 * 4 + ec) * 128:(t % 4 * 4 + ec + 1) * 128], ON[:, t, ec * 128:(ec + 1) * 128], self.IDb, ["L", "IDb"], 6 + (t % 4) // 2)
                if t % 4 == 3:
                    t0 = t - 3
                    self.copy("act", OGT[:, :, t0 * 128:(t0 + 4) * 128].rearrange("p c (t i) -> p t c i", t=4),
                              psT.rearrange("p (t c i) -> p t c i", t=4, c=4), [], ["ps6", "ps7", "EB0", "EB1"])
            nw = 0
            for cbo in range(2):
                wo, ko = self.wload(w_o[h * 512:(h + 1) * 512, cbo * 1024:(cbo + 1) * 1024].rearrange("(j p) f -> p j f", p=128), 4, 1024, False)
                for t in range(NT):
                    for c2 in range(2):
                        b = 2 + nw % 4
                        nw += 1
                        for ec in range(4):
                            self.mm(self.bank(b), OGT[:, ec, t * 128:(t + 1) * 128], wo[:, ec, c2 * 512:(c2 + 1) * 512], ec == 0, ec == 3,
                                    ko + ["EB0", "EB1"], b)
                        hv = self.H[:, t, cbo * 1024 + c2 * 512:cbo * 1024 + (c2 + 1) * 512]
                        self.tt(hv, self.bank(b), hv, ALU.add, ["H%d" % t], ["ps%d" % b, "H%d" % t])
            P.barrier(False)

    def out_proj(self, ON, OGT, w_o, row0, kon, kogt):
        psT = self.bankb(6, 2)
        for t in range(NT):
            for ec in range(4):
                self.tp(psT[:, (t % 4 * 4 + ec) * 128:(t % 4 * 4 + ec + 1) * 128], ON[:, t, ec * 128:(ec + 1) * 128], self.IDb,
                        kon + ["IDb"], 6 + (t % 4) // 2)
            if t % 4 == 3:
                t0 = t - 3
                self.copy("act", OGT[:, :, t0 * 128:(t0 + 4) * 128].rearrange("p c (t i) -> p t c i", t=4),
                          psT.rearrange("p (t c i) -> p t c i", t=4, c=4), [], ["ps6", "ps7"] + kogt)
        nw = 0
        for cbo in range(2):
            wo, ko = self.wload(w_o[row0:row0 + 512, cbo * 1024:(cbo + 1) * 1024].rearrange("(j p) f -> p j f", p=128), 4, 1024, False)
            for t in range(NT):
                for c2 in range(2):
                    b = 2 + nw % 4
                    nw += 1
                    for ec in range(4):
                        self.mm(self.bank(b), OGT[:, ec, t * 128:(t + 1) * 128], wo[:, ec, c2 * 512:(c2 + 1) * 512], ec == 0, ec == 3,
                                ko + kogt, b)
                    hv = self.H[:, t, cbo * 1024 + c2 * 512:cbo * 1024 + (c2 + 1) * 512]
                    self.tt(hv, self.bank(b), hv, ALU.add, ["H%d" % t], ["ps%d" % b, "H%d" % t])

    def headnorm(self, b, QF, SQ, ST, QN, GB, tag):
        self.copy("act", QF, self.bank(b, 256), [], ["ps%d" % b, tag + "QF"])
        self.tt(SQ, QF, QF, ALU.mult, [tag + "QF"], [tag + "SQ"])
        self.P.op("dve", lambda e: e.tensor_reduce(out=ST[:, 0:4], in_=SQ.rearrange("p (h d) -> p h d", h=4), axis=AX.X, op=ALU.add),
                  reads=[tag + "SQ"], writes=[tag + "SS"])
        self.act(ST[:, 4:8], ST[:, 0:4], AF.Ln, [tag + "SS"], [tag + "TM"], scale=1.0 / 64, bias=EPS)
        self.act(ST[:, 8:12], ST[:, 4:8], AF.Exp, [tag + "TM"], [tag + "RS"], scale=-0.5)
        q3 = QF.rearrange("p (h d) -> p h d", h=4)
        self.tt(q3, q3, ST[:, 8:12].unsqueeze(2).broadcast_to([128, 4, 64]), ALU.mult, [tag + "QF", tag + "RS"], [tag + "QF"])
        self.tt(QN.rearrange("p (h d) -> p h d", h=4), q3, GB.unsqueeze(1).broadcast_to([128, 4, 64]), ALU.mult,
                [tag + "QF", tag + "GB"], [tag + "QN"])

    def kv_compute(self, w_k, w_v, g_k, kt_out, va_out, cc_src=None):
        P = self.P
        X = O_X
        KTs = self.words(X, 2048).bitcast(BF16).rearrange("p (h t) -> p h t", h=4)
        VA = self.words(X + 2048, 1040).bitcast(BF16).rearrange("p (t h e) -> p t h e", t=NT, h=4)
        QF = self.words(X + 3200, 256)
        SQ = self.words(X + 3456, 256)
        QN = self.words(X + 3712, 128).bitcast(BF16)
        ST = self.words(X + 3840, 16)
        GK = self.words(X + 3872, 64)
        self.dma_sp(GK, g_k.partition_broadcast(128), [], ["kGB"])
        self.P.op("dve", lambda e: e.memset(VA[:, :, :, 64:65], 1.0), writes=["VA"])
        wk, kk = self.wload(w_k.rearrange("(j p) f -> p j f", p=128), 16, 256, False)
        wv, kv = self.wload(w_v.rearrange("(j p) f -> p j f", p=128), 16, 256, False)
        psT = self.bankb(6, 1)
        for t in range(NT):
            b = t % 2
            for k in range(16):
                self.mm(self.bank(b, 256), self.UT[:, k, t * 128:(t + 1) * 128], wk[:, k, :], k == 0, k == 15, kk + ["UT"], b)
            self.headnorm(b, QF, SQ, ST, QN, GK, "k")
            for kh in range(4):
                self.tp(psT[0:64, kh * 128:(kh + 1) * 128], QN[:, kh * 64:(kh + 1) * 64], self.IDb, ["kQN", "IDb"], 6)
            self.copy("act", KTs[0:64, :, t * 128:(t + 1) * 128], psT[0:64, 0:512].rearrange("p (h i) -> p h i", h=4), [], ["ps6", "KTs"])
            b2 = 2 + t % 2
            for k in range(16):
                self.mm(self.bank(b2, 256), self.UT[:, k, t * 128:(t + 1) * 128], wv[:, k, :], k == 0, k == 15, kv + ["UT"], b2)
            self.copy("act", VA[:, t, :, 0:64], self.bank(b2, 256).rearrange("p (h d) -> p h d", h=4), [], ["ps%d" % b2, "VA"])
        self.dma_sp(kt_out, KTs[0:64], ["KTs"], ["KTO"])
        self.dma_sp(va_out, VA, ["VA"], ["VAO"])
        if cc_src is not None:
            wk_ = self.words(X, 2048)[0:64, :].rearrange("p (h w) -> p h w", h=4)[:, :, 448:512]
            self.dma_sp(cc_src[0:64, :].rearrange("p (h w) -> p h w", h=4), wk_, ["KTs"], ["CCS2"])
            self.dma_sp(cc_src[64:192, 0:130], self.words(X + 2048, 1040)[:, 910:1040], ["VA"], ["CCS2"])
        P.barrier(False)

    def swa(self, w_q, g_q, sinks, w_o, kt_in, kth_in, va_in, vah_in, cc_dst=None):
        P = self.P
        X = O_X
        WX = O_WX
        KTs = self.words(X, 2304).bitcast(BF16).rearrange("p (h t) -> p h t", h=4)
        VA = self.words(X + 2304, 1170).bitcast(BF16).rearrange("p (t h e) -> p t h e", t=9, h=4)
        QTg = self.words(X + 3488, 4096).bitcast(BF16).rearrange("p (h t) -> p h t", h=8)
        GQ = self.words(X + 7584, 64)
        ES = self.words(X + 7648, 32)
        ST = self.words(X + 7680, 16)
        DEN = self.words(X + 7696, 8)
        REC = self.words(X + 7704, 8)
        OA = self.words(WX, 2048).bitcast(BF16).rearrange("p (t e) -> p t e", t=NT)
        OAT = self.words(WX + 2048, 2048).bitcast(BF16).rearrange("p (c t) -> p c t", c=4)
        PT = [self.words(WX + 4096 + i * 1024, 1024).bitcast(BF16).rearrange("p (k n) -> p k n", k=2) for i in range(2)]
        QF = [self.words(WX + 6144 + i * 256, 256) for i in range(2)]
        SQ = self.words(WX + 6656, 256)
        QN = self.words(WX + 6912, 128).bitcast(BF16)
        self.dma_sp(KTs[0:64, :, 128:1152], kt_in, [], ["KTs"])
        self.dma_sp(VA[:, 1:9], va_in, [], ["VA"])
        if cc_dst is None:
            self.dma_sp(KTs[0:64, :, 0:128], kth_in, [], ["KTs"])
            self.dma_sp(VA[:, 0], vah_in, [], ["VA"])
        else:
            g3 = cc_dst.rearrange("(c r) w -> r c w", r=192)
            KTg = self.words(WX, 2048)[0:64, :].rearrange("p (c w) -> p c w", c=8)
            VAg = self.words(WX + 2048, 1040).rearrange("p (c w) -> p c w", c=8)
            self.dma_sp(KTg, g3[0:64], ["CCD2"], ["KTg"])
            self.dma_sp(VAg, g3[64:192, :, 0:130], ["CCD2"], ["VAg"])
            for cp in range(NCORE):
                kin = KTg[:, cp, :].bitcast(BF16).rearrange("p (h i) -> p h i", h=4)
                vin = VAg[:, cp, :].bitcast(BF16).rearrange("p (h e) -> p h e", h=4)
                f = self.FL[:, 16 + cp:17 + cp]
                if cp == 0:
                    self.ts(KTs[0:64, :, 0:128], kin, f[0:64], ALU.mult, ["KTg", "FL"], ["KTs"])
                    self.ts(VA[:, 0], vin, f, ALU.mult, ["VAg", "FL"], ["VA"])
                else:
                    self.stt(KTs[0:64, :, 0:128], kin, f[0:64], KTs[0:64, :, 0:128], ALU.mult, ALU.add, ["KTg", "FL", "KTs"], ["KTs"])
                    self.stt(VA[:, 0], vin, f, VA[:, 0], ALU.mult, ALU.add, ["VAg", "FL", "VA"], ["VA"])
            P.barrier(False)
        self.dma_sp(GQ, g_q.partition_broadcast(128), [], ["GQraw"])
        self.act(GQ, GQ, AF.Copy, ["GQraw"], ["qGB"], scale=0.125)
        self.dma_sp(ES, sinks.partition_broadcast(128), [], ["ESraw"])
        self.act(ES, ES, AF.Exp, ["ESraw"], ["ES"])
        psT = self.bankb(6, 1)
        QF2 = [self.words(WX + 6144 + i * 512, 512) for i in range(2)]
        SQ2 = self.words(WX + 7168, 512)
        QN2 = [self.words(WX + 7680 + i * 256, 256).bitcast(BF16) for i in range(2)]
        ST2 = [self.words(X + 7712 + i * 32, 32) for i in range(2)]
        ns = 0
        for g in range(4):
            wqs = []
            for blk in range(2):
                wqs.append(self.wload(w_q[:, g * 512 + blk * 256:g * 512 + (blk + 1) * 256].rearrange("(j p) f -> p j f", p=128), 16, 256, False))

            def qA(t, g=g, wqs=wqs):
                b = t % 2
                tg = "q%d" % b
                for blk in range(2):
                    wq, kq = wqs[blk]
                    for k in range(16):
                        self.mm(self.ps[:, b, blk * 256:(blk + 1) * 256], self.UT[:, k, t * 128:(t + 1) * 128], wq[:, k, :], k == 0, k == 15, kq + ["UT"], b)
                qf, st = QF2[b], ST2[b]
                self.copy("act", qf, self.bank(b), [], ["ps%d" % b, tg + "QF"])
                self.tt(SQ2, qf, qf, ALU.mult, [tg + "QF"], ["qSQ"])
                self.P.op("dve", lambda e, st=st: e.tensor_reduce(out=st[:, 0:8], in_=SQ2.rearrange("p (h d) -> p h d", h=8), axis=AX.X, op=ALU.add),
                          reads=["qSQ"], writes=[tg + "SS"])
                self.act(st[:, 8:16], st[:, 0:8], AF.Ln, [tg + "SS"], [tg + "TM"], scale=1.0 / 64, bias=EPS)
                self.act(st[:, 16:24], st[:, 8:16], AF.Exp, [tg + "TM"], [tg + "RS"], scale=-0.5)

            def qB(t):
                b = t % 2
                tg = "q%d" % b
                qf, st, qn = QF2[b], ST2[b], QN2[b]
                q3 = qf.rearrange("p (h d) -> p h d", h=8)
                self.tt(q3, q3, st[:, 16:24].unsqueeze(2).broadcast_to([128, 8, 64]), ALU.mult, [tg + "QF", tg + "RS"], [tg + "QF"])
                self.tt(qn.rearrange("p (h d) -> p h d", h=8), q3, GQ.unsqueeze(1).broadcast_to([128, 8, 64]), ALU.mult,
                        [tg + "QF", "qGB"], [tg + "QN"])
                for hd in range(8):
                    self.tp(psT[0:64, hd * 128:(hd + 1) * 128], qn[:, hd * 64:(hd + 1) * 64], self.IDb, [tg + "QN", "IDb"], 6)
                self.copy("act", QTg[0:64, :, t * 128:(t + 1) * 128], psT[0:64, :].rearrange("p (h i) -> p h i", h=8), [], ["ps6", "QTg"])

            for t in range(NT + 1):
                if t < NT:
                    qA(t)
                if t >= 1:
                    qB(t - 1)

            def sS(t, g=g):
                nonlocal ns
                pt = PT[t % 2]
                pk = "PT%d" % (t % 2)
                for kb in range(2):
                    kt = t + kb
                    for hh in range(2):
                        b = 2 + ns % 2
                        ns += 1
                        self.mm(self.bank(b), KTs[0:64, g, kt * 128:(kt + 1) * 128], QTg[0:64, hh * 4:(hh + 1) * 4, t * 128:(t + 1) * 128],
                                True, True, ["KTs", "QTg"], b)
                        self.act(pt[:, kb, hh * 512:(hh + 1) * 512], self.bank(b), AF.Exp, [], ["ps%d" % b, pk])
                    msk = self.MPb if kb == 0 else self.MCb
                    pv = pt[:, kb, :].rearrange("p (h i) -> p h i", h=8)
                    self.tt(pv, pv, msk.unsqueeze(1).broadcast_to([128, 8, 128]), ALU.mult, [pk, "MPb", "MCb"], [pk])

            def sPV(t, g=g):
                pt = PT[t % 2]
                pk = "PT%d" % (t % 2)
                b0 = 4 if t % 2 == 0 else 0
                for hd in range(8):
                    bo = b0 + hd // 4
                    o = self.ps[:, bo, (hd % 4) * 65:(hd % 4) * 65 + 65]
                    for kb in range(2):
                        self.mm(o, pt[:, kb, hd * 128:(hd + 1) * 128], VA[:, t + kb, g, :], kb == 0, kb == 1, [pk, "VA"], bo)
                for bb in range(2):
                    pso = self.ps[:, b0 + bb, 0:260].rearrange("p (h e) -> p h e", h=4)
                    dn = DEN[:, bb * 4:(bb + 1) * 4]
                    self.tt(dn.unsqueeze(2), pso[:, :, 64:65], ES[:, g * 8 + bb * 4:g * 8 + bb * 4 + 4].unsqueeze(2), ALU.add, ["ES"],
                            ["ps%d" % (b0 + bb), "DEN%d" % bb])
                    self.P.op("dve", lambda e, bb=bb, dn=dn: e.reciprocal(out=REC[:, bb * 4:(bb + 1) * 4], in_=dn),
                              reads=["DEN%d" % bb], writes=["REC%d" % bb])
                    self.tt(OA[:, t, bb * 256:(bb + 1) * 256].rearrange("p (h d) -> p h d", h=4), pso[:, :, 0:64],
                            REC[:, bb * 4:(bb + 1) * 4].unsqueeze(2).broadcast_to([128, 4, 64]), ALU.mult, ["REC%d" % bb],
                            ["ps%d" % (b0 + bb), "OA"])

            for t in range(NT + 1):
                if t < NT:
                    sS(t)
                if t >= 1:
                    sPV(t - 1)
            self.out_proj(OA, OAT, w_o, g * 512, ["OA"], ["OAT"])
        P.barrier(False)


def _consts():
    c = np.zeros((128, 512), np.float32)
    j = np.arange(128)[:, None]
    i = np.arange(128)[None, :]
    c[:, 0:128] = np.eye(128, dtype=np.float32)
    c[:, 128:256] = (j <= i)
    c[:, 256:384] = (j <= i) * (-1.0 / 16.0)
    c[:, 384:512] = (j > i)
    return c


def _flags(core):
    f = np.zeros((128, 24), np.float32)
    for cp in range(8):
        m = 1.0 if cp < core else 0.0
        f[:, cp] = m
        f[:, 8 + cp] = 1.0 - m
        f[:, 16 + cp] = 1.0 if cp == core - 1 else 0.0
    return f


def _run(build, in_maps):
    nc = bass.Bass("TRN2", target_bir_lowering=False)
    with contextlib.ExitStack() as st:
        core = Core(nc, st)
        core.setup()
        build(core)
        core.finish()
    for c in range(NCORE):
        in_maps[c]["cst"] = _consts()
        in_maps[c]["fl"] = _flags(c)
    res = run_bass_kernel_spmd(nc, in_maps, core_ids=list(range(NCORE)))
    return res.results


def build_fused(k):
    x = k.din("x", [TOK, D])
    norm_mix = k.din("norm_mix", [4, D])
    norm_mlp = k.din("norm_mlp", [4, D])
    mlp_w1 = k.din("mlp_w1", [4, D, 4 * D])
    mlp_w2 = k.din("mlp_w2", [4, 4 * D, D])
    a_w_in = k.din("a_w_in", [2, D, 6160])
    a_w_g2 = k.din("a_w_g2", [2, 16, 1024])
    a_b_g = k.din("a_b_g", [2, 1024])
    a_g_o = k.din("a_g_o", [2, 512])
    a_w_o = k.din("a_w_o", [2, D, D])
    kv_norm = k.din("kv_norm", [D])
    kv_w_k = k.din("kv_w_k", [D, 256])
    kv_w_v = k.din("kv_w_v", [D, 256])
    kv_g_k = k.din("kv_g_k", [64])
    b_w_q = k.din("b_w_q", [2, D, D])
    b_g_q = k.din("b_g_q", [2, 64])
    b_sinks = k.din("b_sinks", [2, 32])
    b_w_o = k.din("b_w_o", [2, D, D])
    out = k.dout("out", [TOK, D])
    k.load_h(x)
    for i in range(2):
        cs = k.dscratch("cc_src%d" % i, [512, 1026])
        cd = k.dscratch("cc_dst%d" % i, [NCORE * 512, 1026])

        def gather(i=i, cs=cs, cd=cd):
            k.allgather(cs, cd, ["SAO0", "SAO1", "SAO2", "SAO3", "ATO0", "ATO1", "ATO2", "ATO3"], ["CCD"], "cc%d" % i)

        k.norm(norm_mix[i])
        k.gla(a_w_in[i], a_w_g2[i], a_b_g[i], a_g_o[i], None, "A", gather=gather,
              sa_out=lambda h, cs=cs: cs[h * 128:(h + 1) * 128, 0:1024].rearrange("p (c e) -> p c e", c=2),
              at_out=lambda h, cs=cs: cs[h * 128:(h + 1) * 128, 1024:1026])
        k.gla(a_w_in[i], a_w_g2[i], a_b_g[i], a_g_o[i], a_w_o[i], "B",
              sa_in=lambda cp, h, cd=cd: cd[cp * 512 + h * 128:cp * 512 + (h + 1) * 128, 0:1024].rearrange("p (c e) -> p c e", c=2),
              at_in=lambda h, cd=cd: cd[:, 1024:1026].rearrange("(c h d) k -> h d c k", c=NCORE, h=4)[h])
        k.norm(norm_mlp[i], full_after=True)
        k.mlp(mlp_w1[i], mlp_w2[i])
    kvk = k.dscratch("kvs_k", [64, 4, 1024], BF16)
    kvv = k.dscratch("kvs_v", [128, 8, 4, 65], BF16)
    cs2 = k.dscratch("cc_src2", [192, 256])
    cd2 = k.dscratch("cc_dst2", [NCORE * 192, 256])
    k.norm(kv_norm)
    k.kv_compute(kv_w_k, kv_w_v, kv_g_k, kvk, kvv, cc_src=cs2)
    k.allgather(cs2, cd2, ["CCS2"], ["CCD2"], "cc2")
    for j in range(2):
        k.norm(norm_mix[2 + j])
        k.swa(b_w_q[j], b_g_q[j], b_sinks[j], b_w_o[j], kvk, None, kvv, None, cc_dst=cd2)
        k.norm(norm_mlp[2 + j], full_after=True)
        k.mlp(mlp_w1[2 + j], mlp_w2[2 + j])
    k.store_h(out)


def kernel(**inp):
    inp = {k: np.ascontiguousarray(np.asarray(v, dtype=np.float32)) for k, v in inp.items()}
    x = inp["x"][0].reshape(NCORE, TOK, D)
    maps = []
    for c in range(NCORE):
        m = {k: v for k, v in inp.items() if k != "x"}
        m["x"] = x[c]
        maps.append(m)
    res = _run(build_fused, maps)
    out = np.concatenate([r["out"] for r in res], 0)
    return out.reshape(1, NCORE * TOK, D).astype(np.float32)
```

```python
import contextlib
import numpy as np
import concourse.bass as bass
import concourse.mybir as mybir
from concourse.bass_utils import run_bass_kernel_spmd

F32 = mybir.dt.float32
BF16 = mybir.dt.bfloat16
AF = mybir.ActivationFunctionType
ALU = mybir.AluOpType
AX = mybir.AxisListType

ENGS = ("pe", "act", "dve", "pool", "sp")
EPOCH = 30000
NCORE = 8
D = 2048
TOK = 1024
NT = 8
EPS = 1e-6


class Prog:
    def __init__(self, nc):
        self.nc = nc
        self.ops = {e: [] for e in ENGS}
        self.cnt = {e: 0 for e in ENGS}
        self.seen = {e: {} for e in ENGS}
        self.lastw = {}
        self.readers = {}
        self.dma_cnt = {}
        self.semkeys = set()
        self.async_sems = set()

    def _deps(self, eng, reads, writes):
        deps = {}

        def need(tok, raw):
            base, ep, v = tok
            if base == eng and (eng == "pe" or not raw):
                return
            cur = deps.get(base)
            if cur is None or (ep, v) > cur:
                deps[base] = (ep, v)

        for k in reads:
            t = self.lastw.get(k)
            if t is not None:
                need(t, True)
        for k in writes:
            t = self.lastw.get(k)
            if t is not None:
                need(t, k in reads)
            for base, (ep, v) in self.readers.get(k, {}).items():
                need((base, ep, v), False)
        waits = []
        seen = self.seen[eng]
        for base, (ep, v) in deps.items():
            s = seen.get(base)
            if s is None or s < (ep, v):
                seen[base] = (ep, v)
                waits.append((base, ep, v))
        return waits

    def _record(self, tok, reads, writes):
        base, ep, v = tok
        for k in reads:
            if k in writes:
                continue
            self.readers.setdefault(k, {})[base] = (ep, v)
        for k in writes:
            self.lastw[k] = tok
            self.readers[k] = {}

    def op(self, eng, fn, reads=(), writes=()):
        reads = tuple(reads)
        writes = tuple(writes)
        waits = self._deps(eng, reads, writes)
        self.cnt[eng] += 1
        seq = self.cnt[eng]
        sk = (eng, (seq - 1) // EPOCH)
        self.semkeys.add(sk)
        self.ops[eng].append((fn, waits, (sk, 1)))
        self._record((eng, (seq - 1) // EPOCH, (seq - 1) % EPOCH + 1), reads, writes)

    def dma(self, eng, fn, sem, reads=(), writes=(), inc=16):
        reads = tuple(reads)
        writes = tuple(writes)
        waits = self._deps(eng, reads, writes)
        base = "d:" + sem
        self.dma_cnt[base] = self.dma_cnt.get(base, 0) + inc
        v = self.dma_cnt[base]
        self.semkeys.add((base, 0))
        self.ops[eng].append((fn, waits, ((base, 0), inc)))
        self._record((base, 0, v), reads, writes)

    STICKY = ("W", "SAO", "ATO", "CCS", "CCD")

    def pool_sync(self):
        waits = []
        for e in ENGS:
            if e != "pool" and self.cnt[e] > 0:
                tok = (e, (self.cnt[e] - 1) // EPOCH, (self.cnt[e] - 1) % EPOCH + 1)
                s_ = self.seen["pool"].get(e)
                if s_ is None or s_ < tok[1:]:
                    self.seen["pool"][e] = tok[1:]
                    waits.append(tok)
        for base, v in self.dma_cnt.items():
            if base in self.async_sems:
                continue
            s_ = self.seen["pool"].get(base)
            if s_ is None or s_ < (0, v):
                self.seen["pool"][base] = (0, v)
                waits.append((base, 0, v))
        if waits:
            self.ops["pool"].append((None, waits, None))

    def barrier(self, full=True):
        toks = []
        for e in ENGS:
            if self.cnt[e] > 0:
                toks.append((e, (self.cnt[e] - 1) // EPOCH, (self.cnt[e] - 1) % EPOCH + 1))
        for base, v in self.dma_cnt.items():
            if not full and base in self.async_sems:
                continue
            toks.append((base, 0, v))
        for e in ENGS:
            if not full and e == "pool":
                continue
            waits = []
            for base, ep, v in toks:
                if base == e:
                    continue
                s = self.seen[e].get(base)
                if s is None or s < (ep, v):
                    self.seen[e][base] = (ep, v)
                    waits.append((base, ep, v))
            if waits:
                self.ops[e].append((None, waits, None))
        if full:
            self.lastw = {}
            self.readers = {}
        else:
            self.lastw = {k: v for k, v in self.lastw.items() if k.startswith(self.STICKY)}
            self.readers = {k: v for k, v in self.readers.items() if k.startswith(self.STICKY)}

    def emit(self):
        nc = self.nc
        with contextlib.ExitStack() as st:
            sems = {}
            for i, sk in enumerate(sorted(self.semkeys, key=str)):
                sems[sk] = st.enter_context(nc.semaphore("s%d" % i))
            block = st.enter_context(nc.Block())

            def run(engname):
                def body(e):
                    for fn, waits, inc in self.ops[engname]:
                        for base, ep, v in waits:
                            e.wait_ge(sems[(base, ep)], v)
                        if fn is not None:
                            ins = fn(e)
                            if inc is not None:
                                ins.then_inc(sems[inc[0]], inc[1])
                return body

            block.tensor(run("pe"))
            block.scalar(run("act"))
            block.vector(run("dve"))
            block.gpsimd(run("pool"))
            block.sync(run("sp"))


O_H = 0
O_UT = 16384
O_W = 24576
O_WX = 24576 + 8192
O_X = 40960
O_M = 49152
ARENA = 51200


class Core:
    def __init__(self, nc, st):
        self.nc = nc
        self.P = Prog(nc)
        self.arena = st.enter_context(nc.sbuf_tensor("arena", [128, ARENA], F32))
        self.ps = st.enter_context(nc.psum_tensor("ps", [128, 8, 512], F32))
        a = self.arena
        self.H = a[:, O_H:O_H + 16384].rearrange("p (t d) -> p t d", t=NT)
        self.UT = a[:, O_UT:O_UT + 8192].bitcast(BF16).rearrange("p (j t) -> p j t", j=16)
        m = O_M
        self.U = a[:, m:m + 1024].bitcast(BF16)
        m += 1024
        self.IDb = a[:, m:m + 64].bitcast(BF16)
        m += 64
        self.MCb = a[:, m:m + 64].bitcast(BF16)
        m += 64
        self.MPb = a[:, m:m + 64].bitcast(BF16)
        m += 64
        self.CST = a[:, m:m + 384]
        m += 384
        self.GT = a[:, m:m + 16]
        m += 16
        self.STAT = a[:, m:m + 64]
        m += 64
        self.FL = a[:, m:m + 24]
        m += 24
        assert m <= ARENA, m
        self.wslot = 0
        self.ndma = 0
        self.pending = []
        self.inputs = {}
        self.outputs = {}

    def din(self, name, shape, dt=F32):
        t = self.nc.dram_tensor(name, list(shape), dt, kind="ExternalInput").ap()
        self.inputs[name] = t
        return t

    def dout(self, name, shape, dt=F32):
        t = self.nc.dram_tensor(name, list(shape), dt, kind="ExternalOutput").ap()
        self.outputs[name] = t
        return t

    def dscratch(self, name, shape, dt=F32):
        return self.nc.dram_tensor(name, list(shape), dt).ap()

    def defer(self, fn, nloads):
        self.pending.append([nloads, fn])

    def _tick(self):
        for p in self.pending:
            p[0] -= 1
        while self.pending and self.pending[0][0] <= 0:
            self.pending.pop(0)[1]()

    def flush(self):
        while self.pending:
            self.pending.pop(0)[1]()

    def allgather(self, src, dst, rkeys, wkeys, sem):
        self.P.async_sems.add("d:" + sem)
        self.P.dma("pool", lambda e: e.collective_compute("AllGather", ALU.bypass, replica_groups=[list(range(NCORE))],
                                                          ins=[src.opt()], outs=[dst.opt()]),
                   sem, reads=rkeys, writes=wkeys, inc=1)

    def words(self, off, n):
        return self.arena[:, off:off + n]

    def bank(self, b, n=512):
        return self.ps[:, b, 0:n]

    def bankb(self, b0, nb=2):
        return self.ps[:, b0:b0 + nb, :].rearrange("p a b -> p (a b)").bitcast(BF16)

    def mm(self, out, lhsT, rhs, start, stop, reads, bank):
        self.P.op("pe", lambda e: e.matmul(out, lhsT=lhsT, rhs=rhs, start=start, stop=stop),
                  reads=reads, writes=["ps%d" % bank])

    def tp(self, out, in_, ident, reads, bank):
        self.P.op("pe", lambda e: e.transpose(out, in_, ident), reads=reads, writes=["ps%d" % bank])

    def act(self, out, in_, func, reads, writes, scale=1.0, bias=None, accum_out=None):
        kw = {}
        if bias is not None:
            kw["bias"] = bias
        if accum_out is not None:
            kw["accum_out"] = accum_out
        self.P.op("act", lambda e: e.activation(out=out, in_=in_, func=func, scale=scale, **kw),
                  reads=reads, writes=writes)

    def tt(self, out, in0, in1, op, reads, writes, eng="dve"):
        self.P.op(eng, lambda e: e.tensor_tensor(out=out, in0=in0, in1=in1, op=op), reads=reads, writes=writes)

    def ts(self, out, in0, s1, op0, reads, writes, s2=None, op1=None, eng="dve"):
        if op1 is None:
            self.P.op(eng, lambda e: e.tensor_scalar(out=out, in0=in0, scalar1=s1, scalar2=None, op0=op0),
                      reads=reads, writes=writes)
        else:
            self.P.op(eng, lambda e: e.tensor_scalar(out=out, in0=in0, scalar1=s1, scalar2=s2, op0=op0, op1=op1),
                      reads=reads, writes=writes)

    def stt(self, out, in0, scalar, in1, op0, op1, reads, writes, eng="dve"):
        self.P.op(eng, lambda e: e.scalar_tensor_tensor(out=out, in0=in0, scalar=scalar, in1=in1, op0=op0, op1=op1),
                  reads=reads, writes=writes)

    def copy(self, eng, out, in_, reads, writes):
        if eng == "act":
            self.act(out, in_, AF.Copy, reads, writes)
        else:
            self.P.op(eng, lambda e: e.tensor_copy(out=out, in_=in_), reads=reads, writes=writes)

    def dma_sp(self, out, in_, reads, writes):
        self.ndma += 1
        self.P.dma("sp", lambda e: e.dma_start(out=out, in_=in_), "sp%d" % (self.ndma % 8), reads=reads, writes=writes)

    def wload(self, src, j, f, big):
        s = self.wslot % 4
        self.wslot += 1
        if big:
            off = O_W + s * 4096
            n = 4096
            keys = ["W%d" % (2 * s), "W%d" % (2 * s + 1)]
        else:
            off = O_W + s * 2048
            n = 2048
            keys = ["W%d" % s]
        assert j * f <= 2 * n
        view = self.arena[:, off:off + (j * f) // 2].bitcast(BF16).rearrange("p (j f) -> p j f", j=j)
        self.P.dma("pool", lambda e: e.dma_start(out=view, in_=src), "w%d%s" % (s, "b" if big else "s"), writes=keys)
        self._tick()
        return view, keys

    def wload_f32(self, src, j, f, rkeys=()):
        s = self.wslot % 4
        self.wslot += 1
        off = O_W + s * 2048
        keys = ["W%d" % s]
        view = self.arena[:, off:off + j * f].rearrange("p (j f) -> p j f", j=j)
        self.P.dma("sp", lambda e: e.dma_start(out=view, in_=src), "wf%d" % s, reads=list(rkeys), writes=keys)
        return view, keys

    def setup(self):
        cst = self.din("cst", [128, 512])
        fl = self.din("fl", [128, 24])
        self.dma_sp(self.CST, cst[:, 0:384], [], ["CST"])
        self.dma_sp(self.FL, fl, [], ["FL"])
        tmp = self.words(O_X, 128)
        self.dma_sp(tmp, cst[:, 384:512], [], ["TMPC"])
        self.copy("dve", self.IDb, self.CST[:, 0:128], ["CST"], ["IDb"])
        self.copy("dve", self.MCb, self.CST[:, 128:256], ["CST"], ["MCb"])
        self.copy("dve", self.MPb, tmp, ["TMPC"], ["MPb"])
        self.ID32 = self.CST[:, 0:128]
        self.TRI = self.CST[:, 256:384]
        self.P.barrier()

    def load_h(self, src):
        v = src.rearrange("(t p) d -> p t d", p=128)
        for i in range(4):
            self.dma_sp(self.H[:, 2 * i:2 * i + 2, :], v[:, 2 * i:2 * i + 2, :], [], ["H%d" % (2 * i), "H%d" % (2 * i + 1)])

    def store_h(self, dst):
        v = dst.rearrange("(t p) d -> p t d", p=128)
        for i in range(4):
            self.dma_sp(v[:, 2 * i:2 * i + 2, :], self.H[:, 2 * i:2 * i + 2, :], ["H%d" % (2 * i), "H%d" % (2 * i + 1)], ["OUT"])

    def finish(self):
        self.flush()
        self.P.barrier()
        self.P.emit()

    def norm(self, g, full_before=False, full_after=False):
        P = self.P
        P.barrier(full_before)
        g16 = self.words(O_X, 128)[0:16, :]
        self.dma_sp(g16, g.rearrange("(j p) -> j p", p=128), [], ["G16"])
        self.P.op("pe", lambda e: e.transpose(self.ps[:, 0, 0:16], g16, self.ID32[0:16, 0:16]), reads=["G16", "CST"], writes=["ps0"])
        self.copy("dve", self.GT, self.ps[:, 0, 0:16], [], ["ps0", "GT"])
        SS = self.STAT[:, 0:8]
        TM = self.STAT[:, 8:16]
        RS = self.STAT[:, 16:24]
        UU = [self.words(O_X + 256 + i * 1024, 1024).bitcast(BF16) for i in range(2)]
        psTs = [self.bankb(4, 2), self.bankb(6, 2)]

        def nA(t):
            hk = "H%d" % t
            u = UU[t % 2]
            uk = "UU%d" % (t % 2)
            self.act(self.U, self.H[:, t, :], AF.Square, [hk], ["U", "SS%d" % t], accum_out=SS[:, t:t + 1])
            self.act(TM[:, t:t + 1], SS[:, t:t + 1], AF.Ln, ["SS%d" % t], ["TM%d" % t], scale=1.0 / D, bias=EPS)
            self.act(RS[:, t:t + 1], TM[:, t:t + 1], AF.Exp, ["TM%d" % t], ["RS%d" % t], scale=-0.5)
            self.ts(u, self.H[:, t, :], RS[:, t:t + 1], ALU.mult, [hk, "RS%d" % t], [uk])

        def nB(t):
            u = UU[t % 2]
            uk = "UU%d" % (t % 2)
            psT = psTs[t % 2]
            b0 = 4 + 2 * (t % 2)
            for j in range(16):
                self.tp(psT[:, j * 128:(j + 1) * 128], u[:, j * 128:(j + 1) * 128], self.IDb, [uk, "IDb"], b0 + j // 8)
            self.tt(self.UT[:, :, t * 128:(t + 1) * 128], psT.rearrange("p (j c) -> p j c", j=16),
                    self.GT.unsqueeze(2).broadcast_to([128, 16, 128]), ALU.mult, ["GT"], ["ps%d" % b0, "ps%d" % (b0 + 1), "UT"])

        for t in range(NT + 1):
            if t < NT:
                nA(t)
            if t >= 1:
                nB(t - 1)
        if full_after:
            self.flush()
        P.barrier(full_after)

    def mlp(self, w1, w2):
        P = self.P
        HT = [self.words(O_X + i * 2048, 2048).bitcast(BF16).rearrange("p (c t) -> p c t", c=4) for i in range(2)]
        R = [self.words(O_X + 4096 + i * 512, 512) for i in range(2)]
        n1 = 0
        n2 = 0
        self.wslot = (self.wslot + 3) // 4 * 4
        for fb in range(16):
            w1b, k1 = self.wload(w1[:, fb * 512:(fb + 1) * 512].rearrange("(j p) f -> p j f", p=128), 16, 512, True)
            w2b, k2 = self.wload(w2[fb * 512:(fb + 1) * 512, :].rearrange("(j p) f -> p j f", p=128), 4, 2048, True)
            if fb == 0:
                self.P.pool_sync()
            ht = HT[fb % 2]
            hk = "HT%d" % (fb % 2)
            for fc in range(4):
                for half in range(2):
                    b = n1 % 2
                    r = n1 % 2
                    n1 += 1
                    for k in range(16):
                        self.mm(self.bank(b), w1b[:, k, fc * 128:(fc + 1) * 128], self.UT[:, k, half * 512:(half + 1) * 512],
                                k == 0, k == 15, k1 + ["UT"], b)
                    self.act(R[r], self.bank(b), AF.Relu, [], ["ps%d" % b, "R%d" % r])
                    self.tt(ht[:, fc, half * 512:(half + 1) * 512], R[r], R[r], ALU.mult, ["R%d" % r], [hk])
            for t in range(NT):
                for cb in range(4):
                    b = 2 + n2 % 4
                    n2 += 1
                    for fc in range(4):
                        self.mm(self.bank(b), ht[:, fc, t * 128:(t + 1) * 128], w2b[:, fc, cb * 512:(cb + 1) * 512],
                                fc == 0, fc == 3, k2 + [hk], b)
                    hv = self.H[:, t, cb * 512:(cb + 1) * 512]
                    self.tt(hv, self.bank(b), hv, ALU.add, ["H%d" % t], ["ps%d" % b, "H%d" % t])
        self.flush()
        P.barrier(False)

    def gla(self, w_in, w_g2, b_g, g_o, w_o, mode, sa_out=None, at_out=None, sa_in=None, at_in=None, gather=None, scr=None):
        P = self.P
        X = O_X
        QT = self.words(X, 1024).bitcast(BF16).rearrange("p (c t) -> p c t", c=2)
        KT = self.words(X + 1024, 1024).bitcast(BF16).rearrange("p (c t) -> p c t", c=2)
        KD = self.words(X + 2048, 1024).bitcast(BF16).rearrange("p (t d) -> p t d", t=NT)
        V = self.words(X + 3072, 2048).bitcast(BF16).rearrange("p (t e) -> p t e", t=NT)
        S = self.words(X + 5120, 1024).rearrange("p (c e) -> p c e", c=2)
        SB = self.words(X + 6144, 512).bitcast(BF16).rearrange("p (c e) -> p c e", c=2)
        RSC = [self.words(X + 6656 + i * 256, 256) for i in range(2)]
        ATT = self.words(X + 7168, 64).bitcast(BF16)
        AT = self.words(X + 7232, 2)
        OSS = self.words(X + 7240, 8)
        OTM = self.words(X + 7248, 8)
        ORS = self.words(X + 7256, 8)
        CF = self.words(X + 7264, 2)
        GOv = self.words(X + 7296, 512)
        AG = self.words(X + 7808, 64)
        WX = O_WX
        EB = self.words(WX, 2048).rearrange("p (c t) -> p c t", c=2)
        ENB = self.words(WX + 2048, 2048).rearrange("p (c t) -> p c t", c=2)
        KDT = [self.words(WX + 2048 + c * 1024, 512).bitcast(BF16) for c in range(2)]
        L = self.words(WX + 4096, 2048).rearrange("p (t d) -> p t d", t=NT)
        ON = self.words(WX + 4096, 2048).bitcast(BF16).rearrange("p (t e) -> p t e", t=NT)
        GLR = self.words(WX + 6144, 1024)
        WG2 = self.words(WX + 7168, 1024)
        OGT = self.words(WX + 2048, 2048).bitcast(BF16).rearrange("p (c t) -> p c t", c=4)
        ST2 = self.words(WX + 2048, 1024).rearrange("p (c e) -> p c e", c=2)

        self.dma_sp(GOv, g_o.partition_broadcast(128), [], ["GO"])
        if mode == "A":
            self.P.op("dve", lambda e: e.memset(GLR[0:32, :], 1.0), writes=["GLR"])
            self.dma_sp(WG2[0:16, :], w_g2, [], ["WG2"])
            self.dma_sp(WG2[16:17, :], b_g.rearrange("(o f) -> o f", o=1), [], ["WG2"])
            wg, kg = self.wload(w_in[:, 6144:6160].rearrange("(j p) f -> p j f", p=128), 16, 16, False)
            for half in range(2):
                for k in range(16):
                    self.mm(self.ps[0:16, half, :], wg[:, k, :], self.UT[:, k, half * 512:(half + 1) * 512], k == 0, k == 15, kg + ["UT"], half)
                self.copy("act", GLR[0:16, half * 512:(half + 1) * 512], self.ps[0:16, half, :], [], ["ps%d" % half, "GLR"])

        nproj = 0

        def load_head(h):
            self.dma_sp(KT, scr["kt"][h], [], ["KT0", "KT1"])
            self.dma_sp(KD, scr["kd"][h], [], ["KD"])
            self.dma_sp(V, scr["v"][h], [], ["V%d" % t for t in range(NT)])
            self.dma_sp(EB, scr["eb"][h], [], ["EB0", "EB1"])

        if mode == "B":
            load_head(0)
        for h in range(4):
            if mode == "A":
                for t in range(NT):
                    b = t % 2
                    self.mm(self.bank(b, 256), GLR[0:17, t * 128:(t + 1) * 128], WG2[0:17, h * 256:(h + 1) * 256], True, True, ["GLR", "WG2"], b)
                    self.act(L[:, t, :], self.bank(b, 256), AF.Exp, [], ["ps%d" % b, "L"], scale=-1.0)
                    self.act(L[:, t, :], L[:, t, :], AF.Ln, ["L"], ["L"], bias=1.0)
                for c in range(2):
                    pb = self.ps[:, 2 + 2 * c:4 + 2 * c, :].rearrange("p a b -> p (a b)")
                    for t in range(NT):
                        self.mm(pb[:, t * 128:(t + 1) * 128], L[:, t, c * 128:(c + 1) * 128], self.TRI, True, True, ["L", "CST"], 2 + 2 * c + t // 4)
                    bk = ["ps%d" % (2 + 2 * c), "ps%d" % (3 + 2 * c)]
                    self.act(EB[:, c, :], pb, AF.Exp, [], bk + ["EB%d" % c])
                    self.act(ENB[:, c, :], pb, AF.Exp, [], bk + ["ENB%d" % c], scale=-1.0)
                    self.P.op("dve", lambda e, c=c: e.tensor_reduce(out=AT[:, c:c + 1], in_=EB[:, c, 127::128], axis=AX.X, op=ALU.mult),
                              reads=["EB%d" % c], writes=["AT"])
                wk, kk = self.wload(w_in[:, 1024 + h * 256:1024 + (h + 1) * 256].rearrange("(j p) f -> p j f", p=128), 16, 256, False)
                for c in range(2):
                    for half in range(2):
                        b = nproj % 2
                        nproj += 1
                        for k in range(16):
                            self.mm(self.bank(b), wk[:, k, c * 128:(c + 1) * 128], self.UT[:, k, half * 512:(half + 1) * 512], k == 0, k == 15, kk + ["UT"], b)
                        self.tt(KT[:, c, half * 512:(half + 1) * 512], self.bank(b), ENB[:, c, half * 512:(half + 1) * 512], ALU.mult,
                                ["ENB%d" % c], ["ps%d" % b, "KT%d" % c])
                    ebl = EB[:, c, 127::128].unsqueeze(2).broadcast_to([128, NT, 128])
                    self.tt(KDT[c].rearrange("p (t i) -> p t i", t=NT), KT[:, c, :].rearrange("p (t i) -> p t i", t=NT), ebl, ALU.mult,
                            ["KT%d" % c, "EB%d" % c], ["ENB%d" % c])
                psT = self.bankb(6, 2)
                for t in range(NT):
                    for c in range(2):
                        self.tp(psT[:, (t * 2 + c) * 128:(t * 2 + c + 1) * 128], KDT[c][:, t * 128:(t + 1) * 128], self.IDb, ["ENB%d" % c, "IDb"], 6 + t // 4)
                self.copy("act", KD.rearrange("p t d -> p (t d)"), psT, [], ["ps6", "ps7", "KD"])
                for cb in range(2):
                    wv, kv = self.wload(w_in[:, 2048 + h * 512 + cb * 256:2048 + h * 512 + (cb + 1) * 256].rearrange("(j p) f -> p j f", p=128), 16, 256, False)
                    for t in range(NT):
                        b = nproj % 2
                        nproj += 1
                        for k in range(16):
                            self.mm(self.bank(b, 256), self.UT[:, k, t * 128:(t + 1) * 128], wv[:, k, :], k == 0, k == 15, kv + ["UT"], b)
                        self.copy("act", V[:, t, cb * 256:(cb + 1) * 256], self.bank(b, 256), [], ["ps%d" % b, "V%d" % t])
                self.dma_sp(scr["kt"][h], KT, ["KT0", "KT1"], ["SCR"])
                self.dma_sp(scr["kd"][h], KD, ["KD"], ["SCR"])
                self.dma_sp(scr["v"][h], V, ["V%d" % t for t in range(NT)], ["SCR"])
                self.dma_sp(scr["eb"][h], EB, ["EB0", "EB1"], ["SCR"])
            if mode == "B":
                wq, kq = self.wload(w_in[:, h * 256:(h + 1) * 256].rearrange("(j p) f -> p j f", p=128), 16, 256, False)
                for c in range(2):
                    for half in range(2):
                        b = nproj % 2
                        nproj += 1
                        for k in range(16):
                            self.mm(self.bank(b), wq[:, k, c * 128:(c + 1) * 128], self.UT[:, k, half * 512:(half + 1) * 512], k == 0, k == 15, kq + ["UT"], b)
                        self.stt(QT[:, c, half * 512:(half + 1) * 512], self.bank(b), 1.0 / 16.0, EB[:, c, half * 512:(half + 1) * 512],
                                 ALU.mult, ALU.mult, ["EB%d" % c], ["ps%d" % b, "QT"])
            if mode == "A":
                self.P.op("dve", lambda e: e.memset(S.rearrange("p c e -> p (c e)"), 0.0), writes=["S0", "S1"])
            else:
                self.P.op("dve", lambda e: e.memset(S.rearrange("p c e -> p (c e)"), 0.0), writes=["S0", "S1"])
                self.dma_sp(AG[:, 0:16].rearrange("p (q k) -> p q k", k=2), at_in(h), ["CCD"], ["AG"])
                for cp in range(NCORE - 1):
                    sl, ks = self.wload_f32(sa_in(cp, h), 2, 512, rkeys=["CCD"])
                    for c in range(2):
                        ai = cp * 2 + c
                        self.ts(CF[:, c:c + 1], AG[:, ai:ai + 1], self.FL[:, cp:cp + 1], ALU.mult, ["AG", "FL"], ["CF%d" % c],
                                s2=self.FL[:, 8 + cp:9 + cp], op1=ALU.add)
                        self.ts(ST2[:, c, :], sl[:, c, :], self.FL[:, cp:cp + 1], ALU.mult, ks + ["FL"], ["ENB0"])
                        self.stt(S[:, c, :], S[:, c, :], CF[:, c:c + 1], ST2[:, c, :], ALU.mult, ALU.add,
                                 ["S%d" % c, "CF%d" % c, "ENB0"], ["S%d" % c])
                for c in range(2):
                    self.copy("act", SB[:, c, :], S[:, c, :], ["S%d" % c], ["SB%d" % c])
            for t in range(NT):
                tk = slice(t * 128, (t + 1) * 128)
                if mode == "A" or t < NT - 1:
                    for c in range(2):
                        self.mm(self.bank(4 + c), KD[:, t, c * 128:(c + 1) * 128], V[:, t, :], True, True, ["KD", "V%d" % t], 4 + c)
                        self.stt(S[:, c, :], S[:, c, :], EB[:, c, t * 128 + 127:t * 128 + 128], self.bank(4 + c), ALU.mult, ALU.add,
                                 ["S%d" % c, "EB%d" % c], ["ps%d" % (4 + c), "S%d" % c])
                if mode == "B":
                    for c in range(2):
                        self.mm(self.bank(2, 128), KT[:, c, tk], QT[:, c, tk], c == 0, c == 1, ["KT%d" % c, "QT"], 2)
                    self.tt(ATT, self.bank(2, 128), self.MCb, ALU.mult, ["MCb"], ["ps2", "ATT"])
                    self.mm(self.bank(3), ATT, V[:, t, :], True, False, ["ATT", "V%d" % t], 3)
                    for c in range(2):
                        self.mm(self.bank(3), QT[:, c, tk], SB[:, c, :], False, c == 1, ["QT", "SB%d" % c], 3)
                    self.act(self.U[:, 0:512], self.bank(3), AF.Square, [], ["ps3", "U", "OSS%d" % t], accum_out=OSS[:, t:t + 1])
                    self.act(OTM[:, t:t + 1], OSS[:, t:t + 1], AF.Ln, ["OSS%d" % t], ["OTM%d" % t], scale=1.0 / 512, bias=EPS)
                    self.act(ORS[:, t:t + 1], OTM[:, t:t + 1], AF.Exp, ["OTM%d" % t], ["ORS%d" % t], scale=-0.5)
                    self.stt(ON[:, t, :], self.bank(3), ORS[:, t:t + 1], GOv, ALU.mult, ALU.mult, ["ORS%d" % t, "GO"], ["ps3", "L"])
                if mode == "B" and t < NT - 1:
                    for c in range(2):
                        self.copy("act", SB[:, c, :], S[:, c, :], ["S%d" % c], ["SB%d" % c])
            if mode == "A":
                self.dma_sp(sa_out(h), S, ["S0", "S1"], ["SAO%d" % h])
                self.dma_sp(at_out(h), AT, ["AT"], ["ATO%d" % h])
                if h == 3:
                    gather()
                P.barrier(False)
                continue
            if h < 3:
                load_head(h + 1)
            for cb in range(2):
                wr, kr = self.wload(w_in[:, 4096 + h * 512 + cb * 256:4096 + h * 512 + (cb + 1) * 256].rearrange("(j p) f -> p j f", p=128), 16, 256, False)
                for t in range(NT):
                    b = nproj % 2
                    nproj += 1
                    for k in range(16):
                        self.mm(self.bank(b, 256), self.UT[:, k, t * 128:(t + 1) * 128], wr[:, k, :], k == 0, k == 15, kr + ["UT"], b)
                    self.act(RSC[b], self.bank(b, 256), AF.Silu, [], ["ps%d" % b, "RSC%d" % b])
                    ov = ON[:, t, cb * 256:(cb + 1) * 256]
                    self.tt(ov, ov, RSC[b], ALU.mult, ["L", "RSC%d" % b], ["L"])
            psT = self.bankb(6, 2)
            for t in range(NT):
                for ec in range(4):
                    self.tp(psT[:, (t % 4 * 4 + ec) * 128:(t % 4 * 4 + ec + 1) * 128], ON[:, t, ec * 128:(ec + 1) * 128], self.IDb, ["L", "IDb"], 6 + (t % 4) // 2)
                if t % 4 == 3:
                    t0 = t - 3
                    self.copy("act", OGT[:, :, t0 * 128:(t0 + 4) * 128].rearrange("p c (t i) -> p t c i", t=4),
                              psT.rearrange("p (t c i) -> p t c i", t=4, c=4), [], ["ps6", "ps7", "ENB0", "ENB1"])
            nw = 0
            for cbo in range(2):
                wo, ko = self.wload(w_o[h * 512:(h + 1) * 512, cbo * 1024:(cbo + 1) * 1024].rearrange("(j p) f -> p j f", p=128), 4, 1024, False)
                for t in range(NT):
                    for c2 in range(2):
                        b = 2 + nw % 4
                        nw += 1
                        for ec in range(4):
                            self.mm(self.bank(b), OGT[:, ec, t * 128:(t + 1) * 128], wo[:, ec, c2 * 512:(c2 + 1) * 512], ec == 0, ec == 3,
                                    ko + ["ENB0", "ENB1"], b)
                        hv = self.H[:, t, cbo * 1024 + c2 * 512:cbo * 1024 + (c2 + 1) * 512]
                        self.tt(hv, self.bank(b), hv, ALU.add, ["H%d" % t], ["ps%d" % b, "H%d" % t])
            P.barrier(False)

    def out_proj(self, ON, OGT, w_o, row0, kon, kogt):
        psT = self.bankb(6, 2)
        for t in range(NT):
            for ec in range(4):
                self.tp(psT[:, (t % 4 * 4 + ec) * 128:(t % 4 * 4 + ec + 1) * 128], ON[:, t, ec * 128:(ec + 1) * 128], self.IDb,
                        kon + ["IDb"], 6 + (t % 4) // 2)
            if t % 4 == 3:
                t0 = t - 3
                self.copy("act", OGT[:, :, t0 * 128:(t0 + 4) * 128].rearrange("p c (t i) -> p t c i", t=4),
                          psT.rearrange("p (t c i) -> p t c i", t=4, c=4), [], ["ps6", "ps7"] + kogt)
        nw = 0
        for cbo in range(2):
            wo, ko = self.wload(w_o[row0:row0 + 512, cbo * 1024:(cbo + 1) * 1024].rearrange("(j p) f -> p j f", p=128), 4, 1024, False)
            for t in range(NT):
                for c2 in range(2):
                    b = 2 + nw % 4
                    nw += 1
                    for ec in range(4):
                        self.mm(self.bank(b), OGT[:, ec, t * 128:(t + 1) * 128], wo[:, ec, c2 * 512:(c2 + 1) * 512], ec == 0, ec == 3,
                                ko + kogt, b)
                    hv = self.H[:, t, cbo * 1024 + c2 * 512:cbo * 1024 + (c2 + 1) * 512]
                    self.tt(hv, self.bank(b), hv, ALU.add, ["H%d" % t], ["ps%d" % b, "H%d" % t])

    def headnorm(self, b, QF, SQ, ST, QN, GB, tag):
        self.copy("act", QF, self.bank(b, 256), [], ["ps%d" % b, tag + "QF"])
        self.tt(SQ, QF, QF, ALU.mult, [tag + "QF"], [tag + "SQ"])
        self.P.op("dve", lambda e: e.tensor_reduce(out=ST[:, 0:4], in_=SQ.rearrange("p (h d) -> p h d", h=4), axis=AX.X, op=ALU.add),
                  reads=[tag + "SQ"], writes=[tag + "SS"])
        self.act(ST[:, 4:8], ST[:, 0:4], AF.Ln, [tag + "SS"], [tag + "TM"], scale=1.0 / 64, bias=EPS)
        self.act(ST[:, 8:12], ST[:, 4:8], AF.Exp, [tag + "TM"], [tag + "RS"], scale=-0.5)
        q3 = QF.rearrange("p (h d) -> p h d", h=4)
        self.tt(q3, q3, ST[:, 8:12].unsqueeze(2).broadcast_to([128, 4, 64]), ALU.mult, [tag + "QF", tag + "RS"], [tag + "QF"])
        self.tt(QN.rearrange("p (h d) -> p h d", h=4), q3, GB.unsqueeze(1).broadcast_to([128, 4, 64]), ALU.mult,
                [tag + "QF", tag + "GB"], [tag + "QN"])

    def kv_compute(self, w_k, w_v, g_k, kt_out, va_out, cc_src=None):
        P = self.P
        X = O_X
        KTs = self.words(X, 2048).bitcast(BF16).rearrange("p (h t) -> p h t", h=4)
        VA = self.words(X + 2048, 1040).bitcast(BF16).rearrange("p (t h e) -> p t h e", t=NT, h=4)
        QF = self.words(X + 3200, 256)
        SQ = self.words(X + 3456, 256)
        QN = self.words(X + 3712, 128).bitcast(BF16)
        ST = self.words(X + 3840, 16)
        GK = self.words(X + 3872, 64)
        self.dma_sp(GK, g_k.partition_broadcast(128), [], ["kGB"])
        self.P.op("dve", lambda e: e.memset(VA[:, :, :, 64:65], 1.0), writes=["VA"])
        wk, kk = self.wload(w_k.rearrange("(j p) f -> p j f", p=128), 16, 256, False)
        wv, kv = self.wload(w_v.rearrange("(j p) f -> p j f", p=128), 16, 256, False)
        psT = self.bankb(6, 1)
        for t in range(NT):
            b = t % 2
            for k in range(16):
                self.mm(self.bank(b, 256), self.UT[:, k, t * 128:(t + 1) * 128], wk[:, k, :], k == 0, k == 15, kk + ["UT"], b)
            self.headnorm(b, QF, SQ, ST, QN, GK, "k")
            for kh in range(4):
                self.tp(psT[0:64, kh * 128:(kh + 1) * 128], QN[:, kh * 64:(kh + 1) * 64], self.IDb, ["kQN", "IDb"], 6)
            self.copy("act", KTs[0:64, :, t * 128:(t + 1) * 128], psT[0:64, 0:512].rearrange("p (h i) -> p h i", h=4), [], ["ps6", "KTs"])
            b2 = 2 + t % 2
            for k in range(16):
                self.mm(self.bank(b2, 256), self.UT[:, k, t * 128:(t + 1) * 128], wv[:, k, :], k == 0, k == 15, kv + ["UT"], b2)
            self.copy("act", VA[:, t, :, 0:64], self.bank(b2, 256).rearrange("p (h d) -> p h d", h=4), [], ["ps%d" % b2, "VA"])
        self.dma_sp(kt_out, KTs[0:64], ["KTs"], ["KTO"])
        self.dma_sp(va_out, VA, ["VA"], ["VAO"])
        if cc_src is not None:
            wk_ = self.words(X, 2048)[0:64, :].rearrange("p (h w) -> p h w", h=4)[:, :, 448:512]
            self.dma_sp(cc_src[0:64, :].rearrange("p (h w) -> p h w", h=4), wk_, ["KTs"], ["CCS2"])
            self.dma_sp(cc_src[64:192, 0:130], self.words(X + 2048, 1040)[:, 910:1040], ["VA"], ["CCS2"])
        P.barrier(False)

    def swa(self, w_q, g_q, sinks, w_o, kt_in, kth_in, va_in, vah_in, cc_dst=None):
        P = self.P
        X = O_X
        WX = O_WX
        KTs = self.words(X, 2304).bitcast(BF16).rearrange("p (h t) -> p h t", h=4)
        VA = self.words(X + 2304, 1170).bitcast(BF16).rearrange("p (t h e) -> p t h e", t=9, h=4)
        QTg = self.words(X + 3488, 4096).bitcast(BF16).rearrange("p (h t) -> p h t", h=8)
        GQ = self.words(X + 7584, 64)
        ES = self.words(X + 7648, 32)
        ST = self.words(X + 7680, 16)
        DEN = self.words(X + 7696, 8)
        REC = self.words(X + 7704, 8)
        OA = self.words(WX, 2048).bitcast(BF16).rearrange("p (t e) -> p t e", t=NT)
        OAT = self.words(WX + 2048, 2048).bitcast(BF16).rearrange("p (c t) -> p c t", c=4)
        PT = [self.words(WX + 4096 + i * 1024, 1024).bitcast(BF16).rearrange("p (k n) -> p k n", k=2) for i in range(2)]
        QF = [self.words(WX + 6144 + i * 256, 256) for i in range(2)]
        SQ = self.words(WX + 6656, 256)
        QN = self.words(WX + 6912, 128).bitcast(BF16)
        self.dma_sp(KTs[0:64, :, 128:1152], kt_in, [], ["KTs"])
        self.dma_sp(VA[:, 1:9], va_in, [], ["VA"])
        if cc_dst is None:
            self.dma_sp(KTs[0:64, :, 0:128], kth_in, [], ["KTs"])
            self.dma_sp(VA[:, 0], vah_in, [], ["VA"])
        else:
            g3 = cc_dst.rearrange("(c r) w -> r c w", r=192)
            KTg = self.words(WX, 2048)[0:64, :].rearrange("p (c w) -> p c w", c=8)
            VAg = self.words(WX + 2048, 1040).rearrange("p (c w) -> p c w", c=8)
            self.dma_sp(KTg, g3[0:64], ["CCD2"], ["KTg"])
            self.dma_sp(VAg, g3[64:192, :, 0:130], ["CCD2"], ["VAg"])
            for cp in range(NCORE):
                kin = KTg[:, cp, :].bitcast(BF16).rearrange("p (h i) -> p h i", h=4)
                vin = VAg[:, cp, :].bitcast(BF16).rearrange("p (h e) -> p h e", h=4)
                f = self.FL[:, 16 + cp:17 + cp]
                if cp == 0:
                    self.ts(KTs[0:64, :, 0:128], kin, f[0:64], ALU.mult, ["KTg", "FL"], ["KTs"])
                    self.ts(VA[:, 0], vin, f, ALU.mult, ["VAg", "FL"], ["VA"])
                else:
                    self.stt(KTs[0:64, :, 0:128], kin, f[0:64], KTs[0:64, :, 0:128], ALU.mult, ALU.add, ["KTg", "FL", "KTs"], ["KTs"])
                    self.stt(VA[:, 0], vin, f, VA[:, 0], ALU.mult, ALU.add, ["VAg", "FL", "VA"], ["VA"])
            P.barrier(False)
        self.dma_sp(GQ, g_q.partition_broadcast(128), [], ["GQraw"])
        self.act(GQ, GQ, AF.Copy, ["GQraw"], ["qGB"], scale=0.125)
        self.dma_sp(ES, sinks.partition_broadcast(128), [], ["ESraw"])
        self.act(ES, ES, AF.Exp, ["ESraw"], ["ES"])
        psT = self.bankb(6, 1)
        QF2 = [self.words(WX + 6144 + i * 512, 512) for i in range(2)]
        SQ2 = self.words(WX + 7168, 512)
        QN2 = [self.words(WX + 7680 + i * 256, 256).bitcast(BF16) for i in range(2)]
        ST2 = [self.words(X + 7712 + i * 32, 32) for i in range(2)]
        ns = 0
        for g in range(4):
            wqs = []
            for blk in range(2):
                wqs.append(self.wload(w_q[:, g * 512 + blk * 256:g * 512 + (blk + 1) * 256].rearrange("(j p) f -> p j f", p=128), 16, 256, False))

            def qA(t, g=g, wqs=wqs):
                b = t % 2
                tg = "q%d" % b
                for blk in range(2):
                    wq, kq = wqs[blk]
                    for k in range(16):
                        self.mm(self.ps[:, b, blk * 256:(blk + 1) * 256], self.UT[:, k, t * 128:(t + 1) * 128], wq[:, k, :], k == 0, k == 15, kq + ["UT"], b)
                qf, st = QF2[b], ST2[b]
                self.copy("act", qf, self.bank(b), [], ["ps%d" % b, tg + "QF"])
                self.tt(SQ2, qf, qf, ALU.mult, [tg + "QF"], ["qSQ"])
                self.P.op("dve", lambda e, st=st: e.tensor_reduce(out=st[:, 0:8], in_=SQ2.rearrange("p (h d) -> p h d", h=8), axis=AX.X, op=ALU.add),
                          reads=["qSQ"], writes=[tg + "SS"])
                self.act(st[:, 8:16], st[:, 0:8], AF.Ln, [tg + "SS"], [tg + "TM"], scale=1.0 / 64, bias=EPS)
                self.act(st[:, 16:24], st[:, 8:16], AF.Exp, [tg + "TM"], [tg + "RS"], scale=-0.5)

            def qB(t):
                b = t % 2
                tg = "q%d" % b
                qf, st, qn = QF2[b], ST2[b], QN2[b]
                q3 = qf.rearrange("p (h d) -> p h d", h=8)
                self.tt(q3, q3, st[:, 16:24].unsqueeze(2).broadcast_to([128, 8, 64]), ALU.mult, [tg + "QF", tg + "RS"], [tg + "QF"])
                self.tt(qn.rearrange("p (h d) -> p h d", h=8), q3, GQ.unsqueeze(1).broadcast_to([128, 8, 64]), ALU.mult,
                        [tg + "QF", "qGB"], [tg + "QN"])
                for hd in range(8):
                    self.tp(psT[0:64, hd * 128:(hd + 1) * 128], qn[:, hd * 64:(hd + 1) * 64], self.IDb, [tg + "QN", "IDb"], 6)
                self.copy("act", QTg[0:64, :, t * 128:(t + 1) * 128], psT[0:64, :].rearrange("p (h i) -> p h i", h=8), [], ["ps6", "QTg"])

            for t in range(NT + 1):
                if t < NT:
                    qA(t)
                if t >= 1:
                    qB(t - 1)

            def sS(t, g=g):
                nonlocal ns
                pt = PT[t % 2]
                pk = "PT%d" % (t % 2)
                for kb in range(2):
                    kt = t + kb
                    for hh in range(2):
                        b = 2 + ns % 2
                        ns += 1
                        self.mm(self.bank(b), KTs[0:64, g, kt * 128:(kt + 1) * 128], QTg[0:64, hh * 4:(hh + 1) * 4, t * 128:(t + 1) * 128],
                                True, True, ["KTs", "QTg"], b)
                        self.act(pt[:, kb, hh * 512:(hh + 1) * 512], self.bank(b), AF.Exp, [], ["ps%d" % b, pk])
                    msk = self.MPb if kb == 0 else self.MCb
                    pv = pt[:, kb, :].rearrange("p (h i) -> p h i", h=8)
                    self.tt(pv, pv, msk.unsqueeze(1).broadcast_to([128, 8, 128]), ALU.mult, [pk, "MPb", "MCb"], [pk])

            def sPV(t, g=g):
                pt = PT[t % 2]
                pk = "PT%d" % (t % 2)
                b0 = 4 if t % 2 == 0 else 0
                for hd in range(8):
                    bo = b0 + hd // 4
                    o = self.ps[:, bo, (hd % 4) * 65:(hd % 4) * 65 + 65]
                    for kb in range(2):
                        self.mm(o, pt[:, kb, hd * 128:(hd + 1) * 128], VA[:, t + kb, g, :], kb == 0, kb == 1, [pk, "VA"], bo)
                for bb in range(2):
                    pso = self.ps[:, b0 + bb, 0:260].rearrange("p (h e) -> p h e", h=4)
                    dn = DEN[:, bb * 4:(bb + 1) * 4]
                    self.tt(dn.unsqueeze(2), pso[:, :, 64:65], ES[:, g * 8 + bb * 4:g * 8 + bb * 4 + 4].unsqueeze(2), ALU.add, ["ES"],
                            ["ps%d" % (b0 + bb), "DEN%d" % bb])
                    self.P.op("dve", lambda e, bb=bb, dn=dn: e.reciprocal(out=REC[:, bb * 4:(bb + 1) * 4], in_=dn),
                              reads=["DEN%d" % bb], writes=["REC%d" % bb])
                    self.tt(OA[:, t, bb * 256:(bb + 1) * 256].rearrange("p (h d) -> p h d", h=4), pso[:, :, 0:64],
                            REC[:, bb * 4:(bb + 1) * 4].unsqueeze(2).broadcast_to([128, 4, 64]), ALU.mult, ["REC%d" % bb],
                            ["ps%d" % (b0 + bb), "OA"])

            for t in range(NT + 1):
                if t < NT:
                    sS(t)
                if t >= 1:
                    sPV(t - 1)
            self.out_proj(OA, OAT, w_o, g * 512, ["OA"], ["OAT"])
        P.barrier(False)


def _consts():
    c = np.zeros((128, 512), np.float32)
    j = np.arange(128)[:, None]
    i = np.arange(128)[None, :]
    c[:, 0:128] = np.eye(128, dtype=np.float32)
    c[:, 128:256] = (j <= i)
    c[:, 256:384] = (j <= i) * (-1.0 / 16.0)
    c[:, 384:512] = (j > i)
    return c


def _flags(core):
    f = np.zeros((128, 24), np.float32)
    for cp in range(8):
        m = 1.0 if cp < core else 0.0
        f[:, cp] = m
        f[:, 8 + cp] = 1.0 - m
        f[:, 16 + cp] = 1.0 if cp == core - 1 else 0.0
    return f


def _run(build, in_maps):
    nc = bass.Bass("TRN2", target_bir_lowering=False)
    with contextlib.ExitStack() as st:
        core = Core(nc, st)
        core.setup()
        build(core)
        core.finish()
    for c in range(NCORE):
        in_maps[c]["cst"] = _consts()
        in_maps[c]["fl"] = _flags(c)
    res = run_bass_kernel_spmd(nc, in_maps, core_ids=list(range(NCORE)))
    return res.results


def build_fused(k):
    x = k.din("x", [TOK, D])
    norm_mix = k.din("norm_mix", [4, D])
    norm_mlp = k.din("norm_mlp", [4, D])
    mlp_w1 = k.din("mlp_w1", [4, D, 4 * D])
    mlp_w2 = k.din("mlp_w2", [4, 4 * D, D])
    a_w_in = k.din("a_w_in", [2, D, 6160])
    a_w_g2 = k.din("a_w_g2", [2, 16, 1024])
    a_b_g = k.din("a_b_g", [2, 1024])
    a_g_o = k.din("a_g_o", [2, 512])
    a_w_o = k.din("a_w_o", [2, D, D])
    kv_norm = k.din("kv_norm", [D])
    kv_w_k = k.din("kv_w_k", [D, 256])
    kv_w_v = k.din("kv_w_v", [D, 256])
    kv_g_k = k.din("kv_g_k", [64])
    b_w_q = k.din("b_w_q", [2, D, D])
    b_g_q = k.din("b_g_q", [2, 64])
    b_sinks = k.din("b_sinks", [2, 32])
    b_w_o = k.din("b_w_o", [2, D, D])
    out = k.dout("out", [TOK, D])
    k.load_h(x)
    scr = {"kt": [k.dscratch("scr_kt%d" % h, [128, 2, 1024], BF16) for h in range(4)],
           "kd": [k.dscratch("scr_kd%d" % h, [128, 8, 256], BF16) for h in range(4)],
           "v": [k.dscratch("scr_v%d" % h, [128, 8, 512], BF16) for h in range(4)],
           "eb": [k.dscratch("scr_eb%d" % h, [128, 2, 1024]) for h in range(4)]}
    for i in range(2):
        cs = k.dscratch("cc_src%d" % i, [512, 1026])
        cd = k.dscratch("cc_dst%d" % i, [NCORE * 512, 1026])

        def gather(i=i, cs=cs, cd=cd):
            k.allgather(cs, cd, ["SAO0", "SAO1", "SAO2", "SAO3", "ATO0", "ATO1", "ATO2", "ATO3"], ["CCD"], "cc%d" % i)

        k.norm(norm_mix[i])
        k.gla(a_w_in[i], a_w_g2[i], a_b_g[i], a_g_o[i], None, "A", gather=gather, scr=scr,
              sa_out=lambda h, cs=cs: cs[h * 128:(h + 1) * 128, 0:1024].rearrange("p (c e) -> p c e", c=2),
              at_out=lambda h, cs=cs: cs[h * 128:(h + 1) * 128, 1024:1026])
        k.gla(a_w_in[i], a_w_g2[i], a_b_g[i], a_g_o[i], a_w_o[i], "B", scr=scr,
              sa_in=lambda cp, h, cd=cd: cd[cp * 512 + h * 128:cp * 512 + (h + 1) * 128, 0:1024].rearrange("p (c e) -> p c e", c=2),
              at_in=lambda h, cd=cd: cd[:, 1024:1026].rearrange("(c h d) k -> h d c k", c=NCORE, h=4)[h])
        k.norm(norm_mlp[i])
        k.mlp(mlp_w1[i], mlp_w2[i])
    kvk = k.dscratch("kvs_k", [64, 4, 1024], BF16)
    kvv = k.dscratch("kvs_v", [128, 8, 4, 65], BF16)
    cs2 = k.dscratch("cc_src2", [192, 256])
    cd2 = k.dscratch("cc_dst2", [NCORE * 192, 256])
    k.norm(kv_norm)
    k.kv_compute(kv_w_k, kv_w_v, kv_g_k, kvk, kvv, cc_src=cs2)
    k.allgather(cs2, cd2, ["CCS2"], ["CCD2"], "cc2")
    for j in range(2):
        k.norm(norm_mix[2 + j])
        k.swa(b_w_q[j], b_g_q[j], b_sinks[j], b_w_o[j], kvk, None, kvv, None, cc_dst=cd2)
        k.norm(norm_mlp[2 + j])
        k.mlp(mlp_w1[2 + j], mlp_w2[2 + j])
    k.store_h(out)


def kernel(**inp):
    inp = {k: np.ascontiguousarray(np.asarray(v, dtype=np.float32)) for k, v in inp.items()}
    x = inp["x"][0].reshape(NCORE, TOK, D)
    maps = []
    for c in range(NCORE):
        m = {k: v for k, v in inp.items() if k != "x"}
        m["x"] = x[c]
        maps.append(m)
    res = _run(build_fused, maps)
    out = np.concatenate([r["out"] for r in res], 0)
    return out.reshape(1, NCORE * TOK, D).astype(np.float32)
```
